# Optimizing a Trainium2 kernel written in Bass

```python
import math
import jax, jax.numpy as jnp
from jax import lax
import numpy as np

D_MODEL = 1024
BATCH = 2
SEQ = 16384
DEPTH = 4

N_A = DEPTH // 2
N_B = DEPTH - N_A
A_HEAD = 64
A_HEADS = D_MODEL // A_HEAD
LORA_W = 64
LORA_A = 64
LORA_V = 32
GN_EPS = 64e-5
B_QK = 64
B_HEADS = D_MODEL // (2 * B_QK)
B_V = 2 * B_QK
B_WIDTH = B_HEADS * B_V
ROT_DIM = B_QK // 4
ROPE_THETA = 500000.0
Q_BLOCK = 128
NORM_EPS = 1e-6
SUBLN_EPS = 1e-5

kernel_name = "yoco_rwkv7_diffattn_hybrid"


def rmsnorm(x, g, eps=NORM_EPS):
    xf = x.astype(jnp.float32)
    y = xf * lax.rsqrt(jnp.mean(xf * xf, axis=-1, keepdims=True) + eps)
    return (y * g.astype(jnp.float32)).astype(x.dtype)


def partial_rope(t, positions):
    half = ROT_DIM // 2
    inv_freq = ROPE_THETA ** (-jnp.arange(0, ROT_DIM, 2, dtype=jnp.float32) / ROT_DIM)
    ang = positions.astype(jnp.float32)[:, :, None] * inv_freq[None, None, :]
    cos = jnp.cos(ang)[:, :, None, None, :]
    sin = jnp.sin(ang)[:, :, None, None, :]
    tf = t.astype(jnp.float32)
    t1, t2, rest = tf[..., :half], tf[..., half:ROT_DIM], tf[..., ROT_DIM:]
    out = jnp.concatenate([t1 * cos - t2 * sin, t2 * cos + t1 * sin, rest], axis=-1)
    return out.astype(t.dtype)


def wkv7_scan(r, w, k, v, a, b):
    Bn, _, H, N = r.shape

    def step(state, inp):
        r_t, w_t, k_t, v_t, a_t, b_t = inp
        sa = jnp.einsum('bhij,bhj->bhi', state, a_t)
        state = (state * w_t[:, :, None, :] + sa[..., None] * b_t[:, :, None, :]
                 + v_t[..., None] * k_t[:, :, None, :])
        y_t = jnp.einsum('bhij,bhj->bhi', state, r_t)
        return state, y_t

    xs = (jnp.moveaxis(r, 1, 0), jnp.moveaxis(w, 1, 0), jnp.moveaxis(k, 1, 0),
          jnp.moveaxis(v, 1, 0), jnp.moveaxis(a, 1, 0), jnp.moveaxis(b, 1, 0))
    s0 = jnp.zeros((Bn, H, N, N), jnp.float32)
    _, y = lax.scan(step, s0, xs)
    return jnp.moveaxis(y, 0, 1)


def rwkv7_mixer(xn, mu, w_in, w0, w1, w2, a0, a1, a2, k_k, k_a, r_k, ln_w, ln_b, w_out,
                v_first, v_res):
    Bn, S, D = xn.shape
    f32 = jnp.float32
    x_prev = jnp.pad(xn, ((0, 0), (1, 0), (0, 0)))[:, :-1]
    xx = x_prev - xn
    xm = xn[:, :, None, :] + xx[:, :, None, :] * mu[:4]
    rkvg = jnp.einsum('bspd,pde->bspe', xm, w_in)
    r, k, v, g = rkvg[:, :, 0], rkvg[:, :, 1], rkvg[:, :, 2], rkvg[:, :, 3]
    xw = xn + xx * mu[4]
    xa = xn + xx * mu[5]
    w = -jax.nn.softplus(-(w0 + jnp.tanh(xw @ w1) @ w2)) - 0.5
    a = jax.nn.sigmoid(a0 + (xa @ a1) @ a2)
    if v_res is None:
        v_first = v
    else:
        v0, v1, v2 = v_res
        v = v + (v_first - v) * jax.nn.sigmoid(v0 + (xm[:, :, 2] @ v1) @ v2)

    def hd(t):
        return t.reshape(Bn, S, A_HEADS, A_HEAD).astype(f32)

    kk = hd(k * k_k)
    kk = kk / jnp.maximum(jnp.sqrt(jnp.sum(kk * kk, axis=-1, keepdims=True)), 1e-12)
    kmod = hd(k * (1.0 + (a - 1.0) * k_a))
    ah = hd(a)
    rh = hd(r)
    vh = hd(v)
    decay = jnp.exp(-jnp.exp(hd(w)))
    y = wkv7_scan(rh, decay, kmod, vh, -kk, kk * ah)
    mean = jnp.mean(y, axis=-1, keepdims=True)
    var = jnp.mean(jnp.square(y - mean), axis=-1, keepdims=True)
    yn = (y - mean) * lax.rsqrt(var + GN_EPS)
    yn = yn * ln_w.reshape(A_HEADS, A_HEAD).astype(f32) + ln_b.reshape(A_HEADS, A_HEAD).astype(f32)
    bonus = jnp.sum(rh * kmod * r_k.astype(f32), axis=-1, keepdims=True) * vh
    o = (yn + bonus).reshape(Bn, S, D).astype(xn.dtype) * jax.nn.silu(g)
    return o @ w_out, v_first


def diff_attention(q, k, v, lam):
    Bn, S, H = q.shape[:3]
    scale = B_QK ** -0.5
    qh = q.transpose(0, 2, 3, 1, 4)
    kh = k.transpose(0, 2, 3, 1, 4)
    vh = v.transpose(0, 2, 1, 3)
    n_blk = S // Q_BLOCK
    kpos = jnp.arange(S)

    def one_block(i):
        q0 = i * Q_BLOCK
        qb = lax.dynamic_slice_in_dim(qh, q0, Q_BLOCK, axis=3)
        s = jnp.einsum('bhmqd,bhmkd->bhmqk', qb, kh, preferred_element_type=jnp.float32) * scale
        qpos = q0 + jnp.arange(Q_BLOCK)
        s = jnp.where(kpos[None, :] <= qpos[:, None], s, -jnp.inf)
        p = jax.nn.softmax(s, axis=-1)
        attn = p[:, :, 0] - lam * p[:, :, 1]
        return jnp.einsum('bhqk,bhkd->bhqd', attn.astype(vh.dtype), vh)

    o = lax.map(one_block, jnp.arange(n_blk))
    return o.transpose(1, 0, 3, 2, 4).reshape(Bn, S, H, B_V)


def setup_inputs(seed: int = 0) -> dict:
    key = jax.random.key(seed)
    ks = iter(jax.random.split(key, 40))
    f32 = jnp.float32
    D = D_MODEL

    def nrm(shape, scale):
        return jax.random.normal(next(ks), shape, f32) * scale

    def unif(shape, lo, hi):
        return jax.random.uniform(next(ks), shape, f32, lo, hi)

    x = jax.random.normal(next(ks), (BATCH, SEQ, D), f32)
    positions = jnp.broadcast_to(jnp.arange(SEQ, dtype=jnp.int32)[None, :], (BATCH, SEQ))
    inp = {
        "x": x,
        "positions": positions,
        "a_norm": 1.0 + nrm((N_A, D), 0.02),
        "a_mu": unif((N_A, 6, D), 0.0, 1.0),
        "a_w_in": nrm((N_A, 4, D, D), D ** -0.5),
        "a_w0": unif((N_A, D), -6.0, -1.0),
        "a_w1": nrm((N_A, D, LORA_W), D ** -0.5),
        "a_w2": nrm((N_A, LORA_W, D), 0.1 * LORA_W ** -0.5),
        "a_a0": nrm((N_A, D), 0.1),
        "a_a1": nrm((N_A, D, LORA_A), D ** -0.5),
        "a_a2": nrm((N_A, LORA_A, D), 0.1 * LORA_A ** -0.5),
        "a_v0": nrm((N_A - 1, D), 0.1),
        "a_v1": nrm((N_A - 1, D, LORA_V), D ** -0.5),
        "a_v2": nrm((N_A - 1, LORA_V, D), 0.1 * LORA_V ** -0.5),
        "a_k_k": 0.85 + nrm((N_A, D), 0.05),
        "a_k_a": 1.0 + nrm((N_A, D), 0.05),
        "a_r_k": nrm((N_A, A_HEADS, A_HEAD), 0.1),
        "a_ln_w": 1.0 + nrm((N_A, D), 0.02),
        "a_ln_b": nrm((N_A, D), 0.02),
        "a_w_out": nrm((N_A, D, D), D ** -0.5),
        "kv_norm": 1.0 + nrm((D,), 0.02),
        "w_kv": nrm((D, B_HEADS * 2 * B_QK + B_WIDTH), D ** -0.5),
        "b_norm": 1.0 + nrm((N_B, D), 0.02),
        "b_w_in": nrm((N_B, D, B_HEADS * 2 * B_QK + B_WIDTH), D ** -0.5),
        "b_lq1": nrm((N_B, B_QK), 0.1),
        "b_lk1": nrm((N_B, B_QK), 0.1),
        "b_lq2": nrm((N_B, B_QK), 0.1),
        "b_lk2": nrm((N_B, B_QK), 0.1),
        "b_subln": 1.0 + nrm((N_B, B_V), 0.02),
        "b_w_out": nrm((N_B, B_WIDTH, D), B_WIDTH ** -0.5),
        "final_norm": 1.0 + nrm((D,), 0.02),
    }
    return inp


def reference(x, positions, a_norm, a_mu, a_w_in, a_w0, a_w1, a_w2, a_a0, a_a1, a_a2,
              a_v0, a_v1, a_v2, a_k_k, a_k_a, a_r_k, a_ln_w, a_ln_b, a_w_out,
              kv_norm, w_kv, b_norm, b_w_in, b_lq1, b_lk1, b_lq2, b_lk2, b_subln, b_w_out,
              final_norm):
    Bn, S, D = x.shape
    nq = B_HEADS * 2 * B_QK
    v_first = None
    k_shared = None
    v_shared = None
    for layer in range(DEPTH):
        if layer < N_A:
            xn = rmsnorm(x, a_norm[layer])
            v_res = None if layer == 0 else (a_v0[layer - 1], a_v1[layer - 1], a_v2[layer - 1])
            o, v_first = rwkv7_mixer(xn, a_mu[layer], a_w_in[layer], a_w0[layer], a_w1[layer],
                                     a_w2[layer], a_a0[layer], a_a1[layer], a_a2[layer],
                                     a_k_k[layer], a_k_a[layer], a_r_k[layer], a_ln_w[layer],
                                     a_ln_b[layer], a_w_out[layer], v_first, v_res)
            x = x + o
        else:
            j = layer - N_A
            if k_shared is None:
                h = rmsnorm(x, kv_norm)
                kv = h @ w_kv
                k_shared = partial_rope(kv[..., :nq].reshape(Bn, S, B_HEADS, 2, B_QK), positions)
                v_shared = kv[..., nq:].reshape(Bn, S, B_HEADS, B_V)
            xn = rmsnorm(x, b_norm[j])
            proj = xn @ b_w_in[j]
            q = partial_rope(proj[..., :nq].reshape(Bn, S, B_HEADS, 2, B_QK), positions)
            gate = proj[..., nq:]
            lam_init = 0.8 - 0.6 * math.exp(-0.3 * layer)
            lam = (jnp.exp(jnp.sum(b_lq1[j].astype(jnp.float32) * b_lk1[j].astype(jnp.float32)))
                   - jnp.exp(jnp.sum(b_lq2[j].astype(jnp.float32) * b_lk2[j].astype(jnp.float32)))
                   + lam_init)
            o = diff_attention(q, k_shared, v_shared, lam)
            o = rmsnorm(o, b_subln[j], SUBLN_EPS) * (1.0 - lam_init)
            o = o.reshape(Bn, S, B_WIDTH).astype(x.dtype) * jax.nn.silu(gate)
            x = x + o @ b_w_out[j]
    return rmsnorm(x, final_norm)
```

```python
import numpy as np
import ml_dtypes
from contextlib import ExitStack
import concourse.bass as bass
import concourse.mybir as mybir
from concourse.bass_utils import run_bass_kernel_spmd

F32 = mybir.dt.float32
BF16 = mybir.dt.bfloat16
I32 = mybir.dt.int32
AF = mybir.ActivationFunctionType
ALU = mybir.AluOpType
AX = mybir.AxisListType

D = 1024
NCH = 8
ENGS = ["pe", "act", "dve", "pool", "sp"]


class Sched:
    def __init__(self, nc, es):
        self.nc = nc
        self.es = es
        self.streams = {e: [] for e in ENGS}
        self.sem = {e: es.enter_context(nc.semaphore("c_" + e)) for e in ENGS}
        self.count = {e: 0 for e in ENGS}
        self.waited = {e: {} for e in ENGS}
        self.last_w = {}
        self.readers = {}
        self.dsem = {}
        self.out_tokens = []
        self.sub = {}

    def dma_sem(self, name):
        if name not in self.dsem:
            self.dsem[name] = [self.es.enter_context(self.nc.semaphore("d_" + name)), 0]
        return name

    def op(self, eng, fn, reads=(), writes=(), dma=None, is_out=False):
        import os as _os
        self.nops = getattr(self, "nops", 0) + 1
        if self.nops > int(_os.environ.get("OPLIMIT", "100000000")):
            return None
        nk = lambda k: k[:3] if (k.startswith("ps") and len(k) > 3 and k[2].isdigit()) else k
        reads = [nk(k) for k in reads]
        writes = [nk(k) for k in writes]
        writes = writes + [k for k in reads if k.startswith("ps") and k[2].isdigit()]
        reads = [k for k in reads if not (k.startswith("ps") and k[2].isdigit())]
        deps = []

        def bank_of(k):
            return k[:3] if (k.startswith("ps") and len(k) > 3 and k[2].isdigit()) else None

        def is_bank(k):
            return k.startswith("ps") and len(k) == 3 and k[2].isdigit()

        def rdep(k):
            if k in self.last_w:
                deps.append(self.last_w[k])

        def wdep(k):
            if k in self.last_w:
                deps.append(self.last_w[k])
            deps.extend(self.readers.get(k, {}).values())

        for k in reads:
            rdep(k)
            bk = bank_of(k)
            if bk:
                self.sub.setdefault(bk, set()).add(k)
                rdep(bk)
            if is_bank(k):
                for s_ in self.sub.get(k, ()):
                    rdep(s_)
        for k in writes:
            wdep(k)
            bk = bank_of(k)
            if bk:
                self.sub.setdefault(bk, set()).add(k)
                wdep(bk)
            if is_bank(k):
                for s_ in self.sub.get(k, ()):
                    wdep(s_)
        waits = []
        wd = self.waited[eng]
        for (sid, sem, val) in deps:
            if sid == "pe" and eng == "pe":
                continue
            if wd.get(sid, 0) < val:
                wd[sid] = val
                waits.append((sem, val))
        if dma is None:
            self.count[eng] += 1
            tok = (eng, self.sem[eng], self.count[eng])
            inc = (self.sem[eng], 1)
        else:
            self.dma_sem(dma)
            d = self.dsem[dma]
            d[1] += 16
            tok = ("d_" + dma, d[0], d[1])
            inc = (d[0], 16)
        self.streams[eng].append((waits, fn, inc))
        for k in reads:
            self.readers.setdefault(k, {})[tok[0]] = tok
        for k in writes:
            self.last_w[k] = tok
            self.readers[k] = {}
        if is_out:
            self.out_tokens.append(tok)
        return tok

    def finish(self):
        final = {}
        for (sid, sem, val) in self.out_tokens:
            if final.get(sid, (None, 0))[1] < val:
                final[sid] = (sem, val)
        self.streams["sp"].append((list(final.values()), None, None))

    def emit(self, block):
        def run(eng_name):
            def body(eng):
                for waits, fn, inc in self.streams[eng_name]:
                    for (s, v) in waits:
                        eng.wait_ge(s, v)
                    if fn is not None:
                        fn(eng).then_inc(inc[0], inc[1])
            return body
        block.tensor(run("pe"))
        block.scalar(run("act"))
        block.vector(run("dve"))
        block.gpsimd(run("pool"))
        block.sync(run("sp"))


class Ctx:
    def __init__(self, nc, es):
        self.nc = nc
        self.es = es
        self.S = Sched(nc, es)
        self.psall = es.enter_context(nc.psum_tensor("psall", [128, 8, 512], F32))
        self.ps = [self.psall[:, i, :] for i in range(8)]

    def sb(self, name, shape, dt):
        return self.es.enter_context(self.nc.sbuf_tensor(name, list(shape), dt))

    def dram(self, name, shape, dt, kind):
        return self.nc.dram_tensor(name, list(shape), dt, kind=kind).ap()

    def dma(self, out, in_, r, w, sem, eng="sp", is_out=False):
        return self.S.op(eng, lambda e: e.dma_start(out=out, in_=in_), reads=r, writes=w, dma=sem, is_out=is_out)

    def mm(self, out, lhsT, rhs, r, w, start=True, stop=True):
        return self.S.op("pe", lambda e: e.matmul(out, lhsT=lhsT, rhs=rhs, start=start, stop=stop), reads=r, writes=w)

    def tr(self, out, in_, ident, r, w):
        return self.S.op("pe", lambda e: e.transpose(out, in_, ident), reads=r, writes=w)

    def act(self, out, in_, func, r, w, bias=None, scale=None, accum_out=None):
        kw = {}
        if bias is not None:
            kw["bias"] = bias
        if scale is not None:
            kw["scale"] = scale
        if accum_out is not None:
            kw["accum_out"] = accum_out
        return self.S.op("act", lambda e: e.activation(out=out, in_=in_, func=func, **kw), reads=r, writes=w)

    def tt(self, eng, out, in0, in1, op, r, w):
        return self.S.op(eng, lambda e: e.tensor_tensor(out=out, in0=in0, in1=in1, op=op), reads=r, writes=w)

    def ts(self, eng, out, in0, s1, s2, op0, op1, r, w):
        if op1 is None:
            return self.S.op(eng, lambda e: e.tensor_scalar(out=out, in0=in0, scalar1=s1, scalar2=None, op0=op0), reads=r, writes=w)
        return self.S.op(eng, lambda e: e.tensor_scalar(out=out, in0=in0, scalar1=s1, scalar2=s2, op0=op0, op1=op1), reads=r, writes=w)

    def stt(self, eng, out, in0, scalar, in1, op0, op1, r, w):
        return self.S.op(eng, lambda e: e.scalar_tensor_tensor(out=out, in0=in0, scalar=scalar, in1=in1, op0=op0, op1=op1), reads=r, writes=w)

    def cp(self, eng, out, in_, r, w):
        if eng == "act":
            return self.S.op("act", lambda e: e.copy(out=out, in_=in_), reads=r, writes=w)
        return self.S.op(eng, lambda e: e.tensor_copy(out=out, in_=in_), reads=r, writes=w)

    def red(self, eng, out, in_, r, w):
        return self.S.op(eng, lambda e: e.tensor_reduce(out=out, in_=in_, axis=AX.X, op=ALU.add), reads=r, writes=w)

    def recip(self, out, in_, r, w):
        return self.S.op("dve", lambda e: e.reciprocal(out=out, in_=in_), reads=r, writes=w)

    def memset(self, eng, ap, val, w):
        return self.S.op(eng, lambda e: e.memset(ap, val), writes=w)

    def done(self):
        self.S.finish()
        with self.nc.Block() as block:
            self.S.emit(block)


def frontend_alloc(cx, has_prev, BLK=512):
    fe = {}
    fe["xT"] = cx.sb("fe_xT", [128, NCH, BLK], F32)
    fe["sq"] = cx.sb("fe_sq", [128, 2, BLK], F32)
    fe["xh"] = cx.sb("fe_xh", [128, NCH, BLK + 1], BF16)
    fe["rstd"] = cx.sb("fe_rstd", [128, BLK], F32)
    fe["ones"] = cx.sb("fe_ones", [128, 128], F32)
    cx.memset("pool", fe["ones"][:], 1.0, ["fe_ones"])
    cx.memset("pool", fe["xh"][:], 0.0, ["fe_xh"])
    if has_prev:
        fe["oT"] = cx.sb("fe_oT", [128, NCH, BLK], BF16)
        fe["wo"] = cx.sb("fe_wo", [128, NCH, D], BF16)
    return fe


def frontend_load_wout(cx, fe, wout_ap, stage):
    for half in range(2):
        cx.dma(stage[:, 0:4, :], wout_ap.rearrange("(c p) d -> p c d", p=128)[:, half * 4:(half + 1) * 4, :],
               [], ["stage"], "stage")
        cx.cp("dve", fe["wo"][:, half * 4:(half + 1) * 4, :], stage[:, 0:4, :], ["stage"], ["fe_wo"])


def frontend_block(cx, fe, xT_ap, oT_ap, xT_out_ap, b, has_prev, BLK=512, store_x=True, norm=True):
    t0 = b * BLK
    xv = xT_ap.rearrange("(c p) s -> p c s", p=128)
    cx.dma(fe["xT"][:], xv[:, :, t0:t0 + BLK], [], ["fe_xT"], "fe_x")
    if has_prev:
        ov = oT_ap.rearrange("(c p) s -> p c s", p=128)
        cx.dma(fe["oT"][:], ov[:, :, t0:t0 + BLK], [], ["fe_oT"], "fe_o")
        for dc in range(NCH):
            bank = dc % 4
            for kc in range(NCH):
                cx.mm(cx.ps[bank][:], fe["wo"][:, kc, dc * 128:(dc + 1) * 128], fe["oT"][:, kc, :],
                      ["fe_wo", "fe_oT"], ["ps%d" % bank], start=(kc == 0), stop=(kc == NCH - 1))
            cx.tt("dve", fe["xT"][:, dc, :], fe["xT"][:, dc, :], cx.ps[bank][:], ALU.add,
                  ["fe_xT", "ps%d" % bank], ["fe_xT"])
        if store_x:
            ovx = xT_out_ap.rearrange("(c p) s -> p c s", p=128)
            cx.dma(ovx[:, :, t0:t0 + BLK], fe["xT"][:], ["fe_xT"], [], "fe_xs", is_out=True)
    for dc in range(NCH):
        cx.act(fe["sq"][:, dc % 2, :], fe["xT"][:, dc, :], AF.Square, ["fe_xT"], ["fe_sq%d" % (dc % 2)])
        cx.mm(cx.ps[4][:], fe["ones"][:], fe["sq"][:, dc % 2, :], ["fe_ones", "fe_sq%d" % (dc % 2)], ["ps4"],
              start=(dc == 0), stop=(dc == NCH - 1))
    cx.act(fe["rstd"][:], cx.ps[4][:], AF.Sqrt, ["ps4"], ["fe_rstd"], bias=1e-6, scale=1.0 / D)
    cx.recip(fe["rstd"][:], fe["rstd"][:], ["fe_rstd"], ["fe_rstd"])
    if b > 0:
        cx.cp("pool", fe["xh"][:, :, 0:1], fe["xh"][:, :, BLK:BLK + 1], ["fe_xh"], ["fe_xh"])
    for dc in range(NCH):
        eng = "dve" if dc % 2 == 0 else "pool"
        cx.tt(eng, fe["xh"][:, dc, 1:BLK + 1], fe["xT"][:, dc, :], fe["rstd"][:], ALU.mult,
              ["fe_xT", "fe_rstd"], ["fe_xh"])


C_DEC = 0.6065306597126334
HC = 256


def build_rwkv(SL, layer, STOP=99):
    has_prev = layer > 0
    BLK = 512
    NB = SL // BLK
    nc = bass.Bass("TRN2", target_bir_lowering=False)
    with ExitStack() as es:
        cx = Ctx(nc, es)
        ps = cx.ps
        xT = cx.dram("xT", [D, SL], F32, "ExternalInput")
        win = cx.dram("win", [4, D, HC], F32, "ExternalInput")
        gmu = cx.dram("gmu", [128, 7, NCH], F32, "ExternalInput")
        l1w = cx.dram("l1w", [D, 160], F32, "ExternalInput")
        l2w = cx.dram("l2w", [64, 3, HC], F32, "ExternalInput")
        vecs = cx.dram("vecs", [8, HC], F32, "ExternalInput")
        cst = cx.dram("cst", [128, 1152], F32, "ExternalInput")
        oT_out = cx.dram("oT_out", [HC, SL], BF16, "ExternalOutput")
        if has_prev:
            oT = cx.dram("oT", [D, SL], BF16, "ExternalInput")
            wout = cx.dram("wout", [D, D], F32, "ExternalInput")
            vfirst = cx.dram("vfirst", [SL, HC], F32, "ExternalInput")
            xT_out = cx.dram("xT_out", [D, SL], F32, "ExternalOutput")
        else:
            oT = wout = xT_out = None
            vfirst_out = cx.dram("vfirst_out", [SL, HC], F32, "ExternalOutput")
        fe = frontend_alloc(cx, has_prev, BLK)
        stage = cx.sb("stage", [128, NCH, HC * 2], F32)
        csb = cx.sb("csb", [128, 1152], F32)
        ident_bf = cx.sb("ident_bf", [128, 128], BF16)
        gm = cx.sb("gm", [128, 7, NCH], F32)
        coef = cx.sb("coef", [128, 12, NCH], F32)
        Wc = cx.sb("Wc", [128, NCH, 4 * HC], BF16)
        Wp = cx.sb("Wp", [128, NCH, 4 * HC], BF16)
        L1c = cx.sb("L1c", [128, NCH, 160], BF16)
        L1p = cx.sb("L1p", [128, NCH, 160], BF16)
        L2 = cx.sb("L2", [64, 3, HC], BF16)
        L2f = cx.sb("L2f", [64, 3, HC], F32)
        vb = cx.sb("vb", [128, 8, HC], F32)
        h1 = cx.sb("h1", [64, 3, BLK], BF16)
        ST = cx.sb("ST", [64, 4, 64], F32)
        STb = cx.sb("STb", [64, 4, 64], BF16)
        gl = cx.sb("gl", [64, 4], F32)

        def T(name, dt=F32, w=HC):
            return cx.sb(name, [128, w], dt)
        r_sb, k_sb, v_sb, gate, sg, a_sb = T("r_sb"), T("k_sb"), T("v_sb"), T("gate"), T("sg"), T("a_sb")
        t1, t2, t3, kkn, kmod, bvec = T("t1"), T("t2"), T("t3"), T("kkn"), T("kmod"), T("bvec")
        epos, eneg, eprev, ehat = T("epos"), T("eneg"), T("eprev"), T("ehat")
        ss4, bs4, s14, s24 = T("ss4", w=4), T("bs4", w=4), T("s14", w=4), T("s24", w=4)
        tmT = cx.sb("tmT", [128, 4, HC], BF16)
        khat, bhat, vbf = T("khat", BF16), T("bhat", BF16), T("vbf", BF16)
        fm = cx.sb("fm", [64, 4, 4, 128], BF16)
        Am = cx.sb("Am", [128, 4, 512], BF16)
        Nf = cx.sb("Nf", [128, 4, 2, 128], F32)
        NTf = cx.sb("NTf", [128, 4, 2, 128], F32)
        Tf = cx.sb("Tf", [128, 4, 2, 128], F32)
        Wf = cx.sb("Wf", [128, 4, 64], F32)
        Zb = cx.sb("Zb", [128, 4, 64], BF16)
        y_sb, yc, o_bf = T("y_sb"), T("yc"), T("o_bf", BF16)
        oT_sb = cx.sb("oT_sb", [128, 2, 128], BF16)
        vf_sb = T("vf_sb")

        cx.dma(csb[:], cst[:, :], [], ["csb"], "su0")
        cx.dma(gm[:], gmu[:, :, :], [], ["gm"], "su1")
        cx.dma(L2f[:], l2w[:, :, :], [], ["L2f"], "su2")
        for i in range(8):
            cx.dma(vb[:, i, :], vecs[i:i + 1, :].partition_broadcast(128), [], ["vb"], "su3")
        ident_f = csb[:, 0:128]
        tri = csb[:, 128:256]
        sfx = csb[:, 256:384]
        mask4 = csb[:, 384:896]
        maskNT = csb[:, 896:1024]
        ones_col = csb[:, 1024:1025]
        cx.cp("dve", ident_bf[:], ident_f, ["csb"], ["ident_bf"])
        cx.cp("dve", L2[:], L2f[:], ["L2f"], ["L2"])
        for p in range(6):
            cx.tt("dve", coef[:, 6 + p, :], gm[:, 0, :], gm[:, 1 + p, :], ALU.mult, ["gm"], ["coef"])
            cx.tt("dve", coef[:, p, :], gm[:, 0, :], coef[:, 6 + p, :], ALU.subtract, ["gm", "coef"], ["coef"])
        for p in range(4):
            cx.dma(stage[:, :, 0:HC], win[p].rearrange("(c p) n -> p c n", p=128), [], ["stage"], "stage")
            cx.tt("dve", Wc[:, :, p * HC:(p + 1) * HC], stage[:, :, 0:HC],
                  coef[:, p, :].unsqueeze(2).to_broadcast([128, NCH, HC]), ALU.mult, ["stage", "coef"], ["Wc"])
            cx.tt("pool", Wp[:, :, p * HC:(p + 1) * HC], stage[:, :, 0:HC],
                  coef[:, 6 + p, :].unsqueeze(2).to_broadcast([128, NCH, HC]), ALU.mult, ["stage", "coef"], ["Wp"])
        cx.dma(stage[:, :, 0:160], l1w.rearrange("(c p) n -> p c n", p=128), [], ["stage"], "stage")
        for (lo, hi, p) in ((0, 64, 4), (64, 128, 5), (128, 160, 2)):
            cx.tt("dve", L1c[:, :, lo:hi], stage[:, :, lo:hi],
                  coef[:, p, :].unsqueeze(2).to_broadcast([128, NCH, hi - lo]), ALU.mult, ["stage", "coef"], ["L1c"])
            cx.tt("pool", L1p[:, :, lo:hi], stage[:, :, lo:hi],
                  coef[:, 6 + p, :].unsqueeze(2).to_broadcast([128, NCH, hi - lo]), ALU.mult, ["stage", "coef"], ["L1p"])
        if has_prev:
            wv = wout.rearrange("(c p) d -> p c d", p=128)
            for kc in range(NCH):
                cx.dma(stage[:, 0:2, :], wv[:, kc, :].rearrange("p (a n) -> p a n", a=2), [], ["stage"], "stage")
                cx.cp("dve", fe["wo"][:, kc, :].rearrange("p (a n) -> p a n", a=2), stage[:, 0:2, :], ["stage"], ["fe_wo"])
        cx.memset("dve", ST[:], 0.0, ["ST"])
        cx.memset("dve", STb[:], 0.0, ["STb"])

        w0_b, a0_b, v0_b, kk_b, ka_b, rk_b, lnw_b, lnb_b = [vb[:, i, :] for i in range(8)]

        def bc4(ap4):
            return ap4.unsqueeze(2).to_broadcast([128, 4, 64])

        def v3(ap):
            return ap.rearrange("p (h n) -> p h n", h=4)

        for b in range(NB):
            frontend_block(cx, fe, xT, oT, xT_out, b, has_prev, BLK)
            xh = fe["xh"]
            for j, (lo, hi) in enumerate(((0, 64), (64, 128), (128, 160))):
                m = hi - lo
                bank = 5 + j
                for kc in range(NCH):
                    cx.mm(ps[bank][0:m, :], L1c[:, kc, lo:hi], xh[:, kc, 1:BLK + 1], ["L1c", "fe_xh"], ["ps%d" % bank],
                          start=(kc == 0), stop=False)
                    cx.mm(ps[bank][0:m, :], L1p[:, kc, lo:hi], xh[:, kc, 0:BLK], ["L1p", "fe_xh"], ["ps%d" % bank],
                          start=False, stop=(kc == NCH - 1))
                if j == 0:
                    cx.act(h1[0:m, j, :], ps[bank][0:m, :], AF.Tanh, ["ps%d" % bank], ["h1"])
                else:
                    cx.cp("dve", h1[0:m, j, :], ps[bank][0:m, :], ["ps%d" % bank], ["h1"])
            if STOP <= 1:
                break
            for ti in range(BLK // 128):
                tok0 = b * BLK + ti * 128
                c0 = 1 + ti * 128
                for (bank, lo) in ((0, 0), (1, 512)):
                    for kc in range(NCH):
                        cx.mm(ps[bank][:], xh[:, kc, c0:c0 + 128], Wc[:, kc, lo:lo + 512], ["fe_xh", "Wc"], ["ps%d" % bank],
                              start=(kc == 0), stop=False)
                        cx.mm(ps[bank][:], xh[:, kc, c0 - 1:c0 + 127], Wp[:, kc, lo:lo + 512], ["fe_xh", "Wp"], ["ps%d" % bank],
                              start=False, stop=(kc == NCH - 1))
                hs = slice(ti * 128, (ti + 1) * 128)
                cx.mm(ps[2][:, 0:HC], h1[0:64, 0, hs], L2[0:64, 0, :], ["h1", "L2"], ["ps2"])
                cx.mm(ps[2][:, HC:2 * HC], h1[0:64, 1, hs], L2[0:64, 1, :], ["h1", "L2"], ["ps2"])
                if has_prev:
                    cx.mm(ps[3][:, 0:HC], h1[0:32, 2, hs], L2[0:32, 2, :], ["h1", "L2"], ["ps3"])
                if STOP <= 2:
                    break
                cx.cp("act", r_sb[:], ps[0][:, 0:HC], ["ps0"], ["r_sb"])
                cx.cp("dve", k_sb[:], ps[0][:, HC:2 * HC], ["ps0"], ["k_sb"])
                cx.cp("act", v_sb[:], ps[1][:, 0:HC], ["ps1"], ["v_sb"])
                cx.act(t1[:], ps[1][:, HC:2 * HC], AF.Sigmoid, ["ps1"], ["t1"])
                cx.tt("dve", gate[:], ps[1][:, HC:2 * HC], t1[:], ALU.mult, ["ps1", "t1"], ["gate"])
                cx.tt("dve", t2[:], ps[2][:, 0:HC], w0_b, ALU.add, ["ps2", "vb"], ["t2"])
                cx.act(sg[:], t2[:], AF.Sigmoid, ["t2"], ["sg"])
                cx.tt("dve", t3[:], ps[2][:, HC:2 * HC], a0_b, ALU.add, ["ps2", "vb"], ["t3"])
                cx.act(a_sb[:], t3[:], AF.Sigmoid, ["t3"], ["a_sb"])
                if has_prev:
                    cx.dma(vf_sb[:], vfirst[tok0:tok0 + 128, :], [], ["vf_sb"], "vf")
                    cx.tt("dve", t2[:], ps[3][:, 0:HC], v0_b, ALU.add, ["ps3", "vb"], ["t2"])
                    cx.act(t2[:], t2[:], AF.Sigmoid, ["t2"], ["t2"])
                    cx.tt("pool", t3[:], vf_sb[:], v_sb[:], ALU.subtract, ["vf_sb", "v_sb"], ["t3"])
                    cx.tt("pool", t3[:], t3[:], t2[:], ALU.mult, ["t3", "t2"], ["t3"])
                    cx.tt("pool", v_sb[:], v_sb[:], t3[:], ALU.add, ["v_sb", "t3"], ["v_sb"])
                else:
                    cx.dma(vfirst_out[tok0:tok0 + 128, :], v_sb[:], ["v_sb"], [], "vfo", is_out=True)
                cx.mm(ps[4][:, 0:HC], tri, sg[:], ["csb", "sg"], ["ps4"])
                cx.mm(ps[4][:, HC:2 * HC], sfx, sg[:], ["csb", "sg"], ["ps4"])
                for h in range(4):
                    cx.mm(ps[3][0:64, HC + h:HC + h + 1], sg[:, h * 64:(h + 1) * 64], ones_col, ["sg", "csb"], ["ps3b"])
                cx.act(gl[:], ps[3][0:64, HC:HC + 4], AF.Exp, ["ps3b"], ["gl"], scale=-C_DEC)
                cx.act(epos[:], ps[4][:, 0:HC], AF.Exp, ["ps4"], ["epos"], scale=-C_DEC)
                cx.act(eneg[:], ps[4][:, 0:HC], AF.Exp, ["ps4"], ["eneg"], scale=C_DEC)
                cx.tt("dve", t2[:], ps[4][:, 0:HC], sg[:], ALU.subtract, ["ps4", "sg"], ["t2"])
                cx.act(eprev[:], t2[:], AF.Exp, ["t2"], ["eprev"], scale=-C_DEC)
                cx.act(ehat[:], ps[4][:, HC:2 * HC], AF.Exp, ["ps4"], ["ehat"], scale=-C_DEC)
                cx.tt("pool", kkn[:], k_sb[:], kk_b, ALU.mult, ["k_sb", "vb"], ["kkn"])
                cx.tt("pool", t1[:], kkn[:], kkn[:], ALU.mult, ["kkn"], ["t1"])
                cx.red("dve", ss4[:], v3(t1[:]), ["t1"], ["ss4"])
                cx.act(ss4[:], ss4[:], AF.Sqrt, ["ss4"], ["ss4"])
                cx.ts("dve", ss4[:], ss4[:], 1e-12, None, ALU.max, None, ["ss4"], ["ss4"])
                cx.recip(ss4[:], ss4[:], ["ss4"], ["ss4"])
                cx.tt("dve", v3(kkn[:]), v3(kkn[:]), bc4(ss4[:]), ALU.mult, ["kkn", "ss4"], ["kkn"])
                cx.stt("dve", t3[:], a_sb[:], -1.0, ka_b, ALU.add, ALU.mult, ["a_sb", "vb"], ["t3"])
                cx.stt("dve", kmod[:], t3[:], 1.0, k_sb[:], ALU.add, ALU.mult, ["t3", "k_sb"], ["kmod"])
                cx.tt("pool", bvec[:], kkn[:], a_sb[:], ALU.mult, ["kkn", "a_sb"], ["bvec"])
                cx.tt("pool", t1[:], r_sb[:], kmod[:], ALU.mult, ["r_sb", "kmod"], ["t1"])
                cx.tt("pool", t1[:], t1[:], rk_b, ALU.mult, ["t1", "vb"], ["t1"])
                cx.red("dve", bs4[:], v3(t1[:]), ["t1"], ["bs4"])
                cx.stt("dve", tmT[:, 0, :], kkn[:], -1.0, eprev[:], ALU.mult, ALU.mult, ["kkn", "eprev"], ["tmT0"])
                cx.tt("pool", tmT[:, 1, :], r_sb[:], epos[:], ALU.mult, ["r_sb", "epos"], ["tmT1"])
                cx.tt("dve", tmT[:, 2, :], bvec[:], eneg[:], ALU.mult, ["bvec", "eneg"], ["tmT2"])
                cx.tt("pool", tmT[:, 3, :], kmod[:], eneg[:], ALU.mult, ["kmod", "eneg"], ["tmT3"])
                cx.tt("dve", khat[:], kmod[:], ehat[:], ALU.mult, ["kmod", "ehat"], ["khat"])
                cx.tt("pool", bhat[:], bvec[:], ehat[:], ALU.mult, ["bvec", "ehat"], ["bhat"])
                cx.cp("act", vbf[:], v_sb[:], ["v_sb"], ["vbf"])
                if STOP <= 3:
                    break
                for q in range(4):
                    bank = 5 + (q // 2)
                    pv = ps[bank][:].bitcast(BF16)
                    for h in range(4):
                        col = ((q % 2) * 4 + h) * 128
                        cx.tr(pv[0:64, col:col + 128], tmT[:, q, h * 64:(h + 1) * 64], ident_bf[:],
                              ["tmT%d" % q, "ident_bf"], ["ps%d" % bank])
                for half in range(2):
                    pv = ps[5 + half][:].bitcast(BF16)
                    src = pv[0:64, :].rearrange("k (q h t) -> k h q t", q=2, h=4)
                    cx.cp("act" if half == 0 else "dve", fm[:, :, 2 * half:2 * half + 2, :], src, ["ps%d" % (5 + half)], ["fm"])
                if STOP <= 4:
                    break
                for h in range(4):
                    bank = h % 2
                    ar = fm[:, h, 0:2, :].rearrange("k q t -> k (q t)")
                    cx.mm(ps[bank][:, 0:256], fm[:, h, 2, :], ar, ["fm"], ["ps%d" % bank])
                    cx.mm(ps[bank][:, 256:512], fm[:, h, 3, :], ar, ["fm"], ["ps%d" % bank])
                    cx.mm(ps[2 + bank][:, 0:128], fm[:, h, 0, :], fm[:, h, 2, :], ["fm"], ["ps%da" % (2 + bank)])
                    cx.tt("dve", Am[:, h, :], ps[bank][:], mask4, ALU.mult, ["ps%d" % bank, "csb"], ["Am%d" % h])
                    cx.tt("dve", Nf[:, h, 0, :], ps[bank][:, 0:128], mask4[:, 0:128], ALU.mult, ["ps%d" % bank, "csb"], ["Nf%d" % h])
                    cx.tt("dve", NTf[:, h, 0, :], ps[2 + bank][:, 0:128], maskNT, ALU.mult, ["ps%da" % (2 + bank), "csb"], ["NTf%d" % h])
                    cx.tt("pool", Tf[:, h, 0, :], Nf[:, h, 0, :], ident_f, ALU.add, ["Nf%d" % h, "csb"], ["Tf%d" % h])
                if STOP <= 5:
                    break
                for kstep in range(1, 7):
                    src, dst = (kstep - 1) % 2, kstep % 2
                    for h in range(4):
                        bk = 4 + h
                        cx.mm(ps[bk][:, 0:128], Nf[:, h, src, :], NTf[:, h, src, :], ["Nf%d" % h, "NTf%d" % h], ["ps%d" % bk])
                        if kstep < 6:
                            cx.mm(ps[bk][:, 128:256], NTf[:, h, src, :], Nf[:, h, src, :], ["Nf%d" % h, "NTf%d" % h], ["ps%d" % bk])
                    for h in range(4):
                        bk = 4 + h
                        cx.cp("act", NTf[:, h, dst, :], ps[bk][:, 0:128], ["ps%d" % bk], ["NTf%d" % h])
                        if kstep < 6:
                            cx.cp("dve", Nf[:, h, dst, :], ps[bk][:, 128:256], ["ps%d" % bk], ["Nf%d" % h])
                    for h in range(4):
                        bk = 4 + h
                        cx.mm(ps[bk][:, 256:384], NTf[:, h, dst, :], Tf[:, h, src, :], ["NTf%d" % h, "Tf%d" % h], ["ps%d" % bk])
                    for h in range(4):
                        bk = 4 + h
                        cx.tt("dve", Tf[:, h, dst, :], ps[bk][:, 256:384], Tf[:, h, src, :], ALU.add, ["ps%d" % bk, "Tf%d" % h], ["Tf%d" % h])
                TI = 6 % 2
                if STOP <= 6:
                    break
                HS = [slice(h * 64, (h + 1) * 64) for h in range(4)]
                for h in range(4):
                    cx.mm(ps[0][:, HS[h]], fm[:, h, 0, :], STb[:, h, :], ["fm", "STb"], ["ps0"], start=True, stop=False)
                    cx.mm(ps[0][:, HS[h]], Am[:, h, 256:384], vbf[:, HS[h]], ["Am%d" % h, "vbf"], ["ps0"], start=False, stop=True)
                cx.cp("act", Wf[:].rearrange("p h v -> p (h v)"), ps[0][:, 0:HC], ["ps0"], ["Wf"])
                for h in range(4):
                    cx.mm(ps[1][:, HS[h]], Tf[:, h, TI, :], Wf[:, h, :], ["Tf%d" % h, "Wf"], ["ps1"])
                cx.cp("dve", Zb[:].rearrange("p h v -> p (h v)"), ps[1][:, 0:HC], ["ps1"], ["Zb"])
                for h in range(4):
                    cx.mm(ps[2][:, HS[h]], fm[:, h, 1, :], STb[:, h, :], ["fm", "STb"], ["ps2"], start=True, stop=False)
                    cx.mm(ps[2][:, HS[h]], Am[:, h, 128:256], Zb[:, h, :], ["Am%d" % h, "Zb"], ["ps2"], start=False, stop=False)
                    cx.mm(ps[2][:, HS[h]], Am[:, h, 384:512], vbf[:, HS[h]], ["Am%d" % h, "vbf"], ["ps2"], start=False, stop=True)
                for h in range(4):
                    cx.mm(ps[3][0:64, HS[h]], bhat[:, HS[h]], Zb[:, h, :], ["bhat", "Zb"], ["ps3"], start=True, stop=False)
                    cx.mm(ps[3][0:64, HS[h]], khat[:, HS[h]], vbf[:, HS[h]], ["khat", "vbf"], ["ps3"], start=False, stop=True)
                cx.tt("dve", ST[:], ST[:], gl[:].unsqueeze(2).to_broadcast([64, 4, 64]), ALU.mult, ["ST", "gl"], ["ST"])
                cx.tt("dve", ST[:].rearrange("p h v -> p (h v)"), ST[:].rearrange("p h v -> p (h v)"), ps[3][0:64, 0:HC], ALU.add,
                      ["ST", "ps3"], ["ST"])
                cx.cp("act", STb[:], ST[:], ["ST"], ["STb"])
                if STOP <= 7:
                    break
                cx.cp("act", y_sb[:], ps[2][:, 0:HC], ["ps2"], ["y_sb"])
                cx.red("dve", s14[:], v3(y_sb[:]), ["y_sb"], ["s14"])
                cx.ts("dve", s14[:], s14[:], -1.0 / 64, None, ALU.mult, None, ["s14"], ["s14"])
                cx.tt("dve", v3(yc[:]), v3(y_sb[:]), bc4(s14[:]), ALU.add, ["y_sb", "s14"], ["yc"])
                cx.tt("pool", t1[:], yc[:], yc[:], ALU.mult, ["yc"], ["t1"])
                cx.red("dve", s24[:], v3(t1[:]), ["t1"], ["s24"])
                cx.act(s24[:], s24[:], AF.Sqrt, ["s24"], ["s24"], bias=64e-5, scale=1.0 / 64)
                cx.recip(s24[:], s24[:], ["s24"], ["s24"])
                cx.tt("dve", v3(yc[:]), v3(yc[:]), bc4(s24[:]), ALU.mult, ["yc", "s24"], ["yc"])
                cx.tt("pool", yc[:], yc[:], lnw_b, ALU.mult, ["yc", "vb"], ["yc"])
                cx.tt("pool", yc[:], yc[:], lnb_b, ALU.add, ["yc", "vb"], ["yc"])
                cx.tt("dve", v3(t2[:]), v3(v_sb[:]), bc4(bs4[:]), ALU.mult, ["v_sb", "bs4"], ["t2"])
                cx.tt("pool", yc[:], yc[:], t2[:], ALU.add, ["yc", "t2"], ["yc"])
                cx.tt("dve", o_bf[:], yc[:], gate[:], ALU.mult, ["yc", "gate"], ["o_bf"])
                pv = ps[1][:].bitcast(BF16)
                for j in range(2):
                    cx.tr(pv[:, 512 + j * 128:512 + (j + 1) * 128], o_bf[:, j * 128:(j + 1) * 128], ident_bf[:],
                          ["o_bf", "ident_bf"], ["ps1o"])
                cx.cp("act", oT_sb[:], pv[:, 512:768].rearrange("p (j t) -> p j t", j=2), ["ps1o"], ["oT_sb"])
                cx.dma(oT_out.rearrange("(j p) s -> p j s", p=128)[:, :, tok0:tok0 + 128], oT_sb[:], ["oT_sb"], [], "oTo", is_out=True)
        cx.done()
    return nc


def _consts():
    c = np.zeros((128, 1152), np.float32)
    i = np.arange(128)
    c[:, 0:128] = np.eye(128)
    c[:, 128:256] = (i[:, None] <= i[None, :])
    c[:, 256:384] = (i[:, None] > i[None, :])
    su = (i[:, None] < i[None, :]).astype(np.float32)
    iu = (i[:, None] <= i[None, :]).astype(np.float32)
    c[:, 384:512] = su
    c[:, 512:640] = iu
    c[:, 640:768] = su
    c[:, 768:896] = iu
    c[:, 896:1024] = (i[:, None] > i[None, :])
    c[:, 1024] = 1.0
    return c


def fm8(v):
    return np.ascontiguousarray(np.asarray(v).reshape(NCH, 128).T)


def rwkv_in_map(inp, layer, g, xT_b, oT_b=None, vfirst_c=None):
    cs = slice(g * HC, (g + 1) * HC)
    m = {}
    m["xT"] = xT_b
    m["win"] = np.ascontiguousarray(inp["a_w_in"][layer][:, :, cs])
    gmu = np.zeros((128, 7, NCH), np.float32)
    gmu[:, 0, :] = fm8(inp["a_norm"][layer])
    for p in range(6):
        gmu[:, 1 + p, :] = fm8(inp["a_mu"][layer][p])
    m["gmu"] = gmu
    l1 = np.zeros((D, 160), np.float32)
    l1[:, 0:64] = inp["a_w1"][layer]
    l1[:, 64:128] = inp["a_a1"][layer]
    l2 = np.zeros((64, 3, HC), np.float32)
    l2[:, 0, :] = inp["a_w2"][layer][:, cs]
    l2[:, 1, :] = inp["a_a2"][layer][:, cs]
    vecs = np.zeros((8, HC), np.float32)
    vecs[0] = inp["a_w0"][layer][cs]
    vecs[1] = inp["a_a0"][layer][cs]
    if layer > 0:
        l1[:, 128:160] = inp["a_v1"][layer - 1]
        l2[0:32, 2, :] = inp["a_v2"][layer - 1][:, cs]
        vecs[2] = inp["a_v0"][layer - 1][cs]
    vecs[3] = inp["a_k_k"][layer][cs]
    vecs[4] = inp["a_k_a"][layer][cs]
    vecs[5] = inp["a_r_k"][layer].reshape(-1)[cs]
    vecs[6] = inp["a_ln_w"][layer][cs]
    vecs[7] = inp["a_ln_b"][layer][cs]
    m["l1w"], m["l2w"], m["vecs"] = l1, l2, vecs
    m["cst"] = _consts()
    if layer > 0:
        m["oT"] = oT_b
        m["wout"] = np.ascontiguousarray(inp["a_w_out"][layer - 1])
        m["vfirst"] = vfirst_c
    return m


def build_xupd(NT, final):
    BLK = 512
    nc = bass.Bass("TRN2", target_bir_lowering=False)
    with ExitStack() as es:
        cx = Ctx(nc, es)
        xT = cx.dram("xT", [D, NT], F32, "ExternalInput")
        oT = cx.dram("oT", [D, NT], BF16, "ExternalInput")
        wout = cx.dram("wout", [D, D], F32, "ExternalInput")
        xT_out = cx.dram("xT_out", [D, NT], F32, "ExternalOutput")
        fe = frontend_alloc(cx, True, BLK)
        stage = cx.sb("stage", [128, 2, 512], F32)
        wv = wout.rearrange("(c p) d -> p c d", p=128)
        for kc in range(NCH):
            cx.dma(stage[:], wv[:, kc, :].rearrange("p (a n) -> p a n", a=2), [], ["stage"], "stage")
            cx.cp("dve", fe["wo"][:, kc, :].rearrange("p (a n) -> p a n", a=2), stage[:], ["stage"], ["fe_wo"])
        if final:
            gfin = cx.dram("gfin", [128, NCH], F32, "ExternalInput")
            g_sb = cx.sb("g_sb", [128, NCH], F32)
            cx.dma(g_sb[:], gfin[:, :], [], ["g_sb"], "su1")
            yT = cx.sb("yT", [128, NCH, BLK], F32)
        for b in range(NT // BLK):
            frontend_block(cx, fe, xT, oT, xT_out, b, True, BLK, store_x=not final, norm=final)
            if final:
                for dc in range(NCH):
                    cx.stt("dve", yT[:, dc, :], fe["xT"][:, dc, :], g_sb[:, dc:dc + 1], fe["rstd"][:], ALU.mult, ALU.mult,
                           ["fe_xT", "g_sb", "fe_rstd"], ["yT"])
                ovx = xT_out.rearrange("(c p) s -> p c s", p=128)
                cx.dma(ovx[:, :, b * BLK:(b + 1) * BLK], yT[:], ["yT"], [], "yo", is_out=True)
        cx.done()
    return nc


LAM_INIT = {2: 0.8 - 0.6 * float(np.exp(-0.3 * 2)), 3: 0.8 - 0.6 * float(np.exp(-0.3 * 3))}


def build_attn(SL, mode):
    BLK = 512
    NB = SL // BLK
    NKT = SL // 128
    kv = (mode == "kv")
    nc = bass.Bass("TRN2", target_bir_lowering=False)
    with ExitStack() as es:
        cx = Ctx(nc, es)
        ps = cx.ps
        xT = cx.dram("xT", [D, SL], F32, "ExternalInput")
        pos = cx.dram("pos", [1, SL], I32, "ExternalInput")
        wq = cx.dram("wq", [D, 512], F32, "ExternalInput")
        gn = cx.dram("gn", [128, NCH], F32, "ExternalInput")
        invf_d = cx.dram("invf", [128, 1], F32, "ExternalInput")
        if kv:
            KT_io = cx.dram("KT_out", [2, 128, SL], BF16, "ExternalOutput")
            V_io = cx.dram("V_out", [SL, 256], BF16, "ExternalOutput")
        else:
            lam_in = cx.dram("lam_in", [4, 64], F32, "ExternalInput")
            subw = cx.dram("subw", [128, 1], F32, "ExternalInput")
            mask_d = cx.dram("mask", [128, 4, 512], BF16, "ExternalInput")
            KT_io = cx.dram("KT_in", [2, 128, SL], BF16, "ExternalInput")
            V_io = cx.dram("V_in", [SL, 256], BF16, "ExternalInput")
            oT_out = cx.dram("oT_out", [256, SL], BF16, "ExternalOutput")
        fe = frontend_alloc(cx, False, BLK)
        Wq = cx.sb("Wq", [128, NCH, 512], BF16)
        Wqr = cx.sb("Wqr", [128, NCH, 256], BF16)
        gsb = cx.sb("gsb", [128, NCH], F32)
        invf = cx.sb("invf_sb", [128, 1], F32)
        posi = cx.sb("posi", [128, BLK], I32)
        ang = cx.sb("ang", [128, BLK], F32)
        Ct = cx.sb("Ct", [128, BLK], F32)
        St = cx.sb("St", [128, BLK], F32)
        A1 = cx.sb("A1", [128, BLK], F32)
        A2 = cx.sb("A2", [128, BLK], F32)
        sqa = cx.sb("sqa", [128, BLK], F32)
        if kv:
            KTb = cx.sb("KTb", [128, 2, BLK], BF16)
            VAb = cx.sb("VAb", [128, 4, 256], BF16)
        else:
            KT = cx.sb("KT", [128, 2, SL], BF16)
            VA = cx.sb("VA", [128, NKT, 256], BF16)
            maskb = cx.sb("maskb", [128, 4, 512], BF16)
            ones_bf = cx.sb("ones_bf", [128, 128], BF16)
            lam_t = cx.sb("lam_t", [128, 4, 64], F32)
            lam_s = cx.sb("lam_s", [128, 4], F32)
            nlam = cx.sb("nlam", [128, 1], F32)
            sw = cx.sb("sw", [128, 1], F32)
            Qp = cx.sb("Qp", [128, 2, 2, BLK], BF16)
            gT = cx.sb("gT", [128, 2, BLK], BF16)
            pt = [cx.sb("pt%d" % i, [128, BLK], BF16) for i in range(4)]
            rl = [cx.sb("rl%d" % i, [128, BLK], F32) for i in range(2)]
            ob = cx.sb("ob", [128, BLK], BF16)
        stage = fe["xT"]
        cx.dma(gsb[:], gn[:, :], [], ["gsb"], "su1")
        cx.dma(invf[:], invf_d[:, :], [], ["invf"], "su0")
        if not kv:
            cx.dma(sw[:], subw[:, :], [], ["sw"], "su2")
            cx.dma(maskb[:], mask_d[:, :, :], [], ["maskb"], "su4")
            for i in range(4):
                cx.dma(lam_t[:, i, :], lam_in[i:i + 1, :].partition_broadcast(128), [], ["lam_t"], "su3")
            cx.memset("pool", ones_bf[:], 1.0, ["ones_bf"])
            cx.memset("pool", Qp[:], 0.0, ["Qp"])
            lam_init = LAM_INIT[mode]
            cx.tt("dve", lam_t[:, 0, :], lam_t[:, 0, :], lam_t[:, 1, :], ALU.mult, ["lam_t"], ["lam_t"])
            cx.tt("dve", lam_t[:, 2, :], lam_t[:, 2, :], lam_t[:, 3, :], ALU.mult, ["lam_t"], ["lam_t"])
            cx.red("dve", lam_s[:, 0:1], lam_t[:, 0, :], ["lam_t"], ["lam_s"])
            cx.red("dve", lam_s[:, 1:2], lam_t[:, 2, :], ["lam_t"], ["lam_s"])
            cx.act(lam_s[:, 0:2], lam_s[:, 0:2], AF.Exp, ["lam_s"], ["lam_s"])
            cx.tt("dve", lam_s[:, 2:3], lam_s[:, 1:2], lam_s[:, 0:1], ALU.subtract, ["lam_s"], ["lam_s"])
            cx.ts("dve", nlam[:], lam_s[:, 2:3], -lam_init, None, ALU.add, None, ["lam_s"], ["nlam"])
            cx.ts("dve", sw[:], sw[:], 1.0 - lam_init, None, ALU.mult, None, ["sw"], ["sw"])
        cx.dma(stage[:], wq.rearrange("(c p) n -> p c n", p=128), [], ["fe_xT"], "stage")
        cx.tt("dve", Wq[:], stage[:], gsb[:].unsqueeze(2).to_broadcast([128, NCH, 512]), ALU.mult, ["fe_xT", "gsb"], ["W"])
        cx.memset("pool", Wqr[:], 0.0, ["Wr"])
        for mp in range(4):
            c = mp * 64
            cx.ts("dve", Wqr[:, :, c:c + 8], Wq[:, :, c + 8:c + 16], -1.0, None, ALU.mult, None, ["W"], ["Wr"])
            cx.cp("dve", Wqr[:, :, c + 8:c + 16], Wq[:, :, c:c + 8], ["W"], ["Wr"])
        if not kv:
            for j in range(2):
                for q4 in range(4):
                    sl = slice(q4 * (SL // 4), (q4 + 1) * (SL // 4))
                    cx.dma(KT[:, j, sl], KT_io[j, :, sl], [], ["KT"], "ktl")
            vv = V_io.rearrange("(t p) n -> p t n", p=128)
            for q4 in range(4):
                sl = slice(q4 * (NKT // 4), (q4 + 1) * (NKT // 4))
                cx.dma(VA[:, sl, :], vv[:, sl, :], [], ["VA"], "val")

        M = 12582912.0
        TWO_PI = float(2 * np.pi)

        def sin_table(dst, dkey, shift):
            cx.ts("dve", A1[:], ang[:], shift, None, ALU.add, None, ["ang"], ["A1"])
            cx.ts("dve", A2[:], A1[:], 1.0 / TWO_PI, M, ALU.mult, ALU.add, ["A1"], ["A2"])
            cx.ts("dve", A2[:], A2[:], M, None, ALU.subtract, None, ["A2"], ["A2"])
            cx.stt("dve", A1[:], A2[:], -TWO_PI, A1[:], ALU.mult, ALU.add, ["A2", "A1"], ["A1"])
            cx.ts("dve", A1[:], A1[:], -3.1415925, 3.1415925, ALU.max, ALU.min, ["A1"], ["A1"])
            cx.act(dst[:], A1[:], AF.Sin, ["A1"], [dkey])

        def proj_fm(W, wkey, c0, bank):
            for kc in range(NCH):
                cx.mm(ps[bank][:], W[:, kc, c0:c0 + 128], fe["xh"][:, kc, 1:BLK + 1], [wkey, "fe_xh"], ["ps%d" % bank],
                      start=(kc == 0), stop=(kc == NCH - 1))

        def roped(j):
            proj_fm(Wq, "W", j * 128, 6)
            proj_fm(Wqr, "Wr", j * 128, 7)
            cx.tt("dve", A1[:], ps[6][:], Ct[:], ALU.mult, ["ps6", "Ct"], ["A1"])
            cx.tt("dve", A2[:], ps[7][:], St[:], ALU.mult, ["ps7", "St"], ["A2"])

        for b in range(NB):
            t0 = b * BLK
            frontend_block(cx, fe, xT, None, None, b, False, BLK)
            cx.dma(posi[:], pos[0:1, t0:t0 + BLK].partition_broadcast(128), [], ["posi"], "posl")
            cx.cp("dve", sqa[:], posi[:], ["posi"], ["sqa"])
            cx.ts("dve", ang[:], sqa[:], invf[:, 0:1], None, ALU.mult, None, ["sqa", "invf"], ["ang"])
            sin_table(St, "St", 0.0)
            sin_table(Ct, "Ct", float(np.pi / 2))
            for j in range(2):
                roped(j)
                if kv:
                    cx.tt("pool", KTb[:, j, :], A1[:], A2[:], ALU.add, ["A1", "A2"], ["KTb"])
                    for ti in range(4):
                        for kc in range(NCH):
                            cx.mm(ps[5][:, 0:128], fe["xh"][:, kc, 1 + ti * 128:1 + (ti + 1) * 128], Wq[:, kc, 256 + j * 128:256 + (j + 1) * 128],
                                  ["fe_xh", "W"], ["ps5"], start=(kc == 0), stop=(kc == NCH - 1))
                        cx.cp("act", VAb[:, ti, j * 128:(j + 1) * 128], ps[5][:, 0:128], ["ps5"], ["VAb"])
                else:
                    cx.tt("pool", Qp[0:64, j, 0, :], A1[0:64, :], A2[0:64, :], ALU.add, ["A1", "A2"], ["Qp"])
                    cx.tt("pool", Qp[64:128, j, 1, :], A1[64:128, :], A2[64:128, :], ALU.add, ["A1", "A2"], ["Qp"])
                    proj_fm(Wq, "W", 256 + j * 128, 6)
                    cx.act(sqa[:], ps[6][:], AF.Sigmoid, ["ps6"], ["sqa"])
                    cx.tt("dve", gT[:, j, :], ps[6][:], sqa[:], ALU.mult, ["ps6", "sqa"], ["gT"])
            if kv:
                for j in range(2):
                    cx.dma(KT_io[j, :, t0:t0 + BLK], KTb[:, j, :], ["KTb"], [], "kto", is_out=True)
                vo = V_io.rearrange("(t p) n -> p t n", p=128)
                cx.dma(vo[:, b * 4:(b + 1) * 4, :], VAb[:], ["VAb"], [], "vao", is_out=True)
                continue
            for j in range(2):
                nkt = 4 * b + 4
                pairs = [(kt, m) for kt in range(nkt) for m in range(2)]
                NP = len(pairs)
                DEPTH = 3
                SCB = (0, 1, 6, 7)

                def emit_sc(i):
                    kt, m = pairs[i]
                    sbk = SCB[i % 4]
                    p_t = pt[i % 4]
                    pk = "pt%d" % (i % 4)
                    cx.mm(ps[sbk][:], KT[:, j, kt * 128:(kt + 1) * 128], Qp[:, j, m, :], ["KT", "Qp"], ["ps%d" % sbk])
                    cx.act(p_t[:], ps[sbk][:], AF.Exp, ["ps%d" % sbk], [pk], scale=0.125)
                    if kt >= 4 * b:
                        cx.tt("dve", p_t[:], p_t[:], maskb[:, kt - 4 * b, :], ALU.mult, [pk, "maskb"], [pk])

                def emit_pv(i):
                    kt, m = pairs[i]
                    p_t = pt[i % 4]
                    pk = "pt%d" % (i % 4)
                    cx.mm(ps[2 + m][:], VA[:, kt, j * 128:(j + 1) * 128], p_t[:], ["VA", pk], ["ps%d" % (2 + m)],
                          start=(kt == 0), stop=(kt == nkt - 1))
                    if kt == 0:
                        cx.cp("dve", rl[m][:], p_t[:], [pk], ["rl%d" % m])
                    else:
                        cx.tt("dve", rl[m][:], rl[m][:], p_t[:], ALU.add, ["rl%d" % m, pk], ["rl%d" % m])

                for i in range(NP + DEPTH):
                    if i < NP:
                        emit_sc(i)
                    if i >= DEPTH:
                        emit_pv(i - DEPTH)
                for m in range(2):
                    cx.mm(ps[4 + m][:], fe["ones"][:], rl[m][:], ["fe_ones", "rl%d" % m], ["ps%d" % (4 + m)])
                cx.recip(rl[0][:], ps[4][:], ["ps4"], ["rl0"])
                cx.recip(rl[1][:], ps[5][:], ["ps5"], ["rl1"])
                cx.tt("dve", A1[:], ps[2][:], rl[0][:], ALU.mult, ["ps2", "rl0"], ["A1"])
                cx.tt("dve", A2[:], ps[3][:], rl[1][:], ALU.mult, ["ps3", "rl1"], ["A2"])
                cx.stt("dve", A1[:], A2[:], nlam[:, 0:1], A1[:], ALU.mult, ALU.add, ["A2", "nlam", "A1"], ["A1"])
                cx.tt("pool", sqa[:], A1[:], A1[:], ALU.mult, ["A1"], ["sqa"])
                cx.mm(ps[6][:], fe["ones"][:], sqa[:], ["fe_ones", "sqa"], ["ps6"])
                cx.act(sqa[:], ps[6][:], AF.Sqrt, ["ps6"], ["sqa"], bias=1e-5, scale=1.0 / 128)
                cx.recip(sqa[:], sqa[:], ["sqa"], ["sqa"])
                cx.stt("dve", A1[:], A1[:], sw[:, 0:1], sqa[:], ALU.mult, ALU.mult, ["A1", "sw", "sqa"], ["A1"])
                cx.tt("dve", ob[:], A1[:], gT[:, j, :], ALU.mult, ["A1", "gT"], ["ob"])
                cx.dma(oT_out[j * 128:(j + 1) * 128, t0:t0 + BLK], ob[:], ["ob"], [], "oTo", is_out=True)
        cx.done()
    return nc


def _attn_consts():
    i = np.arange(128)
    q = np.arange(512)
    mask = np.zeros((128, 4, 512), np.float32)
    for a in range(4):
        mask[:, a, :] = (q[None, :] >= (i[:, None] + 128 * a))
    invf = np.zeros((128, 1), np.float32)
    base = (500000.0 ** (-np.arange(0, 16, 2, dtype=np.float32) / np.float32(16))).astype(np.float32)
    for m in range(2):
        for d in range(16):
            invf[m * 64 + d, 0] = base[d % 8]
    return mask.astype(ml_dtypes.bfloat16), invf


def attn_in_map(inp, mode, g, xT_b, pos_b, KT_c=None, V_c=None):
    cs = slice(g * 256, (g + 1) * 256)
    cs2 = slice(1024 + g * 256, 1024 + (g + 1) * 256)
    mask, invf = _attn_consts()
    m = {"xT": xT_b, "pos": np.ascontiguousarray(pos_b.reshape(1, -1)), "invf": invf}
    if mode == "kv":
        w = inp["w_kv"]
        m["gn"] = fm8(inp["kv_norm"])
    else:
        j = mode - 2
        w = inp["b_w_in"][j]
        m["gn"] = fm8(inp["b_norm"][j])
        m["lam_in"] = np.stack([inp["b_lq1"][j], inp["b_lk1"][j], inp["b_lq2"][j], inp["b_lk2"][j]]).astype(np.float32)
        m["subw"] = np.ascontiguousarray(inp["b_subln"][j].reshape(128, 1))
        m["mask"] = mask
        m["KT_in"] = KT_c
        m["V_in"] = V_c
    m["wq"] = np.ascontiguousarray(np.concatenate([w[:, cs], w[:, cs2]], axis=1))
    return m


def _run(nc, maps):
    return run_bass_kernel_spmd(nc, maps, core_ids=list(range(8))).results


def _xupd(xT, oT, wout, SL, final, gfin=None):
    NT = SL // 4
    nc = build_xupd(NT, final)
    maps = []
    for c in range(8):
        b, q = c // 4, c % 4
        m = {"xT": np.ascontiguousarray(xT[b][:, q * NT:(q + 1) * NT]),
             "oT": np.ascontiguousarray(oT[b][:, q * NT:(q + 1) * NT]),
             "wout": np.ascontiguousarray(wout)}
        if final:
            m["gfin"] = gfin
        maps.append(m)
    r = _run(nc, maps)
    return [np.concatenate([r[b * 4 + q]["xT_out"] for q in range(4)], axis=1) for b in range(2)]


def _forward(inp, SL):
    inp = {k: np.asarray(v) for k, v in inp.items()}
    xT = [np.ascontiguousarray(inp["x"][b].T) for b in range(2)]
    gather = lambda r: [np.concatenate([r[b * 4 + g]["oT_out"] for g in range(4)], axis=0) for b in range(2)]
    r = _run(build_rwkv(SL, 0), [rwkv_in_map(inp, 0, c % 4, xT[c // 4]) for c in range(8)])
    oT = gather(r)
    vf = [r[c]["vfirst_out"] for c in range(8)]
    r = _run(build_rwkv(SL, 1), [rwkv_in_map(inp, 1, c % 4, xT[c // 4], oT[c // 4], vf[c]) for c in range(8)])
    xT = [r[b * 4]["xT_out"] for b in range(2)]
    oT = gather(r)
    xT = _xupd(xT, oT, inp["a_w_out"][1], SL, False)
    r = _run(build_attn(SL, "kv"), [attn_in_map(inp, "kv", c % 4, xT[c // 4], inp["positions"][c // 4]) for c in range(8)])
    KT = [r[c]["KT_out"] for c in range(8)]
    V = [r[c]["V_out"] for c in range(8)]
    for j in range(2):
        r = _run(build_attn(SL, 2 + j),
                 [attn_in_map(inp, 2 + j, c % 4, xT[c // 4], inp["positions"][c // 4], KT[c], V[c]) for c in range(8)])
        oT = gather(r)
        if j == 0:
            xT = _xupd(xT, oT, inp["b_w_out"][0], SL, False)
        else:
            xT = _xupd(xT, oT, inp["b_w_out"][1], SL, True, fm8(inp["final_norm"]))
    return np.stack([np.ascontiguousarray(xT[b].T) for b in range(2)]).astype(np.float32)


def kernel(**inputs):
    return _forward(inputs, 16384)
```

```python
import numpy as np
import ml_dtypes
from contextlib import ExitStack
import concourse.bass as bass
import concourse.mybir as mybir
from concourse.bass_utils import run_bass_kernel_spmd

F32 = mybir.dt.float32
BF16 = mybir.dt.bfloat16
I32 = mybir.dt.int32
AF = mybir.ActivationFunctionType
ALU = mybir.AluOpType
AX = mybir.AxisListType

D = 1024
NCH = 8
ENGS = ["pe", "act", "dve", "pool", "sp"]


class Sched:
    def __init__(self, nc, es):
        self.nc = nc
        self.es = es
        self.streams = {e: [] for e in ENGS}
        self.sem = {e: es.enter_context(nc.semaphore("c_" + e)) for e in ENGS}
        self.count = {e: 0 for e in ENGS}
        self.waited = {e: {} for e in ENGS}
        self.last_w = {}
        self.readers = {}
        self.dsem = {}
        self.out_tokens = []
        self.sub = {}
        self.parity = None
        self.dbkeys = set()

    def dma_sem(self, name):
        if name not in self.dsem:
            self.dsem[name] = [self.es.enter_context(self.nc.semaphore("d_" + name)), 0]
        return name

    def op(self, eng, fn, reads=(), writes=(), dma=None, is_out=False):
        import os as _os
        self.nops = getattr(self, "nops", 0) + 1
        if self.nops > int(_os.environ.get("OPLIMIT", "100000000")):
            return None
        if self.parity is not None:
            reads = [k + "_%d" % self.parity if k in self.dbkeys else k for k in reads]
            writes = [k + "_%d" % self.parity if k in self.dbkeys else k for k in writes]
        nk = lambda k: k[:3] if (k.startswith("ps") and len(k) > 3 and k[2].isdigit()) else k
        reads = [nk(k) for k in reads]
        writes = [nk(k) for k in writes]
        writes = writes + [k for k in reads if k.startswith("ps") and k[2].isdigit()]
        reads = [k for k in reads if not (k.startswith("ps") and k[2].isdigit())]
        deps = []

        def bank_of(k):
            return k[:3] if (k.startswith("ps") and len(k) > 3 and k[2].isdigit()) else None

        def is_bank(k):
            return k.startswith("ps") and len(k) == 3 and k[2].isdigit()

        def rdep(k):
            if k in self.last_w:
                deps.append(self.last_w[k])

        def wdep(k):
            if k in self.last_w:
                deps.append(self.last_w[k])
            deps.extend(self.readers.get(k, {}).values())

        for k in reads:
            rdep(k)
            bk = bank_of(k)
            if bk:
                self.sub.setdefault(bk, set()).add(k)
                rdep(bk)
            if is_bank(k):
                for s_ in self.sub.get(k, ()):
                    rdep(s_)
        for k in writes:
            wdep(k)
            bk = bank_of(k)
            if bk:
                self.sub.setdefault(bk, set()).add(k)
                wdep(bk)
            if is_bank(k):
                for s_ in self.sub.get(k, ()):
                    wdep(s_)
        waits = []
        wd = self.waited[eng]
        for (sid, sem, val) in deps:
            if sid == "pe" and eng == "pe":
                continue
            if wd.get(sid, 0) < val:
                wd[sid] = val
                waits.append((sem, val))
        if dma is None:
            self.count[eng] += 1
            tok = (eng, self.sem[eng], self.count[eng])
            inc = (self.sem[eng], 1)
        else:
            self.dma_sem(dma)
            d = self.dsem[dma]
            d[1] += 16
            tok = ("d_" + dma, d[0], d[1])
            inc = (d[0], 16)
        self.streams[eng].append((waits, fn, inc))
        for k in reads:
            self.readers.setdefault(k, {})[tok[0]] = tok
        for k in writes:
            self.last_w[k] = tok
            self.readers[k] = {}
        if is_out:
            self.out_tokens.append(tok)
        return tok

    def finish(self):
        final = {}
        for (sid, sem, val) in self.out_tokens:
            if final.get(sid, (None, 0))[1] < val:
                final[sid] = (sem, val)
        self.streams["sp"].append((list(final.values()), None, None))

    def emit(self, block):
        def run(eng_name):
            def body(eng):
                for waits, fn, inc in self.streams[eng_name]:
                    for (s, v) in waits:
                        eng.wait_ge(s, v)
                    if fn is not None:
                        fn(eng).then_inc(inc[0], inc[1])
            return body
        block.tensor(run("pe"))
        block.scalar(run("act"))
        block.vector(run("dve"))
        block.gpsimd(run("pool"))
        block.sync(run("sp"))


class Ctx:
    def __init__(self, nc, es):
        self.nc = nc
        self.es = es
        self.S = Sched(nc, es)
        self.psall = es.enter_context(nc.psum_tensor("psall", [128, 8, 512], F32))
        self.ps = [self.psall[:, i, :] for i in range(8)]

    def sb(self, name, shape, dt):
        return self.es.enter_context(self.nc.sbuf_tensor(name, list(shape), dt))

    def dram(self, name, shape, dt, kind):
        return self.nc.dram_tensor(name, list(shape), dt, kind=kind).ap()

    def dma(self, out, in_, r, w, sem, eng="sp", is_out=False):
        return self.S.op(eng, lambda e: e.dma_start(out=out, in_=in_), reads=r, writes=w, dma=sem, is_out=is_out)

    def mm(self, out, lhsT, rhs, r, w, start=True, stop=True):
        return self.S.op("pe", lambda e: e.matmul(out, lhsT=lhsT, rhs=rhs, start=start, stop=stop), reads=r, writes=w)

    def tr(self, out, in_, ident, r, w):
        return self.S.op("pe", lambda e: e.transpose(out, in_, ident), reads=r, writes=w)

    def act(self, out, in_, func, r, w, bias=None, scale=None, accum_out=None):
        kw = {}
        if bias is not None:
            kw["bias"] = bias
        if scale is not None:
            kw["scale"] = scale
        if accum_out is not None:
            kw["accum_out"] = accum_out
        return self.S.op("act", lambda e: e.activation(out=out, in_=in_, func=func, **kw), reads=r, writes=w)

    def tt(self, eng, out, in0, in1, op, r, w):
        return self.S.op(eng, lambda e: e.tensor_tensor(out=out, in0=in0, in1=in1, op=op), reads=r, writes=w)

    def ts(self, eng, out, in0, s1, s2, op0, op1, r, w):
        if op1 is None:
            return self.S.op(eng, lambda e: e.tensor_scalar(out=out, in0=in0, scalar1=s1, scalar2=None, op0=op0), reads=r, writes=w)
        return self.S.op(eng, lambda e: e.tensor_scalar(out=out, in0=in0, scalar1=s1, scalar2=s2, op0=op0, op1=op1), reads=r, writes=w)

    def stt(self, eng, out, in0, scalar, in1, op0, op1, r, w):
        return self.S.op(eng, lambda e: e.scalar_tensor_tensor(out=out, in0=in0, scalar=scalar, in1=in1, op0=op0, op1=op1), reads=r, writes=w)

    def cp(self, eng, out, in_, r, w):
        if eng == "act":
            return self.S.op("act", lambda e: e.copy(out=out, in_=in_), reads=r, writes=w)
        return self.S.op(eng, lambda e: e.tensor_copy(out=out, in_=in_), reads=r, writes=w)

    def red(self, eng, out, in_, r, w):
        return self.S.op(eng, lambda e: e.tensor_reduce(out=out, in_=in_, axis=AX.X, op=ALU.add), reads=r, writes=w)

    def recip(self, out, in_, r, w):
        return self.S.op("dve", lambda e: e.reciprocal(out=out, in_=in_), reads=r, writes=w)

    def memset(self, eng, ap, val, w):
        return self.S.op(eng, lambda e: e.memset(ap, val), writes=w)

    def done(self):
        self.S.finish()
        with self.nc.Block() as block:
            self.S.emit(block)


def frontend_alloc(cx, has_prev, BLK=512):
    fe = {}
    fe["xT"] = cx.sb("fe_xT", [128, NCH, BLK], F32)
    fe["sq"] = cx.sb("fe_sq", [128, 2, BLK], F32)
    fe["xh"] = cx.sb("fe_xh", [128, NCH, BLK + 1], BF16)
    fe["rstd"] = cx.sb("fe_rstd", [128, BLK], F32)
    fe["ones"] = cx.sb("fe_ones", [128, 128], F32)
    cx.memset("pool", fe["ones"][:], 1.0, ["fe_ones"])
    cx.memset("pool", fe["xh"][:], 0.0, ["fe_xh"])
    if has_prev:
        fe["oT"] = cx.sb("fe_oT", [128, NCH, BLK], BF16)
        fe["wo"] = cx.sb("fe_wo", [128, NCH, D], BF16)
    return fe


def frontend_load_wout(cx, fe, wout_ap, stage):
    for half in range(2):
        cx.dma(stage[:, 0:4, :], wout_ap.rearrange("(c p) d -> p c d", p=128)[:, half * 4:(half + 1) * 4, :],
               [], ["stage"], "stage")
        cx.cp("dve", fe["wo"][:, half * 4:(half + 1) * 4, :], stage[:, 0:4, :], ["stage"], ["fe_wo"])


def frontend_block(cx, fe, xT_ap, oT_ap, xT_out_ap, b, has_prev, BLK=512, store_x=True, norm=True):
    t0 = b * BLK
    xv = xT_ap.rearrange("(c p) s -> p c s", p=128)
    cx.dma(fe["xT"][:], xv[:, :, t0:t0 + BLK], [], ["fe_xT"], "fe_x")
    if has_prev:
        ov = oT_ap.rearrange("(c p) s -> p c s", p=128)
        cx.dma(fe["oT"][:], ov[:, :, t0:t0 + BLK], [], ["fe_oT"], "fe_o")
        for dc in range(NCH):
            bank = dc % 4
            for kc in range(NCH):
                cx.mm(cx.ps[bank][:], fe["wo"][:, kc, dc * 128:(dc + 1) * 128], fe["oT"][:, kc, :],
                      ["fe_wo", "fe_oT"], ["ps%d" % bank], start=(kc == 0), stop=(kc == NCH - 1))
            cx.tt("dve", fe["xT"][:, dc, :], fe["xT"][:, dc, :], cx.ps[bank][:], ALU.add,
                  ["fe_xT", "ps%d" % bank], ["fe_xT"])
        if store_x:
            ovx = xT_out_ap.rearrange("(c p) s -> p c s", p=128)
            cx.dma(ovx[:, :, t0:t0 + BLK], fe["xT"][:], ["fe_xT"], [], "fe_xs", is_out=True)
    for dc in range(NCH):
        cx.act(fe["sq"][:, dc % 2, :], fe["xT"][:, dc, :], AF.Square, ["fe_xT"], ["fe_sq%d" % (dc % 2)])
        cx.mm(cx.ps[4][:], fe["ones"][:], fe["sq"][:, dc % 2, :], ["fe_ones", "fe_sq%d" % (dc % 2)], ["ps4"],
              start=(dc == 0), stop=(dc == NCH - 1))
    cx.act(fe["rstd"][:], cx.ps[4][:], AF.Sqrt, ["ps4"], ["fe_rstd"], bias=1e-6, scale=1.0 / D)
    cx.recip(fe["rstd"][:], fe["rstd"][:], ["fe_rstd"], ["fe_rstd"])
    if b > 0:
        cx.cp("pool", fe["xh"][:, :, 0:1], fe["xh"][:, :, BLK:BLK + 1], ["fe_xh"], ["fe_xh"])
    for dc in range(NCH):
        eng = "dve" if dc % 2 == 0 else "pool"
        cx.tt(eng, fe["xh"][:, dc, 1:BLK + 1], fe["xT"][:, dc, :], fe["rstd"][:], ALU.mult,
              ["fe_xT", "fe_rstd"], ["fe_xh"])


C_DEC = 0.6065306597126334
HC = 256


def build_rwkv(SL, layer, STOP=99):
    has_prev = layer > 0
    BLK = 512
    NB = SL // BLK
    nc = bass.Bass("TRN2", target_bir_lowering=False)
    with ExitStack() as es:
        cx = Ctx(nc, es)
        ps = cx.ps
        xT = cx.dram("xT", [D, SL], F32, "ExternalInput")
        win = cx.dram("win", [4, D, HC], F32, "ExternalInput")
        gmu = cx.dram("gmu", [128, 7, NCH], F32, "ExternalInput")
        l1w = cx.dram("l1w", [D, 160], F32, "ExternalInput")
        l2w = cx.dram("l2w", [64, 3, HC], F32, "ExternalInput")
        vecs = cx.dram("vecs", [8, HC], F32, "ExternalInput")
        cst = cx.dram("cst", [128, 1152], F32, "ExternalInput")
        oT_out = cx.dram("oT_out", [HC, SL], BF16, "ExternalOutput")
        if has_prev:
            oT = cx.dram("oT", [D, SL], BF16, "ExternalInput")
            wout = cx.dram("wout", [D, D], F32, "ExternalInput")
            vfirst = cx.dram("vfirst", [SL, HC], F32, "ExternalInput")
            xT_out = cx.dram("xT_out", [D, SL], F32, "ExternalOutput")
        else:
            oT = wout = xT_out = None
            vfirst_out = cx.dram("vfirst_out", [SL, HC], F32, "ExternalOutput")
        fe = frontend_alloc(cx, has_prev, BLK)
        stage = cx.sb("stage", [128, NCH, HC * 2], F32)
        csb = cx.sb("csb", [128, 1152], F32)
        ident_bf = cx.sb("ident_bf", [128, 128], BF16)
        gm = cx.sb("gm", [128, 7, NCH], F32)
        coef = cx.sb("coef", [128, 12, NCH], F32)
        Wc = cx.sb("Wc", [128, NCH, 4 * HC], BF16)
        Wp = cx.sb("Wp", [128, NCH, 4 * HC], BF16)
        L1c = cx.sb("L1c", [128, NCH, 160], BF16)
        L1p = cx.sb("L1p", [128, NCH, 160], BF16)
        L2 = cx.sb("L2", [64, 3, HC], BF16)
        L2f = cx.sb("L2f", [64, 3, HC], F32)
        vb = cx.sb("vb", [128, 8, HC], F32)
        h1 = cx.sb("h1", [64, 3, BLK], BF16)
        ST = cx.sb("ST", [64, 4, 64], F32)
        STb = cx.sb("STb", [64, 4, 64], BF16)
        gl = cx.sb("gl", [64, 4], F32)

        def T(name, dt=F32, w=HC):
            return cx.sb(name, [128, w], dt)
        r_sb, k_sb, v_sb, gate, sg, a_sb = T("r_sb"), T("k_sb"), T("v_sb"), T("gate"), T("sg"), T("a_sb")
        t1, t2, t3, kkn, kmod, bvec = T("t1"), T("t2"), T("t3"), T("kkn"), T("kmod"), T("bvec")
        epos, eneg, eprev, ehat = T("epos"), T("eneg"), T("eprev"), T("ehat")
        ss4, bs4, s14, s24 = T("ss4", w=4), T("bs4", w=4), T("s14", w=4), T("s24", w=4)
        tmT = cx.sb("tmT", [128, 4, HC], BF16)
        khat, bhat, vbf = T("khat", BF16), T("bhat", BF16), T("vbf", BF16)
        fm = cx.sb("fm", [64, 4, 4, 128], BF16)
        Am = cx.sb("Am", [128, 4, 512], BF16)
        Nf = cx.sb("Nf", [128, 4, 2, 128], F32)
        NTf = cx.sb("NTf", [128, 4, 2, 128], F32)
        Tf = cx.sb("Tf", [128, 4, 2, 128], F32)
        Wf = cx.sb("Wf", [128, 4, 64], F32)
        Zb = cx.sb("Zb", [128, 4, 64], BF16)
        y_sb, yc, o_bf = T("y_sb"), T("yc"), T("o_bf", BF16)
        oT_sb = cx.sb("oT_sb", [128, 2, 128], BF16)
        vf_sb = T("vf_sb")

        cx.dma(csb[:], cst[:, :], [], ["csb"], "su0")
        cx.dma(gm[:], gmu[:, :, :], [], ["gm"], "su1")
        cx.dma(L2f[:], l2w[:, :, :], [], ["L2f"], "su2")
        for i in range(8):
            cx.dma(vb[:, i, :], vecs[i:i + 1, :].partition_broadcast(128), [], ["vb"], "su3")
        ident_f = csb[:, 0:128]
        tri = csb[:, 128:256]
        sfx = csb[:, 256:384]
        mask4 = csb[:, 384:896]
        maskNT = csb[:, 896:1024]
        ones_col = csb[:, 1024:1025]
        cx.cp("dve", ident_bf[:], ident_f, ["csb"], ["ident_bf"])
        cx.cp("dve", L2[:], L2f[:], ["L2f"], ["L2"])
        for p in range(6):
            cx.tt("dve", coef[:, 6 + p, :], gm[:, 0, :], gm[:, 1 + p, :], ALU.mult, ["gm"], ["coef"])
            cx.tt("dve", coef[:, p, :], gm[:, 0, :], coef[:, 6 + p, :], ALU.subtract, ["gm", "coef"], ["coef"])
        for p in range(4):
            cx.dma(stage[:, :, 0:HC], win[p].rearrange("(c p) n -> p c n", p=128), [], ["stage"], "stage")
            cx.tt("dve", Wc[:, :, p * HC:(p + 1) * HC], stage[:, :, 0:HC],
                  coef[:, p, :].unsqueeze(2).to_broadcast([128, NCH, HC]), ALU.mult, ["stage", "coef"], ["Wc"])
            cx.tt("pool", Wp[:, :, p * HC:(p + 1) * HC], stage[:, :, 0:HC],
                  coef[:, 6 + p, :].unsqueeze(2).to_broadcast([128, NCH, HC]), ALU.mult, ["stage", "coef"], ["Wp"])
        cx.dma(stage[:, :, 0:160], l1w.rearrange("(c p) n -> p c n", p=128), [], ["stage"], "stage")
        for (lo, hi, p) in ((0, 64, 4), (64, 128, 5), (128, 160, 2)):
            cx.tt("dve", L1c[:, :, lo:hi], stage[:, :, lo:hi],
                  coef[:, p, :].unsqueeze(2).to_broadcast([128, NCH, hi - lo]), ALU.mult, ["stage", "coef"], ["L1c"])
            cx.tt("pool", L1p[:, :, lo:hi], stage[:, :, lo:hi],
                  coef[:, 6 + p, :].unsqueeze(2).to_broadcast([128, NCH, hi - lo]), ALU.mult, ["stage", "coef"], ["L1p"])
        if has_prev:
            wv = wout.rearrange("(c p) d -> p c d", p=128)
            for kc in range(NCH):
                cx.dma(stage[:, 0:2, :], wv[:, kc, :].rearrange("p (a n) -> p a n", a=2), [], ["stage"], "stage")
                cx.cp("dve", fe["wo"][:, kc, :].rearrange("p (a n) -> p a n", a=2), stage[:, 0:2, :], ["stage"], ["fe_wo"])
        cx.memset("dve", ST[:], 0.0, ["ST"])
        cx.memset("dve", STb[:], 0.0, ["STb"])

        w0_b, a0_b, v0_b, kk_b, ka_b, rk_b, lnw_b, lnb_b = [vb[:, i, :] for i in range(8)]

        def bc4(ap4):
            return ap4.unsqueeze(2).to_broadcast([128, 4, 64])

        def v3(ap):
            return ap.rearrange("p (h n) -> p h n", h=4)

        e1, e2 = T("e1"), T("e2")
        DB = [(v_sb, bs4, gate, y_sb), (T("v_sb_b"), T("bs4_b", w=4), T("gate_b"), T("y_sb_b"))]
        cx.S.dbkeys = set(["v_sb", "bs4", "gate", "y_sb"])

        def tile_pre(b, ti, pp):
            v_sb, bs4, gate, y_sb = DB[pp]
            tok0 = b * BLK + ti * 128
            c0 = 1 + ti * 128
            tok0 = b * BLK + ti * 128
            c0 = 1 + ti * 128
            for (bank, lo) in ((0, 0), (1, 512)):
                for kc in range(NCH):
                    cx.mm(ps[bank][:], xh[:, kc, c0:c0 + 128], Wc[:, kc, lo:lo + 512], ["fe_xh", "Wc"], ["ps%d" % bank],
                          start=(kc == 0), stop=False)
                    cx.mm(ps[bank][:], xh[:, kc, c0 - 1:c0 + 127], Wp[:, kc, lo:lo + 512], ["fe_xh", "Wp"], ["ps%d" % bank],
                          start=False, stop=(kc == NCH - 1))
            hs = slice(ti * 128, (ti + 1) * 128)
            cx.mm(ps[2][:, 0:HC], h1[0:64, 0, hs], L2[0:64, 0, :], ["h1", "L2"], ["ps2"])
            cx.mm(ps[2][:, HC:2 * HC], h1[0:64, 1, hs], L2[0:64, 1, :], ["h1", "L2"], ["ps2"])
            if has_prev:
                cx.mm(ps[3][:, 0:HC], h1[0:32, 2, hs], L2[0:32, 2, :], ["h1", "L2"], ["ps3"])
            cx.cp("act", r_sb[:], ps[0][:, 0:HC], ["ps0"], ["r_sb"])
            cx.cp("dve", k_sb[:], ps[0][:, HC:2 * HC], ["ps0"], ["k_sb"])
            cx.cp("act", v_sb[:], ps[1][:, 0:HC], ["ps1"], ["v_sb"])
            cx.act(t1[:], ps[1][:, HC:2 * HC], AF.Sigmoid, ["ps1"], ["t1"])
            cx.tt("dve", gate[:], ps[1][:, HC:2 * HC], t1[:], ALU.mult, ["ps1", "t1"], ["gate"])
            cx.tt("dve", t2[:], ps[2][:, 0:HC], w0_b, ALU.add, ["ps2", "vb"], ["t2"])
            cx.act(sg[:], t2[:], AF.Sigmoid, ["t2"], ["sg"])
            cx.tt("dve", t3[:], ps[2][:, HC:2 * HC], a0_b, ALU.add, ["ps2", "vb"], ["t3"])
            cx.act(a_sb[:], t3[:], AF.Sigmoid, ["t3"], ["a_sb"])
            if has_prev:
                cx.dma(vf_sb[:], vfirst[tok0:tok0 + 128, :], [], ["vf_sb"], "vf")
                cx.tt("dve", t2[:], ps[3][:, 0:HC], v0_b, ALU.add, ["ps3", "vb"], ["t2"])
                cx.act(t2[:], t2[:], AF.Sigmoid, ["t2"], ["t2"])
                cx.tt("pool", t3[:], vf_sb[:], v_sb[:], ALU.subtract, ["vf_sb", "v_sb"], ["t3"])
                cx.tt("pool", t3[:], t3[:], t2[:], ALU.mult, ["t3", "t2"], ["t3"])
                cx.tt("pool", v_sb[:], v_sb[:], t3[:], ALU.add, ["v_sb", "t3"], ["v_sb"])
            else:
                cx.dma(vfirst_out[tok0:tok0 + 128, :], v_sb[:], ["v_sb"], [], "vfo", is_out=True)
            cx.mm(ps[4][:, 0:HC], tri, sg[:], ["csb", "sg"], ["ps4"])
            cx.mm(ps[4][:, HC:2 * HC], sfx, sg[:], ["csb", "sg"], ["ps4"])
            for h in range(4):
                cx.mm(ps[3][0:64, HC + h:HC + h + 1], sg[:, h * 64:(h + 1) * 64], ones_col, ["sg", "csb"], ["ps3b"])
            cx.act(gl[:], ps[3][0:64, HC:HC + 4], AF.Exp, ["ps3b"], ["gl"], scale=-C_DEC)
            cx.act(epos[:], ps[4][:, 0:HC], AF.Exp, ["ps4"], ["epos"], scale=-C_DEC)
            cx.act(eneg[:], ps[4][:, 0:HC], AF.Exp, ["ps4"], ["eneg"], scale=C_DEC)
            cx.tt("dve", t2[:], ps[4][:, 0:HC], sg[:], ALU.subtract, ["ps4", "sg"], ["t2"])
            cx.act(eprev[:], t2[:], AF.Exp, ["t2"], ["eprev"], scale=-C_DEC)
            cx.act(ehat[:], ps[4][:, HC:2 * HC], AF.Exp, ["ps4"], ["ehat"], scale=-C_DEC)
            cx.tt("pool", kkn[:], k_sb[:], kk_b, ALU.mult, ["k_sb", "vb"], ["kkn"])
            cx.tt("pool", t1[:], kkn[:], kkn[:], ALU.mult, ["kkn"], ["t1"])
            cx.red("dve", ss4[:], v3(t1[:]), ["t1"], ["ss4"])
            cx.act(ss4[:], ss4[:], AF.Sqrt, ["ss4"], ["ss4"])
            cx.ts("dve", ss4[:], ss4[:], 1e-12, None, ALU.max, None, ["ss4"], ["ss4"])
            cx.recip(ss4[:], ss4[:], ["ss4"], ["ss4"])
            cx.tt("dve", v3(kkn[:]), v3(kkn[:]), bc4(ss4[:]), ALU.mult, ["kkn", "ss4"], ["kkn"])
            cx.stt("dve", t3[:], a_sb[:], -1.0, ka_b, ALU.add, ALU.mult, ["a_sb", "vb"], ["t3"])
            cx.stt("dve", kmod[:], t3[:], 1.0, k_sb[:], ALU.add, ALU.mult, ["t3", "k_sb"], ["kmod"])
            cx.tt("pool", bvec[:], kkn[:], a_sb[:], ALU.mult, ["kkn", "a_sb"], ["bvec"])
            cx.tt("pool", t1[:], r_sb[:], kmod[:], ALU.mult, ["r_sb", "kmod"], ["t1"])
            cx.tt("pool", t1[:], t1[:], rk_b, ALU.mult, ["t1", "vb"], ["t1"])
            cx.red("dve", bs4[:], v3(t1[:]), ["t1"], ["bs4"])
            cx.stt("dve", tmT[:, 0, :], kkn[:], -1.0, eprev[:], ALU.mult, ALU.mult, ["kkn", "eprev"], ["tmT0"])
            cx.tt("pool", tmT[:, 1, :], r_sb[:], epos[:], ALU.mult, ["r_sb", "epos"], ["tmT1"])
            cx.tt("dve", tmT[:, 2, :], bvec[:], eneg[:], ALU.mult, ["bvec", "eneg"], ["tmT2"])
            cx.tt("pool", tmT[:, 3, :], kmod[:], eneg[:], ALU.mult, ["kmod", "eneg"], ["tmT3"])
            cx.tt("dve", khat[:], kmod[:], ehat[:], ALU.mult, ["kmod", "ehat"], ["khat"])
            cx.tt("pool", bhat[:], bvec[:], ehat[:], ALU.mult, ["bvec", "ehat"], ["bhat"])
            cx.cp("act", vbf[:], v_sb[:], ["v_sb"], ["vbf"])
            for q in range(4):
                bank = 5 + (q // 2)
                pv = ps[bank][:].bitcast(BF16)
                for h in range(4):
                    col = ((q % 2) * 4 + h) * 128
                    cx.tr(pv[0:64, col:col + 128], tmT[:, q, h * 64:(h + 1) * 64], ident_bf[:],
                          ["tmT%d" % q, "ident_bf"], ["ps%d" % bank])
            for half in range(2):
                pv = ps[5 + half][:].bitcast(BF16)
                src = pv[0:64, :].rearrange("k (q h t) -> k h q t", q=2, h=4)
                cx.cp("act" if half == 0 else "dve", fm[:, :, 2 * half:2 * half + 2, :], src, ["ps%d" % (5 + half)], ["fm"])
            for h in range(4):
                bank = h % 2
                ar = fm[:, h, 0:2, :].rearrange("k q t -> k (q t)")
                cx.mm(ps[bank][:, 0:256], fm[:, h, 2, :], ar, ["fm"], ["ps%d" % bank])
                cx.mm(ps[bank][:, 256:512], fm[:, h, 3, :], ar, ["fm"], ["ps%d" % bank])
                cx.mm(ps[2 + bank][:, 0:128], fm[:, h, 0, :], fm[:, h, 2, :], ["fm"], ["ps%da" % (2 + bank)])
                cx.tt("dve", Am[:, h, :], ps[bank][:], mask4, ALU.mult, ["ps%d" % bank, "csb"], ["Am%d" % h])
                cx.tt("dve", Nf[:, h, 0, :], ps[bank][:, 0:128], mask4[:, 0:128], ALU.mult, ["ps%d" % bank, "csb"], ["Nf%d" % h])
                cx.tt("dve", NTf[:, h, 0, :], ps[2 + bank][:, 0:128], maskNT, ALU.mult, ["ps%da" % (2 + bank), "csb"], ["NTf%d" % h])
                cx.tt("pool", Tf[:, h, 0, :], Nf[:, h, 0, :], ident_f, ALU.add, ["Nf%d" % h, "csb"], ["Tf%d" % h])

        def inverse(pp):
            for kstep in range(1, 7):
                src, dst = (kstep - 1) % 2, kstep % 2
                for h in range(4):
                    bk = 4 + h
                    cx.mm(ps[bk][:, 0:128], Nf[:, h, src, :], NTf[:, h, src, :], ["Nf%d" % h, "NTf%d" % h], ["ps%d" % bk])
                    if kstep < 6:
                        cx.mm(ps[bk][:, 128:256], NTf[:, h, src, :], Nf[:, h, src, :], ["Nf%d" % h, "NTf%d" % h], ["ps%d" % bk])
                yield
                for h in range(4):
                    bk = 4 + h
                    cx.cp("act", NTf[:, h, dst, :], ps[bk][:, 0:128], ["ps%d" % bk], ["NTf%d" % h])
                    if kstep < 6:
                        cx.cp("dve", Nf[:, h, dst, :], ps[bk][:, 128:256], ["ps%d" % bk], ["Nf%d" % h])
                yield
                for h in range(4):
                    bk = 4 + h
                    cx.mm(ps[bk][:, 256:384], NTf[:, h, dst, :], Tf[:, h, src, :], ["NTf%d" % h, "Tf%d" % h], ["ps%d" % bk])
                yield
                for h in range(4):
                    bk = 4 + h
                    cx.tt("dve", Tf[:, h, dst, :], ps[bk][:, 256:384], Tf[:, h, src, :], ALU.add, ["ps%d" % bk, "Tf%d" % h], ["Tf%d" % h])

        def tile_state(b, ti, pp):
            v_sb, bs4, gate, y_sb = DB[pp]
            TI = 6 % 2
            HS = [slice(h * 64, (h + 1) * 64) for h in range(4)]
            for h in range(4):
                cx.mm(ps[0][:, HS[h]], fm[:, h, 0, :], STb[:, h, :], ["fm", "STb"], ["ps0"], start=True, stop=False)
                cx.mm(ps[0][:, HS[h]], Am[:, h, 256:384], vbf[:, HS[h]], ["Am%d" % h, "vbf"], ["ps0"], start=False, stop=True)
            cx.cp("act", Wf[:].rearrange("p h v -> p (h v)"), ps[0][:, 0:HC], ["ps0"], ["Wf"])
            for h in range(4):
                cx.mm(ps[1][:, HS[h]], Tf[:, h, TI, :], Wf[:, h, :], ["Tf%d" % h, "Wf"], ["ps1"])
            cx.cp("dve", Zb[:].rearrange("p h v -> p (h v)"), ps[1][:, 0:HC], ["ps1"], ["Zb"])
            for h in range(4):
                cx.mm(ps[2][:, HS[h]], fm[:, h, 1, :], STb[:, h, :], ["fm", "STb"], ["ps2"], start=True, stop=False)
                cx.mm(ps[2][:, HS[h]], Am[:, h, 128:256], Zb[:, h, :], ["Am%d" % h, "Zb"], ["ps2"], start=False, stop=False)
                cx.mm(ps[2][:, HS[h]], Am[:, h, 384:512], vbf[:, HS[h]], ["Am%d" % h, "vbf"], ["ps2"], start=False, stop=True)
            for h in range(4):
                cx.mm(ps[3][0:64, HS[h]], bhat[:, HS[h]], Zb[:, h, :], ["bhat", "Zb"], ["ps3"], start=True, stop=False)
                cx.mm(ps[3][0:64, HS[h]], khat[:, HS[h]], vbf[:, HS[h]], ["khat", "vbf"], ["ps3"], start=False, stop=True)
            cx.tt("dve", ST[:], ST[:], gl[:].unsqueeze(2).to_broadcast([64, 4, 64]), ALU.mult, ["ST", "gl"], ["ST"])
            cx.tt("dve", ST[:].rearrange("p h v -> p (h v)"), ST[:].rearrange("p h v -> p (h v)"), ps[3][0:64, 0:HC], ALU.add,
                  ["ST", "ps3"], ["ST"])
            cx.cp("act", STb[:], ST[:], ["ST"], ["STb"])
            cx.cp("act", y_sb[:], ps[2][:, 0:HC], ["ps2"], ["y_sb"])

        def epilogue(b, ti, pp):
            v_sb, bs4, gate, y_sb = DB[pp]
            tok0 = b * BLK + ti * 128
            pass
            cx.red("dve", s14[:], v3(y_sb[:]), ["y_sb"], ["s14"])
            cx.ts("dve", s14[:], s14[:], -1.0 / 64, None, ALU.mult, None, ["s14"], ["s14"])
            yield
            cx.tt("dve", v3(yc[:]), v3(y_sb[:]), bc4(s14[:]), ALU.add, ["y_sb", "s14"], ["yc"])
            cx.tt("pool", e1[:], yc[:], yc[:], ALU.mult, ["yc"], ["e1"])
            yield
            cx.red("dve", s24[:], v3(e1[:]), ["e1"], ["s24"])
            cx.act(s24[:], s24[:], AF.Sqrt, ["s24"], ["s24"], bias=64e-5, scale=1.0 / 64)
            yield
            cx.recip(s24[:], s24[:], ["s24"], ["s24"])
            cx.tt("dve", v3(yc[:]), v3(yc[:]), bc4(s24[:]), ALU.mult, ["yc", "s24"], ["yc"])
            yield
            cx.tt("pool", yc[:], yc[:], lnw_b, ALU.mult, ["yc", "vb"], ["yc"])
            cx.tt("pool", yc[:], yc[:], lnb_b, ALU.add, ["yc", "vb"], ["yc"])
            yield
            cx.tt("dve", v3(e2[:]), v3(v_sb[:]), bc4(bs4[:]), ALU.mult, ["v_sb", "bs4"], ["e2"])
            cx.tt("pool", yc[:], yc[:], e2[:], ALU.add, ["yc", "e2"], ["yc"])
            yield
            cx.tt("dve", o_bf[:], yc[:], gate[:], ALU.mult, ["yc", "gate"], ["o_bf"])
            pv = ps[1][:].bitcast(BF16)
            for j in range(2):
                cx.tr(pv[:, 512 + j * 128:512 + (j + 1) * 128], o_bf[:, j * 128:(j + 1) * 128], ident_bf[:],
                      ["o_bf", "ident_bf"], ["ps1o"])
            cx.cp("act", oT_sb[:], pv[:, 512:768].rearrange("p (j t) -> p j t", j=2), ["ps1o"], ["oT_sb"])
            yield
            cx.dma(oT_out.rearrange("(j p) s -> p j s", p=128)[:, :, tok0:tok0 + 128], oT_sb[:], ["oT_sb"], [], "oTo", is_out=True)

            yield

        def drive(gens):
            live = list(gens)
            while live:
                nxt = []
                for (g, p) in live:
                    cx.S.parity = p
                    try:
                        next(g)
                        nxt.append((g, p))
                    except StopIteration:
                        pass
                live = nxt
            cx.S.parity = None

        pending = None
        t = 0
        for b in range(NB):
            frontend_block(cx, fe, xT, oT, xT_out, b, has_prev, BLK)
            xh = fe["xh"]
            for j, (lo, hi) in enumerate(((0, 64), (64, 128), (128, 160))):
                m = hi - lo
                bank = 5 + j
                for kc in range(NCH):
                    cx.mm(ps[bank][0:m, :], L1c[:, kc, lo:hi], xh[:, kc, 1:BLK + 1], ["L1c", "fe_xh"], ["ps%d" % bank],
                          start=(kc == 0), stop=False)
                    cx.mm(ps[bank][0:m, :], L1p[:, kc, lo:hi], xh[:, kc, 0:BLK], ["L1p", "fe_xh"], ["ps%d" % bank],
                          start=False, stop=(kc == NCH - 1))
                if j == 0:
                    cx.act(h1[0:m, j, :], ps[bank][0:m, :], AF.Tanh, ["ps%d" % bank], ["h1"])
                else:
                    cx.cp("dve", h1[0:m, j, :], ps[bank][0:m, :], ["ps%d" % bank], ["h1"])
            for ti in range(BLK // 128):
                pp = t % 2
                cx.S.parity = pp
                tile_pre(b, ti, pp)
                gens = [(inverse(pp), pp)]
                if pending is not None:
                    gens.append((epilogue(*pending), pending[2]))
                drive(gens)
                cx.S.parity = pp
                tile_state(b, ti, pp)
                cx.S.parity = None
                pending = (b, ti, pp)
                t += 1
        drive([(epilogue(*pending), pending[2])])
        cx.done()
    return nc


def _consts():
    c = np.zeros((128, 1152), np.float32)
    i = np.arange(128)
    c[:, 0:128] = np.eye(128)
    c[:, 128:256] = (i[:, None] <= i[None, :])
    c[:, 256:384] = (i[:, None] > i[None, :])
    su = (i[:, None] < i[None, :]).astype(np.float32)
    iu = (i[:, None] <= i[None, :]).astype(np.float32)
    c[:, 384:512] = su
    c[:, 512:640] = iu
    c[:, 640:768] = su
    c[:, 768:896] = iu
    c[:, 896:1024] = (i[:, None] > i[None, :])
    c[:, 1024] = 1.0
    return c


def fm8(v):
    return np.ascontiguousarray(np.asarray(v).reshape(NCH, 128).T)


def rwkv_in_map(inp, layer, g, xT_b, oT_b=None, vfirst_c=None):
    cs = slice(g * HC, (g + 1) * HC)
    m = {}
    m["xT"] = xT_b
    m["win"] = np.ascontiguousarray(inp["a_w_in"][layer][:, :, cs])
    gmu = np.zeros((128, 7, NCH), np.float32)
    gmu[:, 0, :] = fm8(inp["a_norm"][layer])
    for p in range(6):
        gmu[:, 1 + p, :] = fm8(inp["a_mu"][layer][p])
    m["gmu"] = gmu
    l1 = np.zeros((D, 160), np.float32)
    l1[:, 0:64] = inp["a_w1"][layer]
    l1[:, 64:128] = inp["a_a1"][layer]
    l2 = np.zeros((64, 3, HC), np.float32)
    l2[:, 0, :] = inp["a_w2"][layer][:, cs]
    l2[:, 1, :] = inp["a_a2"][layer][:, cs]
    vecs = np.zeros((8, HC), np.float32)
    vecs[0] = inp["a_w0"][layer][cs]
    vecs[1] = inp["a_a0"][layer][cs]
    if layer > 0:
        l1[:, 128:160] = inp["a_v1"][layer - 1]
        l2[0:32, 2, :] = inp["a_v2"][layer - 1][:, cs]
        vecs[2] = inp["a_v0"][layer - 1][cs]
    vecs[3] = inp["a_k_k"][layer][cs]
    vecs[4] = inp["a_k_a"][layer][cs]
    vecs[5] = inp["a_r_k"][layer].reshape(-1)[cs]
    vecs[6] = inp["a_ln_w"][layer][cs]
    vecs[7] = inp["a_ln_b"][layer][cs]
    m["l1w"], m["l2w"], m["vecs"] = l1, l2, vecs
    m["cst"] = _consts()
    if layer > 0:
        m["oT"] = oT_b
        m["wout"] = np.ascontiguousarray(inp["a_w_out"][layer - 1])
        m["vfirst"] = vfirst_c
    return m


def build_xupd(NT, final):
    BLK = 512
    nc = bass.Bass("TRN2", target_bir_lowering=False)
    with ExitStack() as es:
        cx = Ctx(nc, es)
        xT = cx.dram("xT", [D, NT], F32, "ExternalInput")
        oT = cx.dram("oT", [D, NT], BF16, "ExternalInput")
        wout = cx.dram("wout", [D, D], F32, "ExternalInput")
        xT_out = cx.dram("xT_out", [D, NT], F32, "ExternalOutput")
        fe = frontend_alloc(cx, True, BLK)
        stage = cx.sb("stage", [128, 2, 512], F32)
        wv = wout.rearrange("(c p) d -> p c d", p=128)
        for kc in range(NCH):
            cx.dma(stage[:], wv[:, kc, :].rearrange("p (a n) -> p a n", a=2), [], ["stage"], "stage")
            cx.cp("dve", fe["wo"][:, kc, :].rearrange("p (a n) -> p a n", a=2), stage[:], ["stage"], ["fe_wo"])
        if final:
            gfin = cx.dram("gfin", [128, NCH], F32, "ExternalInput")
            g_sb = cx.sb("g_sb", [128, NCH], F32)
            cx.dma(g_sb[:], gfin[:, :], [], ["g_sb"], "su1")
            yT = cx.sb("yT", [128, NCH, BLK], F32)
        for b in range(NT // BLK):
            frontend_block(cx, fe, xT, oT, xT_out, b, True, BLK, store_x=not final, norm=final)
            if final:
                for dc in range(NCH):
                    cx.stt("dve", yT[:, dc, :], fe["xT"][:, dc, :], g_sb[:, dc:dc + 1], fe["rstd"][:], ALU.mult, ALU.mult,
                           ["fe_xT", "g_sb", "fe_rstd"], ["yT"])
                ovx = xT_out.rearrange("(c p) s -> p c s", p=128)
                cx.dma(ovx[:, :, b * BLK:(b + 1) * BLK], yT[:], ["yT"], [], "yo", is_out=True)
        cx.done()
    return nc


LAM_INIT = {2: 0.8 - 0.6 * float(np.exp(-0.3 * 2)), 3: 0.8 - 0.6 * float(np.exp(-0.3 * 3))}


def build_attn(SL, mode):
    BLK = 512
    NB = SL // BLK
    NKT = SL // 128
    kv = (mode == "kv")
    nc = bass.Bass("TRN2", target_bir_lowering=False)
    with ExitStack() as es:
        cx = Ctx(nc, es)
        ps = cx.ps
        xT = cx.dram("xT", [D, SL], F32, "ExternalInput")
        pos = cx.dram("pos", [1, SL], I32, "ExternalInput")
        wq = cx.dram("wq", [D, 512], F32, "ExternalInput")
        gn = cx.dram("gn", [128, NCH], F32, "ExternalInput")
        invf_d = cx.dram("invf", [128, 1], F32, "ExternalInput")
        if kv:
            KT_io = cx.dram("KT_out", [2, 128, SL], BF16, "ExternalOutput")
            V_io = cx.dram("V_out", [SL, 256], BF16, "ExternalOutput")
        else:
            lam_in = cx.dram("lam_in", [4, 64], F32, "ExternalInput")
            subw = cx.dram("subw", [128, 1], F32, "ExternalInput")
            mask_d = cx.dram("mask", [128, 4, 512], BF16, "ExternalInput")
            KT_io = cx.dram("KT_in", [2, 128, SL], BF16, "ExternalInput")
            V_io = cx.dram("V_in", [SL, 256], BF16, "ExternalInput")
            oT_out = cx.dram("oT_out", [256, SL], BF16, "ExternalOutput")
        fe = frontend_alloc(cx, False, BLK)
        Wq = cx.sb("Wq", [128, NCH, 512], BF16)
        Wqr = cx.sb("Wqr", [128, NCH, 256], BF16)
        gsb = cx.sb("gsb", [128, NCH], F32)
        invf = cx.sb("invf_sb", [128, 1], F32)
        posi = cx.sb("posi", [128, BLK], I32)
        ang = cx.sb("ang", [128, BLK], F32)
        Ct = cx.sb("Ct", [128, BLK], F32)
        St = cx.sb("St", [128, BLK], F32)
        A1 = cx.sb("A1", [128, BLK], F32)
        A2 = cx.sb("A2", [128, BLK], F32)
        sqa = cx.sb("sqa", [128, BLK], F32)
        if kv:
            KTb = cx.sb("KTb", [128, 2, BLK], BF16)
            VAb = cx.sb("VAb", [128, 4, 256], BF16)
        else:
            KT = cx.sb("KT", [128, 2, SL], BF16)
            VA = cx.sb("VA", [128, NKT, 256], BF16)
            maskb = cx.sb("maskb", [128, 4, 512], BF16)
            ones_bf = cx.sb("ones_bf", [128, 128], BF16)
            lam_t = cx.sb("lam_t", [128, 4, 64], F32)
            lam_s = cx.sb("lam_s", [128, 4], F32)
            nlam = cx.sb("nlam", [128, 1], F32)
            sw = cx.sb("sw", [128, 1], F32)
            Qp = cx.sb("Qp", [128, 2, 2, BLK], BF16)
            gT = cx.sb("gT", [128, 2, BLK], BF16)
            pt = [cx.sb("pt%d" % i, [128, BLK], BF16) for i in range(4)]
            rl = [cx.sb("rl%d" % i, [128, BLK], F32) for i in range(2)]
            ob = cx.sb("ob", [128, BLK], BF16)
        stage = fe["xT"]
        cx.dma(gsb[:], gn[:, :], [], ["gsb"], "su1")
        cx.dma(invf[:], invf_d[:, :], [], ["invf"], "su0")
        if not kv:
            cx.dma(sw[:], subw[:, :], [], ["sw"], "su2")
            cx.dma(maskb[:], mask_d[:, :, :], [], ["maskb"], "su4")
            for i in range(4):
                cx.dma(lam_t[:, i, :], lam_in[i:i + 1, :].partition_broadcast(128), [], ["lam_t"], "su3")
            cx.memset("pool", ones_bf[:], 1.0, ["ones_bf"])
            cx.memset("pool", Qp[:], 0.0, ["Qp"])
            lam_init = LAM_INIT[mode]
            cx.tt("dve", lam_t[:, 0, :], lam_t[:, 0, :], lam_t[:, 1, :], ALU.mult, ["lam_t"], ["lam_t"])
            cx.tt("dve", lam_t[:, 2, :], lam_t[:, 2, :], lam_t[:, 3, :], ALU.mult, ["lam_t"], ["lam_t"])
            cx.red("dve", lam_s[:, 0:1], lam_t[:, 0, :], ["lam_t"], ["lam_s"])
            cx.red("dve", lam_s[:, 1:2], lam_t[:, 2, :], ["lam_t"], ["lam_s"])
            cx.act(lam_s[:, 0:2], lam_s[:, 0:2], AF.Exp, ["lam_s"], ["lam_s"])
            cx.tt("dve", lam_s[:, 2:3], lam_s[:, 1:2], lam_s[:, 0:1], ALU.subtract, ["lam_s"], ["lam_s"])
            cx.ts("dve", nlam[:], lam_s[:, 2:3], -lam_init, None, ALU.add, None, ["lam_s"], ["nlam"])
            cx.ts("dve", sw[:], sw[:], 1.0 - lam_init, None, ALU.mult, None, ["sw"], ["sw"])
        cx.dma(stage[:], wq.rearrange("(c p) n -> p c n", p=128), [], ["fe_xT"], "stage")
        cx.tt("dve", Wq[:], stage[:], gsb[:].unsqueeze(2).to_broadcast([128, NCH, 512]), ALU.mult, ["fe_xT", "gsb"], ["W"])
        cx.memset("pool", Wqr[:], 0.0, ["Wr"])
        for mp in range(4):
            c = mp * 64
            cx.ts("dve", Wqr[:, :, c:c + 8], Wq[:, :, c + 8:c + 16], -1.0, None, ALU.mult, None, ["W"], ["Wr"])
            cx.cp("dve", Wqr[:, :, c + 8:c + 16], Wq[:, :, c:c + 8], ["W"], ["Wr"])
        if not kv:
            for j in range(2):
                for q4 in range(4):
                    sl = slice(q4 * (SL // 4), (q4 + 1) * (SL // 4))
                    cx.dma(KT[:, j, sl], KT_io[j, :, sl], [], ["KT"], "ktl")
            vv = V_io.rearrange("(t p) n -> p t n", p=128)
            for q4 in range(4):
                sl = slice(q4 * (NKT // 4), (q4 + 1) * (NKT // 4))
                cx.dma(VA[:, sl, :], vv[:, sl, :], [], ["VA"], "val")

        M = 12582912.0
        TWO_PI = float(2 * np.pi)

        def sin_table(dst, dkey, shift):
            cx.ts("dve", A1[:], ang[:], shift, None, ALU.add, None, ["ang"], ["A1"])
            cx.ts("dve", A2[:], A1[:], 1.0 / TWO_PI, M, ALU.mult, ALU.add, ["A1"], ["A2"])
            cx.ts("dve", A2[:], A2[:], M, None, ALU.subtract, None, ["A2"], ["A2"])
            cx.stt("dve", A1[:], A2[:], -TWO_PI, A1[:], ALU.mult, ALU.add, ["A2", "A1"], ["A1"])
            cx.ts("dve", A1[:], A1[:], -3.1415925, 3.1415925, ALU.max, ALU.min, ["A1"], ["A1"])
            cx.act(dst[:], A1[:], AF.Sin, ["A1"], [dkey])

        def proj_fm(W, wkey, c0, bank):
            for kc in range(NCH):
                cx.mm(ps[bank][:], W[:, kc, c0:c0 + 128], fe["xh"][:, kc, 1:BLK + 1], [wkey, "fe_xh"], ["ps%d" % bank],
                      start=(kc == 0), stop=(kc == NCH - 1))

        def roped(j):
            proj_fm(Wq, "W", j * 128, 6)
            proj_fm(Wqr, "Wr", j * 128, 7)
            cx.tt("dve", A1[:], ps[6][:], Ct[:], ALU.mult, ["ps6", "Ct"], ["A1"])
            cx.tt("dve", A2[:], ps[7][:], St[:], ALU.mult, ["ps7", "St"], ["A2"])

        for b in range(NB):
            t0 = b * BLK
            frontend_block(cx, fe, xT, None, None, b, False, BLK)
            cx.dma(posi[:], pos[0:1, t0:t0 + BLK].partition_broadcast(128), [], ["posi"], "posl")
            cx.cp("dve", sqa[:], posi[:], ["posi"], ["sqa"])
            cx.ts("dve", ang[:], sqa[:], invf[:, 0:1], None, ALU.mult, None, ["sqa", "invf"], ["ang"])
            sin_table(St, "St", 0.0)
            sin_table(Ct, "Ct", float(np.pi / 2))
            for j in range(2):
                roped(j)
                if kv:
                    cx.tt("pool", KTb[:, j, :], A1[:], A2[:], ALU.add, ["A1", "A2"], ["KTb"])
                    for ti in range(4):
                        for kc in range(NCH):
                            cx.mm(ps[5][:, 0:128], fe["xh"][:, kc, 1 + ti * 128:1 + (ti + 1) * 128], Wq[:, kc, 256 + j * 128:256 + (j + 1) * 128],
                                  ["fe_xh", "W"], ["ps5"], start=(kc == 0), stop=(kc == NCH - 1))
                        cx.cp("act", VAb[:, ti, j * 128:(j + 1) * 128], ps[5][:, 0:128], ["ps5"], ["VAb"])
                else:
                    cx.tt("pool", Qp[0:64, j, 0, :], A1[0:64, :], A2[0:64, :], ALU.add, ["A1", "A2"], ["Qp"])
                    cx.tt("pool", Qp[64:128, j, 1, :], A1[64:128, :], A2[64:128, :], ALU.add, ["A1", "A2"], ["Qp"])
                    proj_fm(Wq, "W", 256 + j * 128, 6)
                    cx.act(sqa[:], ps[6][:], AF.Sigmoid, ["ps6"], ["sqa"])
                    cx.tt("dve", gT[:, j, :], ps[6][:], sqa[:], ALU.mult, ["ps6", "sqa"], ["gT"])
            if kv:
                for j in range(2):
                    cx.dma(KT_io[j, :, t0:t0 + BLK], KTb[:, j, :], ["KTb"], [], "kto", is_out=True)
                vo = V_io.rearrange("(t p) n -> p t n", p=128)
                cx.dma(vo[:, b * 4:(b + 1) * 4, :], VAb[:], ["VAb"], [], "vao", is_out=True)
                continue
            for j in range(2):
                nkt = 4 * b + 4
                pairs = [(kt, m) for kt in range(nkt) for m in range(2)]
                NP = len(pairs)
                DEPTH = 3
                SCB = (0, 1, 6, 7)

                def emit_sc(i):
                    kt, m = pairs[i]
                    sbk = SCB[i % 4]
                    p_t = pt[i % 4]
                    pk = "pt%d" % (i % 4)
                    cx.mm(ps[sbk][:], KT[:, j, kt * 128:(kt + 1) * 128], Qp[:, j, m, :], ["KT", "Qp"], ["ps%d" % sbk])
                    cx.act(p_t[:], ps[sbk][:], AF.Exp, ["ps%d" % sbk], [pk], scale=0.125)
                    if kt >= 4 * b:
                        cx.tt("dve", p_t[:], p_t[:], maskb[:, kt - 4 * b, :], ALU.mult, [pk, "maskb"], [pk])

                def emit_pv(i):
                    kt, m = pairs[i]
                    p_t = pt[i % 4]
                    pk = "pt%d" % (i % 4)
                    cx.mm(ps[2 + m][:], VA[:, kt, j * 128:(j + 1) * 128], p_t[:], ["VA", pk], ["ps%d" % (2 + m)],
                          start=(kt == 0), stop=(kt == nkt - 1))
                    if m == 1:
                        cx.mm(ps[5][:], ones_bf[:], p_t[:], ["ones_bf", pk], ["ps5"], start=(kt == 0), stop=(kt == nkt - 1))
                    elif kt == 0:
                        cx.cp("dve", rl[0][:], p_t[:], [pk], ["rl0"])
                    else:
                        cx.tt("dve", rl[0][:], rl[0][:], p_t[:], ALU.add, ["rl0", pk], ["rl0"])

                for i in range(NP + DEPTH):
                    if i < NP:
                        emit_sc(i)
                    if i >= DEPTH:
                        emit_pv(i - DEPTH)
                cx.mm(ps[4][:], fe["ones"][:], rl[0][:], ["fe_ones", "rl0"], ["ps4"])
                cx.recip(rl[0][:], ps[4][:], ["ps4"], ["rl0"])
                cx.recip(rl[1][:], ps[5][:], ["ps5"], ["rl1"])
                cx.tt("dve", A1[:], ps[2][:], rl[0][:], ALU.mult, ["ps2", "rl0"], ["A1"])
                cx.tt("dve", A2[:], ps[3][:], rl[1][:], ALU.mult, ["ps3", "rl1"], ["A2"])
                cx.stt("dve", A1[:], A2[:], nlam[:, 0:1], A1[:], ALU.mult, ALU.add, ["A2", "nlam", "A1"], ["A1"])
                cx.tt("pool", sqa[:], A1[:], A1[:], ALU.mult, ["A1"], ["sqa"])
                cx.mm(ps[6][:], fe["ones"][:], sqa[:], ["fe_ones", "sqa"], ["ps6"])
                cx.act(sqa[:], ps[6][:], AF.Sqrt, ["ps6"], ["sqa"], bias=1e-5, scale=1.0 / 128)
                cx.recip(sqa[:], sqa[:], ["sqa"], ["sqa"])
                cx.stt("dve", A1[:], A1[:], sw[:, 0:1], sqa[:], ALU.mult, ALU.mult, ["A1", "sw", "sqa"], ["A1"])
                cx.tt("dve", ob[:], A1[:], gT[:, j, :], ALU.mult, ["A1", "gT"], ["ob"])
                cx.dma(oT_out[j * 128:(j + 1) * 128, t0:t0 + BLK], ob[:], ["ob"], [], "oTo", is_out=True)
        cx.done()
    return nc


def _attn_consts():
    i = np.arange(128)
    q = np.arange(512)
    mask = np.zeros((128, 4, 512), np.float32)
    for a in range(4):
        mask[:, a, :] = (q[None, :] >= (i[:, None] + 128 * a))
    invf = np.zeros((128, 1), np.float32)
    base = (500000.0 ** (-np.arange(0, 16, 2, dtype=np.float32) / np.float32(16))).astype(np.float32)
    for m in range(2):
        for d in range(16):
            invf[m * 64 + d, 0] = base[d % 8]
    return mask.astype(ml_dtypes.bfloat16), invf


def attn_in_map(inp, mode, g, xT_b, pos_b, KT_c=None, V_c=None):
    cs = slice(g * 256, (g + 1) * 256)
    cs2 = slice(1024 + g * 256, 1024 + (g + 1) * 256)
    mask, invf = _attn_consts()
    m = {"xT": xT_b, "pos": np.ascontiguousarray(pos_b.reshape(1, -1)), "invf": invf}
    if mode == "kv":
        w = inp["w_kv"]
        m["gn"] = fm8(inp["kv_norm"])
    else:
        j = mode - 2
        w = inp["b_w_in"][j]
        m["gn"] = fm8(inp["b_norm"][j])
        m["lam_in"] = np.stack([inp["b_lq1"][j], inp["b_lk1"][j], inp["b_lq2"][j], inp["b_lk2"][j]]).astype(np.float32)
        m["subw"] = np.ascontiguousarray(inp["b_subln"][j].reshape(128, 1))
        m["mask"] = mask
        m["KT_in"] = KT_c
        m["V_in"] = V_c
    m["wq"] = np.ascontiguousarray(np.concatenate([w[:, cs], w[:, cs2]], axis=1))
    return m


def _run(nc, maps):
    return run_bass_kernel_spmd(nc, maps, core_ids=list(range(8))).results


def _xupd(xT, oT, wout, SL, final, gfin=None):
    NT = SL // 4
    nc = build_xupd(NT, final)
    maps = []
    for c in range(8):
        b, q = c // 4, c % 4
        m = {"xT": np.ascontiguousarray(xT[b][:, q * NT:(q + 1) * NT]),
             "oT": np.ascontiguousarray(oT[b][:, q * NT:(q + 1) * NT]),
             "wout": np.ascontiguousarray(wout)}
        if final:
            m["gfin"] = gfin
        maps.append(m)
    r = _run(nc, maps)
    return [np.concatenate([r[b * 4 + q]["xT_out"] for q in range(4)], axis=1) for b in range(2)]


def _forward(inp, SL):
    inp = {k: np.asarray(v) for k, v in inp.items()}
    xT = [np.ascontiguousarray(inp["x"][b].T) for b in range(2)]
    gather = lambda r: [np.concatenate([r[b * 4 + g]["oT_out"] for g in range(4)], axis=0) for b in range(2)]
    r = _run(build_rwkv(SL, 0), [rwkv_in_map(inp, 0, c % 4, xT[c // 4]) for c in range(8)])
    oT = gather(r)
    vf = [r[c]["vfirst_out"] for c in range(8)]
    r = _run(build_rwkv(SL, 1), [rwkv_in_map(inp, 1, c % 4, xT[c // 4], oT[c // 4], vf[c]) for c in range(8)])
    xT = [r[b * 4]["xT_out"] for b in range(2)]
    oT = gather(r)
    xT = _xupd(xT, oT, inp["a_w_out"][1], SL, False)
    r = _run(build_attn(SL, "kv"), [attn_in_map(inp, "kv", c % 4, xT[c // 4], inp["positions"][c // 4]) for c in range(8)])
    KT = [r[c]["KT_out"] for c in range(8)]
    V = [r[c]["V_out"] for c in range(8)]
    for j in range(2):
        r = _run(build_attn(SL, 2 + j),
                 [attn_in_map(inp, 2 + j, c % 4, xT[c // 4], inp["positions"][c // 4], KT[c], V[c]) for c in range(8)])
        oT = gather(r)
        if j == 0:
            xT = _xupd(xT, oT, inp["b_w_out"][0], SL, False)
        else:
            xT = _xupd(xT, oT, inp["b_w_out"][1], SL, True, fm8(inp["final_norm"]))
    return np.stack([np.ascontiguousarray(xT[b].T) for b in range(2)]).astype(np.float32)


def kernel(**inputs):
    return _forward(inputs, 16384)
```

```python
import numpy as np
import ml_dtypes
from contextlib import ExitStack
import concourse.bass as bass
import concourse.mybir as mybir
from concourse.bass_utils import run_bass_kernel_spmd

F32 = mybir.dt.float32
BF16 = mybir.dt.bfloat16
I32 = mybir.dt.int32
AF = mybir.ActivationFunctionType
ALU = mybir.AluOpType
AX = mybir.AxisListType

D = 1024
NCH = 8
ENGS = ["pe", "act", "dve", "pool", "sp"]


class Sched:
    def __init__(self, nc, es):
        self.nc = nc
        self.es = es
        self.streams = {e: [] for e in ENGS}
        self.sem = {e: es.enter_context(nc.semaphore("c_" + e)) for e in ENGS}
        self.count = {e: 0 for e in ENGS}
        self.waited = {e: {} for e in ENGS}
        self.last_w = {}
        self.readers = {}
        self.dsem = {}
        self.out_tokens = []
        self.sub = {}
        self.parity = None
        self.dbkeys = set()

    def dma_sem(self, name):
        if name not in self.dsem:
            self.dsem[name] = [self.es.enter_context(self.nc.semaphore("d_" + name)), 0]
        return name

    def op(self, eng, fn, reads=(), writes=(), dma=None, is_out=False):
        import os as _os
        self.nops = getattr(self, "nops", 0) + 1
        if self.nops > int(_os.environ.get("OPLIMIT", "100000000")):
            return None
        if self.parity is not None:
            reads = [k + "_%d" % self.parity if k in self.dbkeys else k for k in reads]
            writes = [k + "_%d" % self.parity if k in self.dbkeys else k for k in writes]
        nk = lambda k: k[:3] if (k.startswith("ps") and len(k) > 3 and k[2].isdigit()) else k
        reads = [nk(k) for k in reads]
        writes = [nk(k) for k in writes]
        writes = writes + [k for k in reads if k.startswith("ps") and k[2].isdigit()]
        reads = [k for k in reads if not (k.startswith("ps") and k[2].isdigit())]
        deps = []

        def bank_of(k):
            return k[:3] if (k.startswith("ps") and len(k) > 3 and k[2].isdigit()) else None

        def is_bank(k):
            return k.startswith("ps") and len(k) == 3 and k[2].isdigit()

        def rdep(k):
            if k in self.last_w:
                deps.append(self.last_w[k])

        def wdep(k):
            if k in self.last_w:
                deps.append(self.last_w[k])
            deps.extend(self.readers.get(k, {}).values())

        for k in reads:
            rdep(k)
            bk = bank_of(k)
            if bk:
                self.sub.setdefault(bk, set()).add(k)
                rdep(bk)
            if is_bank(k):
                for s_ in self.sub.get(k, ()):
                    rdep(s_)
        for k in writes:
            wdep(k)
            bk = bank_of(k)
            if bk:
                self.sub.setdefault(bk, set()).add(k)
                wdep(bk)
            if is_bank(k):
                for s_ in self.sub.get(k, ()):
                    wdep(s_)
        waits = []
        wd = self.waited[eng]
        for (sid, sem, val) in deps:
            if sid == "pe" and eng == "pe":
                continue
            if wd.get(sid, 0) < val:
                wd[sid] = val
                waits.append((sem, val))
        if dma is None:
            self.count[eng] += 1
            tok = (eng, self.sem[eng], self.count[eng])
            inc = (self.sem[eng], 1)
        else:
            self.dma_sem(dma)
            d = self.dsem[dma]
            d[1] += 16
            tok = ("d_" + dma, d[0], d[1])
            inc = (d[0], 16)
        self.streams[eng].append((waits, fn, inc))
        for k in reads:
            self.readers.setdefault(k, {})[tok[0]] = tok
        for k in writes:
            self.last_w[k] = tok
            self.readers[k] = {}
        if is_out:
            self.out_tokens.append(tok)
        return tok

    def finish(self):
        final = {}
        for (sid, sem, val) in self.out_tokens:
            if final.get(sid, (None, 0))[1] < val:
                final[sid] = (sem, val)
        self.streams["sp"].append((list(final.values()), None, None))

    def emit(self, block):
        def run(eng_name):
            def body(eng):
                for waits, fn, inc in self.streams[eng_name]:
                    for (s, v) in waits:
                        eng.wait_ge(s, v)
                    if fn is not None:
                        fn(eng).then_inc(inc[0], inc[1])
            return body
        block.tensor(run("pe"))
        block.scalar(run("act"))
        block.vector(run("dve"))
        block.gpsimd(run("pool"))
        block.sync(run("sp"))


class Ctx:
    def __init__(self, nc, es):
        self.nc = nc
        self.es = es
        self.S = Sched(nc, es)
        self.psall = es.enter_context(nc.psum_tensor("psall", [128, 8, 512], F32))
        self.ps = [self.psall[:, i, :] for i in range(8)]

    def sb(self, name, shape, dt):
        return self.es.enter_context(self.nc.sbuf_tensor(name, list(shape), dt))

    def dram(self, name, shape, dt, kind):
        return self.nc.dram_tensor(name, list(shape), dt, kind=kind).ap()

    def dma(self, out, in_, r, w, sem, eng="sp", is_out=False):
        return self.S.op(eng, lambda e: e.dma_start(out=out, in_=in_), reads=r, writes=w, dma=sem, is_out=is_out)

    def mm(self, out, lhsT, rhs, r, w, start=True, stop=True):
        return self.S.op("pe", lambda e: e.matmul(out, lhsT=lhsT, rhs=rhs, start=start, stop=stop), reads=r, writes=w)

    def tr(self, out, in_, ident, r, w):
        return self.S.op("pe", lambda e: e.transpose(out, in_, ident), reads=r, writes=w)

    def act(self, out, in_, func, r, w, bias=None, scale=None, accum_out=None):
        kw = {}
        if bias is not None:
            kw["bias"] = bias
        if scale is not None:
            kw["scale"] = scale
        if accum_out is not None:
            kw["accum_out"] = accum_out
        return self.S.op("act", lambda e: e.activation(out=out, in_=in_, func=func, **kw), reads=r, writes=w)

    def tt(self, eng, out, in0, in1, op, r, w):
        return self.S.op(eng, lambda e: e.tensor_tensor(out=out, in0=in0, in1=in1, op=op), reads=r, writes=w)

    def ts(self, eng, out, in0, s1, s2, op0, op1, r, w):
        if op1 is None:
            return self.S.op(eng, lambda e: e.tensor_scalar(out=out, in0=in0, scalar1=s1, scalar2=None, op0=op0), reads=r, writes=w)
        return self.S.op(eng, lambda e: e.tensor_scalar(out=out, in0=in0, scalar1=s1, scalar2=s2, op0=op0, op1=op1), reads=r, writes=w)

    def stt(self, eng, out, in0, scalar, in1, op0, op1, r, w):
        return self.S.op(eng, lambda e: e.scalar_tensor_tensor(out=out, in0=in0, scalar=scalar, in1=in1, op0=op0, op1=op1), reads=r, writes=w)

    def cp(self, eng, out, in_, r, w):
        if eng == "act":
            return self.S.op("act", lambda e: e.copy(out=out, in_=in_), reads=r, writes=w)
        return self.S.op(eng, lambda e: e.tensor_copy(out=out, in_=in_), reads=r, writes=w)

    def red(self, eng, out, in_, r, w):
        return self.S.op(eng, lambda e: e.tensor_reduce(out=out, in_=in_, axis=AX.X, op=ALU.add), reads=r, writes=w)

    def recip(self, out, in_, r, w):
        return self.S.op("dve", lambda e: e.reciprocal(out=out, in_=in_), reads=r, writes=w)

    def memset(self, eng, ap, val, w):
        return self.S.op(eng, lambda e: e.memset(ap, val), writes=w)

    def done(self):
        self.S.finish()
        with self.nc.Block() as block:
            self.S.emit(block)


def frontend_alloc(cx, has_prev, BLK=512):
    fe = {}
    fe["xT"] = cx.sb("fe_xT", [128, NCH, BLK], F32)
    fe["sq"] = cx.sb("fe_sq", [128, 2, BLK], F32)
    fe["xh"] = cx.sb("fe_xh", [128, NCH, BLK + 1], BF16)
    fe["rstd"] = cx.sb("fe_rstd", [128, BLK], F32)
    fe["ones"] = cx.sb("fe_ones", [128, 128], F32)
    cx.memset("pool", fe["ones"][:], 1.0, ["fe_ones"])
    cx.memset("pool", fe["xh"][:], 0.0, ["fe_xh"])
    if has_prev:
        fe["oT"] = cx.sb("fe_oT", [128, NCH, BLK], BF16)
        fe["wo"] = cx.sb("fe_wo", [128, NCH, D], BF16)
    return fe


def frontend_load_wout(cx, fe, wout_ap, stage):
    for half in range(2):
        cx.dma(stage[:, 0:4, :], wout_ap.rearrange("(c p) d -> p c d", p=128)[:, half * 4:(half + 1) * 4, :],
               [], ["stage"], "stage")
        cx.cp("dve", fe["wo"][:, half * 4:(half + 1) * 4, :], stage[:, 0:4, :], ["stage"], ["fe_wo"])


def frontend_block(cx, fe, xT_ap, oT_ap, xT_out_ap, b, has_prev, BLK=512, store_x=True, norm=True):
    t0 = b * BLK
    xv = xT_ap.rearrange("(c p) s -> p c s", p=128)
    cx.dma(fe["xT"][:], xv[:, :, t0:t0 + BLK], [], ["fe_xT"], "fe_x")
    if has_prev:
        ov = oT_ap.rearrange("(c p) s -> p c s", p=128)
        cx.dma(fe["oT"][:], ov[:, :, t0:t0 + BLK], [], ["fe_oT"], "fe_o")
        for dc in range(NCH):
            bank = dc % 4
            for kc in range(NCH):
                cx.mm(cx.ps[bank][:], fe["wo"][:, kc, dc * 128:(dc + 1) * 128], fe["oT"][:, kc, :],
                      ["fe_wo", "fe_oT"], ["ps%d" % bank], start=(kc == 0), stop=(kc == NCH - 1))
            cx.tt("dve", fe["xT"][:, dc, :], fe["xT"][:, dc, :], cx.ps[bank][:], ALU.add,
                  ["fe_xT", "ps%d" % bank], ["fe_xT"])
        if store_x:
            ovx = xT_out_ap.rearrange("(c p) s -> p c s", p=128)
            cx.dma(ovx[:, :, t0:t0 + BLK], fe["xT"][:], ["fe_xT"], [], "fe_xs", is_out=True)
    for dc in range(NCH):
        cx.act(fe["sq"][:, dc % 2, :], fe["xT"][:, dc, :], AF.Square, ["fe_xT"], ["fe_sq%d" % (dc % 2)])
        cx.mm(cx.ps[4][:], fe["ones"][:], fe["sq"][:, dc % 2, :], ["fe_ones", "fe_sq%d" % (dc % 2)], ["ps4"],
              start=(dc == 0), stop=(dc == NCH - 1))
    cx.act(fe["rstd"][:], cx.ps[4][:], AF.Sqrt, ["ps4"], ["fe_rstd"], bias=1e-6, scale=1.0 / D)
    cx.recip(fe["rstd"][:], fe["rstd"][:], ["fe_rstd"], ["fe_rstd"])
    if b > 0:
        cx.cp("pool", fe["xh"][:, :, 0:1], fe["xh"][:, :, BLK:BLK + 1], ["fe_xh"], ["fe_xh"])
    for dc in range(NCH):
        eng = "dve" if dc % 2 == 0 else "pool"
        cx.tt(eng, fe["xh"][:, dc, 1:BLK + 1], fe["xT"][:, dc, :], fe["rstd"][:], ALU.mult,
              ["fe_xT", "fe_rstd"], ["fe_xh"])


C_DEC = 0.6065306597126334
HC = 256


def build_rwkv(SL, layer, STOP=99):
    has_prev = layer > 0
    BLK = 512
    NB = SL // BLK
    nc = bass.Bass("TRN2", target_bir_lowering=False)
    with ExitStack() as es:
        cx = Ctx(nc, es)
        ps = cx.ps
        xT = cx.dram("xT", [D, SL], F32, "ExternalInput")
        win = cx.dram("win", [4, D, HC], F32, "ExternalInput")
        gmu = cx.dram("gmu", [128, 7, NCH], F32, "ExternalInput")
        l1w = cx.dram("l1w", [D, 160], F32, "ExternalInput")
        l2w = cx.dram("l2w", [64, 3, HC], F32, "ExternalInput")
        vecs = cx.dram("vecs", [8, HC], F32, "ExternalInput")
        cst = cx.dram("cst", [128, 1152], F32, "ExternalInput")
        oT_out = cx.dram("oT_out", [HC, SL], BF16, "ExternalOutput")
        if has_prev:
            oT = cx.dram("oT", [D, SL], BF16, "ExternalInput")
            wout = cx.dram("wout", [D, D], F32, "ExternalInput")
            vfirst = cx.dram("vfirst", [SL, HC], F32, "ExternalInput")
            xT_out = cx.dram("xT_out", [D, SL], F32, "ExternalOutput")
        else:
            oT = wout = xT_out = None
            vfirst_out = cx.dram("vfirst_out", [SL, HC], F32, "ExternalOutput")
        fe = frontend_alloc(cx, has_prev, BLK)
        stage = cx.sb("stage", [128, NCH, HC * 2], F32)
        csb = cx.sb("csb", [128, 1152], F32)
        ident_bf = cx.sb("ident_bf", [128, 128], BF16)
        gm = cx.sb("gm", [128, 7, NCH], F32)
        coef = cx.sb("coef", [128, 12, NCH], F32)
        Wc = cx.sb("Wc", [128, NCH, 4 * HC], BF16)
        Wp = cx.sb("Wp", [128, NCH, 4 * HC], BF16)
        L1c = cx.sb("L1c", [128, NCH, 160], BF16)
        L1p = cx.sb("L1p", [128, NCH, 160], BF16)
        L2 = cx.sb("L2", [64, 3, HC], BF16)
        L2f = cx.sb("L2f", [64, 3, HC], F32)
        vb = cx.sb("vb", [128, 8, HC], F32)
        h1 = cx.sb("h1", [64, 3, BLK], BF16)
        ST = cx.sb("ST", [64, 4, 64], F32)
        STb = cx.sb("STb", [64, 4, 64], BF16)
        gl = cx.sb("gl", [64, 4], F32)

        def T(name, dt=F32, w=HC):
            return cx.sb(name, [128, w], dt)
        r_sb, k_sb, v_sb, gate, sg, a_sb = T("r_sb"), T("k_sb"), T("v_sb"), T("gate"), T("sg"), T("a_sb")
        t1, t2, t3, kkn, kmod, bvec = T("t1"), T("t2"), T("t3"), T("kkn"), T("kmod"), T("bvec")
        epos, eneg, eprev, ehat = T("epos"), T("eneg"), T("eprev"), T("ehat")
        ss4, bs4, s14, s24 = T("ss4", w=4), T("bs4", w=4), T("s14", w=4), T("s24", w=4)
        tmT = cx.sb("tmT", [128, 4, HC], BF16)
        khat, bhat, vbf = T("khat", BF16), T("bhat", BF16), T("vbf", BF16)
        fm = cx.sb("fm", [64, 4, 4, 128], BF16)
        Am = cx.sb("Am", [128, 4, 512], BF16)
        Nf = cx.sb("Nf", [128, 4, 2, 128], BF16)
        NTf = cx.sb("NTf", [128, 4, 2, 128], BF16)
        Tf = cx.sb("Tf", [128, 4, 2, 128], BF16)
        Wf = cx.sb("Wf", [128, 4, 64], BF16)
        Zb = cx.sb("Zb", [128, 4, 64], BF16)
        y_sb, yc, o_bf = T("y_sb"), T("yc"), T("o_bf", BF16)
        oT_sb = cx.sb("oT_sb", [128, 2, 128], BF16)
        vf_sb = T("vf_sb")

        cx.dma(csb[:], cst[:, :], [], ["csb"], "su0")
        cx.dma(gm[:], gmu[:, :, :], [], ["gm"], "su1")
        cx.dma(L2f[:], l2w[:, :, :], [], ["L2f"], "su2")
        for i in range(8):
            cx.dma(vb[:, i, :], vecs[i:i + 1, :].partition_broadcast(128), [], ["vb"], "su3")
        ident_f = csb[:, 0:128]
        tri = csb[:, 128:256]
        sfx = csb[:, 256:384]
        mask4 = csb[:, 384:896]
        maskNT = csb[:, 896:1024]
        ones_col = csb[:, 1024:1025]
        cx.cp("dve", ident_bf[:], ident_f, ["csb"], ["ident_bf"])
        cx.cp("dve", L2[:], L2f[:], ["L2f"], ["L2"])
        for p in range(6):
            cx.tt("dve", coef[:, 6 + p, :], gm[:, 0, :], gm[:, 1 + p, :], ALU.mult, ["gm"], ["coef"])
            cx.tt("dve", coef[:, p, :], gm[:, 0, :], coef[:, 6 + p, :], ALU.subtract, ["gm", "coef"], ["coef"])
        for p in range(4):
            cx.dma(stage[:, :, 0:HC], win[p].rearrange("(c p) n -> p c n", p=128), [], ["stage"], "stage")
            cx.tt("dve", Wc[:, :, p * HC:(p + 1) * HC], stage[:, :, 0:HC],
                  coef[:, p, :].unsqueeze(2).to_broadcast([128, NCH, HC]), ALU.mult, ["stage", "coef"], ["Wc"])
            cx.tt("pool", Wp[:, :, p * HC:(p + 1) * HC], stage[:, :, 0:HC],
                  coef[:, 6 + p, :].unsqueeze(2).to_broadcast([128, NCH, HC]), ALU.mult, ["stage", "coef"], ["Wp"])
        cx.dma(stage[:, :, 0:160], l1w.rearrange("(c p) n -> p c n", p=128), [], ["stage"], "stage")
        for (lo, hi, p) in ((0, 64, 4), (64, 128, 5), (128, 160, 2)):
            cx.tt("dve", L1c[:, :, lo:hi], stage[:, :, lo:hi],
                  coef[:, p, :].unsqueeze(2).to_broadcast([128, NCH, hi - lo]), ALU.mult, ["stage", "coef"], ["L1c"])
            cx.tt("pool", L1p[:, :, lo:hi], stage[:, :, lo:hi],
                  coef[:, 6 + p, :].unsqueeze(2).to_broadcast([128, NCH, hi - lo]), ALU.mult, ["stage", "coef"], ["L1p"])
        if has_prev:
            wv = wout.rearrange("(c p) d -> p c d", p=128)
            for kc in range(NCH):
                cx.dma(stage[:, 0:2, :], wv[:, kc, :].rearrange("p (a n) -> p a n", a=2), [], ["stage"], "stage")
                cx.cp("dve", fe["wo"][:, kc, :].rearrange("p (a n) -> p a n", a=2), stage[:, 0:2, :], ["stage"], ["fe_wo"])
        cx.memset("dve", ST[:], 0.0, ["ST"])
        cx.memset("dve", STb[:], 0.0, ["STb"])

        w0_b, a0_b, v0_b, kk_b, ka_b, rk_b, lnw_b, lnb_b = [vb[:, i, :] for i in range(8)]

        def bc4(ap4):
            return ap4.unsqueeze(2).to_broadcast([128, 4, 64])

        def v3(ap):
            return ap.rearrange("p (h n) -> p h n", h=4)

        e1, e2 = T("e1"), T("e2")
        DB = [(v_sb, bs4, gate, y_sb), (T("v_sb_b"), T("bs4_b", w=4), T("gate_b"), T("y_sb_b"))]
        cx.S.dbkeys = set(["v_sb", "bs4", "gate", "y_sb"])

        def tile_pre(b, ti, pp):
            v_sb, bs4, gate, y_sb = DB[pp]
            tok0 = b * BLK + ti * 128
            c0 = 1 + ti * 128
            tok0 = b * BLK + ti * 128
            c0 = 1 + ti * 128
            for (bank, lo) in ((0, 0), (1, 512)):
                for kc in range(NCH):
                    cx.mm(ps[bank][:], xh[:, kc, c0:c0 + 128], Wc[:, kc, lo:lo + 512], ["fe_xh", "Wc"], ["ps%d" % bank],
                          start=(kc == 0), stop=False)
                    cx.mm(ps[bank][:], xh[:, kc, c0 - 1:c0 + 127], Wp[:, kc, lo:lo + 512], ["fe_xh", "Wp"], ["ps%d" % bank],
                          start=False, stop=(kc == NCH - 1))
            hs = slice(ti * 128, (ti + 1) * 128)
            cx.mm(ps[2][:, 0:HC], h1[0:64, 0, hs], L2[0:64, 0, :], ["h1", "L2"], ["ps2"])
            cx.mm(ps[2][:, HC:2 * HC], h1[0:64, 1, hs], L2[0:64, 1, :], ["h1", "L2"], ["ps2"])
            if has_prev:
                cx.mm(ps[3][:, 0:HC], h1[0:32, 2, hs], L2[0:32, 2, :], ["h1", "L2"], ["ps3"])
            cx.cp("act", r_sb[:], ps[0][:, 0:HC], ["ps0"], ["r_sb"])
            cx.cp("dve", k_sb[:], ps[0][:, HC:2 * HC], ["ps0"], ["k_sb"])
            cx.cp("act", v_sb[:], ps[1][:, 0:HC], ["ps1"], ["v_sb"])
            cx.act(t1[:], ps[1][:, HC:2 * HC], AF.Sigmoid, ["ps1"], ["t1"])
            cx.tt("dve", gate[:], ps[1][:, HC:2 * HC], t1[:], ALU.mult, ["ps1", "t1"], ["gate"])
            cx.tt("dve", t2[:], ps[2][:, 0:HC], w0_b, ALU.add, ["ps2", "vb"], ["t2"])
            cx.act(sg[:], t2[:], AF.Sigmoid, ["t2"], ["sg"])
            cx.tt("dve", t3[:], ps[2][:, HC:2 * HC], a0_b, ALU.add, ["ps2", "vb"], ["t3"])
            cx.act(a_sb[:], t3[:], AF.Sigmoid, ["t3"], ["a_sb"])
            if has_prev:
                cx.dma(vf_sb[:], vfirst[tok0:tok0 + 128, :], [], ["vf_sb"], "vf")
                cx.tt("dve", t2[:], ps[3][:, 0:HC], v0_b, ALU.add, ["ps3", "vb"], ["t2"])
                cx.act(t2[:], t2[:], AF.Sigmoid, ["t2"], ["t2"])
                cx.tt("pool", t3[:], vf_sb[:], v_sb[:], ALU.subtract, ["vf_sb", "v_sb"], ["t3"])
                cx.tt("pool", t3[:], t3[:], t2[:], ALU.mult, ["t3", "t2"], ["t3"])
                cx.tt("pool", v_sb[:], v_sb[:], t3[:], ALU.add, ["v_sb", "t3"], ["v_sb"])
            else:
                cx.dma(vfirst_out[tok0:tok0 + 128, :], v_sb[:], ["v_sb"], [], "vfo", is_out=True)
            cx.mm(ps[4][:, 0:HC], tri, sg[:], ["csb", "sg"], ["ps4"])
            cx.mm(ps[4][:, HC:2 * HC], sfx, sg[:], ["csb", "sg"], ["ps4"])
            for h in range(4):
                cx.mm(ps[3][0:64, HC + h:HC + h + 1], sg[:, h * 64:(h + 1) * 64], ones_col, ["sg", "csb"], ["ps3b"])
            cx.act(gl[:], ps[3][0:64, HC:HC + 4], AF.Exp, ["ps3b"], ["gl"], scale=-C_DEC)
            cx.act(epos[:], ps[4][:, 0:HC], AF.Exp, ["ps4"], ["epos"], scale=-C_DEC)
            cx.act(eneg[:], ps[4][:, 0:HC], AF.Exp, ["ps4"], ["eneg"], scale=C_DEC)
            cx.tt("dve", t2[:], ps[4][:, 0:HC], sg[:], ALU.subtract, ["ps4", "sg"], ["t2"])
            cx.act(eprev[:], t2[:], AF.Exp, ["t2"], ["eprev"], scale=-C_DEC)
            cx.act(ehat[:], ps[4][:, HC:2 * HC], AF.Exp, ["ps4"], ["ehat"], scale=-C_DEC)
            cx.tt("pool", kkn[:], k_sb[:], kk_b, ALU.mult, ["k_sb", "vb"], ["kkn"])
            cx.tt("pool", t1[:], kkn[:], kkn[:], ALU.mult, ["kkn"], ["t1"])
            cx.red("dve", ss4[:], v3(t1[:]), ["t1"], ["ss4"])
            cx.act(ss4[:], ss4[:], AF.Sqrt, ["ss4"], ["ss4"])
            cx.ts("dve", ss4[:], ss4[:], 1e-12, None, ALU.max, None, ["ss4"], ["ss4"])
            cx.recip(ss4[:], ss4[:], ["ss4"], ["ss4"])
            cx.tt("dve", v3(kkn[:]), v3(kkn[:]), bc4(ss4[:]), ALU.mult, ["kkn", "ss4"], ["kkn"])
            cx.stt("dve", t3[:], a_sb[:], -1.0, ka_b, ALU.add, ALU.mult, ["a_sb", "vb"], ["t3"])
            cx.stt("dve", kmod[:], t3[:], 1.0, k_sb[:], ALU.add, ALU.mult, ["t3", "k_sb"], ["kmod"])
            cx.tt("pool", bvec[:], kkn[:], a_sb[:], ALU.mult, ["kkn", "a_sb"], ["bvec"])
            cx.tt("pool", t1[:], r_sb[:], kmod[:], ALU.mult, ["r_sb", "kmod"], ["t1"])
            cx.tt("pool", t1[:], t1[:], rk_b, ALU.mult, ["t1", "vb"], ["t1"])
            cx.red("dve", bs4[:], v3(t1[:]), ["t1"], ["bs4"])
            cx.stt("dve", tmT[:, 0, :], kkn[:], -1.0, eprev[:], ALU.mult, ALU.mult, ["kkn", "eprev"], ["tmT0"])
            cx.tt("pool", tmT[:, 1, :], r_sb[:], epos[:], ALU.mult, ["r_sb", "epos"], ["tmT1"])
            cx.tt("dve", tmT[:, 2, :], bvec[:], eneg[:], ALU.mult, ["bvec", "eneg"], ["tmT2"])
            cx.tt("pool", tmT[:, 3, :], kmod[:], eneg[:], ALU.mult, ["kmod", "eneg"], ["tmT3"])
            cx.tt("dve", khat[:], kmod[:], ehat[:], ALU.mult, ["kmod", "ehat"], ["khat"])
            cx.tt("pool", bhat[:], bvec[:], ehat[:], ALU.mult, ["bvec", "ehat"], ["bhat"])
            cx.cp("act", vbf[:], v_sb[:], ["v_sb"], ["vbf"])
            for q in range(4):
                bank = 5 + (q // 2)
                pv = ps[bank][:].bitcast(BF16)
                for h in range(4):
                    col = ((q % 2) * 4 + h) * 128
                    cx.tr(pv[0:64, col:col + 128], tmT[:, q, h * 64:(h + 1) * 64], ident_bf[:],
                          ["tmT%d" % q, "ident_bf"], ["ps%d" % bank])
            for half in range(2):
                pv = ps[5 + half][:].bitcast(BF16)
                src = pv[0:64, :].rearrange("k (q h t) -> k h q t", q=2, h=4)
                cx.cp("act" if half == 0 else "dve", fm[:, :, 2 * half:2 * half + 2, :], src, ["ps%d" % (5 + half)], ["fm"])
            for h in range(4):
                bank = h % 2
                ar = fm[:, h, 0:2, :].rearrange("k q t -> k (q t)")
                cx.mm(ps[bank][:, 0:256], fm[:, h, 2, :], ar, ["fm"], ["ps%d" % bank])
                cx.mm(ps[bank][:, 256:512], fm[:, h, 3, :], ar, ["fm"], ["ps%d" % bank])
                cx.mm(ps[2 + bank][:, 0:128], fm[:, h, 0, :], fm[:, h, 2, :], ["fm"], ["ps%da" % (2 + bank)])
                cx.tt("dve", Am[:, h, :], ps[bank][:], mask4, ALU.mult, ["ps%d" % bank, "csb"], ["Am%d" % h])
                cx.tt("dve", Nf[:, h, 0, :], ps[bank][:, 0:128], mask4[:, 0:128], ALU.mult, ["ps%d" % bank, "csb"], ["Nf%d" % h])
                cx.tt("dve", NTf[:, h, 0, :], ps[2 + bank][:, 0:128], maskNT, ALU.mult, ["ps%da" % (2 + bank), "csb"], ["NTf%d" % h])
                cx.tt("pool", Tf[:, h, 0, :], Nf[:, h, 0, :], ident_f, ALU.add, ["Nf%d" % h, "csb"], ["Tf%d" % h])

        def inverse(pp):
            for kstep in range(1, 7):
                src, dst = (kstep - 1) % 2, kstep % 2
                for h in range(4):
                    bk = 4 + h
                    cx.mm(ps[bk][:, 0:128], Nf[:, h, src, :], NTf[:, h, src, :], ["Nf%d" % h, "NTf%d" % h], ["ps%d" % bk])
                    if kstep < 6:
                        cx.mm(ps[bk][:, 128:256], NTf[:, h, src, :], Nf[:, h, src, :], ["Nf%d" % h, "NTf%d" % h], ["ps%d" % bk])
                yield
                for h in range(4):
                    bk = 4 + h
                    cx.cp("act", NTf[:, h, dst, :], ps[bk][:, 0:128], ["ps%d" % bk], ["NTf%d" % h])
                    if kstep < 6:
                        cx.cp("dve", Nf[:, h, dst, :], ps[bk][:, 128:256], ["ps%d" % bk], ["Nf%d" % h])
                yield
                for h in range(4):
                    bk = 4 + h
                    cx.mm(ps[bk][:, 256:384], NTf[:, h, dst, :], Tf[:, h, src, :], ["NTf%d" % h, "Tf%d" % h], ["ps%d" % bk])
                yield
                for h in range(4):
                    bk = 4 + h
                    cx.tt("dve", Tf[:, h, dst, :], ps[bk][:, 256:384], Tf[:, h, src, :], ALU.add, ["ps%d" % bk, "Tf%d" % h], ["Tf%d" % h])

        def tile_state(b, ti, pp):
            v_sb, bs4, gate, y_sb = DB[pp]
            TI = 6 % 2
            HS = [slice(h * 64, (h + 1) * 64) for h in range(4)]
            for h in range(4):
                cx.mm(ps[0][:, HS[h]], fm[:, h, 0, :], STb[:, h, :], ["fm", "STb"], ["ps0"], start=True, stop=False)
                cx.mm(ps[0][:, HS[h]], Am[:, h, 256:384], vbf[:, HS[h]], ["Am%d" % h, "vbf"], ["ps0"], start=False, stop=True)
            cx.cp("act", Wf[:].rearrange("p h v -> p (h v)"), ps[0][:, 0:HC], ["ps0"], ["Wf"])
            for h in range(4):
                cx.mm(ps[1][:, HS[h]], Tf[:, h, TI, :], Wf[:, h, :], ["Tf%d" % h, "Wf"], ["ps1"])
            cx.cp("dve", Zb[:].rearrange("p h v -> p (h v)"), ps[1][:, 0:HC], ["ps1"], ["Zb"])
            for h in range(4):
                cx.mm(ps[2][:, HS[h]], fm[:, h, 1, :], STb[:, h, :], ["fm", "STb"], ["ps2"], start=True, stop=False)
                cx.mm(ps[2][:, HS[h]], Am[:, h, 128:256], Zb[:, h, :], ["Am%d" % h, "Zb"], ["ps2"], start=False, stop=False)
                cx.mm(ps[2][:, HS[h]], Am[:, h, 384:512], vbf[:, HS[h]], ["Am%d" % h, "vbf"], ["ps2"], start=False, stop=True)
            for h in range(4):
                cx.mm(ps[3][0:64, HS[h]], bhat[:, HS[h]], Zb[:, h, :], ["bhat", "Zb"], ["ps3"], start=True, stop=False)
                cx.mm(ps[3][0:64, HS[h]], khat[:, HS[h]], vbf[:, HS[h]], ["khat", "vbf"], ["ps3"], start=False, stop=True)
            cx.tt("dve", ST[:], ST[:], gl[:].unsqueeze(2).to_broadcast([64, 4, 64]), ALU.mult, ["ST", "gl"], ["ST"])
            cx.tt("dve", ST[:].rearrange("p h v -> p (h v)"), ST[:].rearrange("p h v -> p (h v)"), ps[3][0:64, 0:HC], ALU.add,
                  ["ST", "ps3"], ["ST"])
            cx.cp("act", STb[:], ST[:], ["ST"], ["STb"])
            cx.cp("act", y_sb[:], ps[2][:, 0:HC], ["ps2"], ["y_sb"])

        def epilogue(b, ti, pp):
            v_sb, bs4, gate, y_sb = DB[pp]
            tok0 = b * BLK + ti * 128
            pass
            cx.red("dve", s14[:], v3(y_sb[:]), ["y_sb"], ["s14"])
            cx.ts("dve", s14[:], s14[:], -1.0 / 64, None, ALU.mult, None, ["s14"], ["s14"])
            yield
            cx.tt("dve", v3(yc[:]), v3(y_sb[:]), bc4(s14[:]), ALU.add, ["y_sb", "s14"], ["yc"])
            cx.tt("pool", e1[:], yc[:], yc[:], ALU.mult, ["yc"], ["e1"])
            yield
            cx.red("dve", s24[:], v3(e1[:]), ["e1"], ["s24"])
            cx.act(s24[:], s24[:], AF.Sqrt, ["s24"], ["s24"], bias=64e-5, scale=1.0 / 64)
            yield
            cx.recip(s24[:], s24[:], ["s24"], ["s24"])
            cx.tt("dve", v3(yc[:]), v3(yc[:]), bc4(s24[:]), ALU.mult, ["yc", "s24"], ["yc"])
            yield
            cx.tt("pool", yc[:], yc[:], lnw_b, ALU.mult, ["yc", "vb"], ["yc"])
            cx.tt("pool", yc[:], yc[:], lnb_b, ALU.add, ["yc", "vb"], ["yc"])
            yield
            cx.tt("dve", v3(e2[:]), v3(v_sb[:]), bc4(bs4[:]), ALU.mult, ["v_sb", "bs4"], ["e2"])
            cx.tt("pool", yc[:], yc[:], e2[:], ALU.add, ["yc", "e2"], ["yc"])
            yield
            cx.tt("dve", o_bf[:], yc[:], gate[:], ALU.mult, ["yc", "gate"], ["o_bf"])
            pv = ps[1][:].bitcast(BF16)
            for j in range(2):
                cx.tr(pv[:, 512 + j * 128:512 + (j + 1) * 128], o_bf[:, j * 128:(j + 1) * 128], ident_bf[:],
                      ["o_bf", "ident_bf"], ["ps1o"])
            cx.cp("act", oT_sb[:], pv[:, 512:768].rearrange("p (j t) -> p j t", j=2), ["ps1o"], ["oT_sb"])
            yield
            cx.dma(oT_out.rearrange("(j p) s -> p j s", p=128)[:, :, tok0:tok0 + 128], oT_sb[:], ["oT_sb"], [], "oTo", is_out=True)

            yield

        def drive(gens):
            live = list(gens)
            while live:
                nxt = []
                for (g, p) in live:
                    cx.S.parity = p
                    try:
                        next(g)
                        nxt.append((g, p))
                    except StopIteration:
                        pass
                live = nxt
            cx.S.parity = None

        pending = None
        t = 0
        for b in range(NB):
            frontend_block(cx, fe, xT, oT, xT_out, b, has_prev, BLK)
            xh = fe["xh"]
            for j, (lo, hi) in enumerate(((0, 64), (64, 128), (128, 160))):
                m = hi - lo
                bank = 5 + j
                for kc in range(NCH):
                    cx.mm(ps[bank][0:m, :], L1c[:, kc, lo:hi], xh[:, kc, 1:BLK + 1], ["L1c", "fe_xh"], ["ps%d" % bank],
                          start=(kc == 0), stop=False)
                    cx.mm(ps[bank][0:m, :], L1p[:, kc, lo:hi], xh[:, kc, 0:BLK], ["L1p", "fe_xh"], ["ps%d" % bank],
                          start=False, stop=(kc == NCH - 1))
                if j == 0:
                    cx.act(h1[0:m, j, :], ps[bank][0:m, :], AF.Tanh, ["ps%d" % bank], ["h1"])
                else:
                    cx.cp("dve", h1[0:m, j, :], ps[bank][0:m, :], ["ps%d" % bank], ["h1"])
            for ti in range(BLK // 128):
                pp = t % 2
                cx.S.parity = pp
                tile_pre(b, ti, pp)
                gens = [(inverse(pp), pp)]
                if pending is not None:
                    gens.append((epilogue(*pending), pending[2]))
                drive(gens)
                cx.S.parity = pp
                tile_state(b, ti, pp)
                cx.S.parity = None
                pending = (b, ti, pp)
                t += 1
        drive([(epilogue(*pending), pending[2])])
        cx.done()
    return nc


def _consts():
    c = np.zeros((128, 1152), np.float32)
    i = np.arange(128)
    c[:, 0:128] = np.eye(128)
    c[:, 128:256] = (i[:, None] <= i[None, :])
    c[:, 256:384] = (i[:, None] > i[None, :])
    su = (i[:, None] < i[None, :]).astype(np.float32)
    iu = (i[:, None] <= i[None, :]).astype(np.float32)
    c[:, 384:512] = su
    c[:, 512:640] = iu
    c[:, 640:768] = su
    c[:, 768:896] = iu
    c[:, 896:1024] = (i[:, None] > i[None, :])
    c[:, 1024] = 1.0
    return c


def fm8(v):
    return np.ascontiguousarray(np.asarray(v).reshape(NCH, 128).T)


def rwkv_in_map(inp, layer, g, xT_b, oT_b=None, vfirst_c=None):
    cs = slice(g * HC, (g + 1) * HC)
    m = {}
    m["xT"] = xT_b
    m["win"] = np.ascontiguousarray(inp["a_w_in"][layer][:, :, cs])
    gmu = np.zeros((128, 7, NCH), np.float32)
    gmu[:, 0, :] = fm8(inp["a_norm"][layer])
    for p in range(6):
        gmu[:, 1 + p, :] = fm8(inp["a_mu"][layer][p])
    m["gmu"] = gmu
    l1 = np.zeros((D, 160), np.float32)
    l1[:, 0:64] = inp["a_w1"][layer]
    l1[:, 64:128] = inp["a_a1"][layer]
    l2 = np.zeros((64, 3, HC), np.float32)
    l2[:, 0, :] = inp["a_w2"][layer][:, cs]
    l2[:, 1, :] = inp["a_a2"][layer][:, cs]
    vecs = np.zeros((8, HC), np.float32)
    vecs[0] = inp["a_w0"][layer][cs]
    vecs[1] = inp["a_a0"][layer][cs]
    if layer > 0:
        l1[:, 128:160] = inp["a_v1"][layer - 1]
        l2[0:32, 2, :] = inp["a_v2"][layer - 1][:, cs]
        vecs[2] = inp["a_v0"][layer - 1][cs]
    vecs[3] = inp["a_k_k"][layer][cs]
    vecs[4] = inp["a_k_a"][layer][cs]
    vecs[5] = inp["a_r_k"][layer].reshape(-1)[cs]
    vecs[6] = inp["a_ln_w"][layer][cs]
    vecs[7] = inp["a_ln_b"][layer][cs]
    m["l1w"], m["l2w"], m["vecs"] = l1, l2, vecs
    m["cst"] = _consts()
    if layer > 0:
        m["oT"] = oT_b
        m["wout"] = np.ascontiguousarray(inp["a_w_out"][layer - 1])
        m["vfirst"] = vfirst_c
    return m


def build_xupd(NT, final):
    BLK = 512
    nc = bass.Bass("TRN2", target_bir_lowering=False)
    with ExitStack() as es:
        cx = Ctx(nc, es)
        xT = cx.dram("xT", [D, NT], F32, "ExternalInput")
        oT = cx.dram("oT", [D, NT], BF16, "ExternalInput")
        wout = cx.dram("wout", [D, D], F32, "ExternalInput")
        xT_out = cx.dram("xT_out", [D, NT], F32, "ExternalOutput")
        fe = frontend_alloc(cx, True, BLK)
        stage = cx.sb("stage", [128, 2, 512], F32)
        wv = wout.rearrange("(c p) d -> p c d", p=128)
        for kc in range(NCH):
            cx.dma(stage[:], wv[:, kc, :].rearrange("p (a n) -> p a n", a=2), [], ["stage"], "stage")
            cx.cp("dve", fe["wo"][:, kc, :].rearrange("p (a n) -> p a n", a=2), stage[:], ["stage"], ["fe_wo"])
        if final:
            gfin = cx.dram("gfin", [128, NCH], F32, "ExternalInput")
            g_sb = cx.sb("g_sb", [128, NCH], F32)
            cx.dma(g_sb[:], gfin[:, :], [], ["g_sb"], "su1")
            yT = cx.sb("yT", [128, NCH, BLK], F32)
        for b in range(NT // BLK):
            frontend_block(cx, fe, xT, oT, xT_out, b, True, BLK, store_x=not final, norm=final)
            if final:
                for dc in range(NCH):
                    cx.stt("dve", yT[:, dc, :], fe["xT"][:, dc, :], g_sb[:, dc:dc + 1], fe["rstd"][:], ALU.mult, ALU.mult,
                           ["fe_xT", "g_sb", "fe_rstd"], ["yT"])
                ovx = xT_out.rearrange("(c p) s -> p c s", p=128)
                cx.dma(ovx[:, :, b * BLK:(b + 1) * BLK], yT[:], ["yT"], [], "yo", is_out=True)
        cx.done()
    return nc


LAM_INIT = {2: 0.8 - 0.6 * float(np.exp(-0.3 * 2)), 3: 0.8 - 0.6 * float(np.exp(-0.3 * 3))}


def build_attn(SL, mode):
    BLK = 512
    NB = SL // BLK
    NKT = SL // 128
    kv = (mode == "kv")
    nc = bass.Bass("TRN2", target_bir_lowering=False)
    with ExitStack() as es:
        cx = Ctx(nc, es)
        ps = cx.ps
        xT = cx.dram("xT", [D, SL], F32, "ExternalInput")
        pos = cx.dram("pos", [1, SL], I32, "ExternalInput")
        wq = cx.dram("wq", [D, 512], F32, "ExternalInput")
        gn = cx.dram("gn", [128, NCH], F32, "ExternalInput")
        invf_d = cx.dram("invf", [128, 1], F32, "ExternalInput")
        if kv:
            KT_io = cx.dram("KT_out", [2, 128, SL], BF16, "ExternalOutput")
            V_io = cx.dram("V_out", [SL, 256], BF16, "ExternalOutput")
        else:
            lam_in = cx.dram("lam_in", [4, 64], F32, "ExternalInput")
            subw = cx.dram("subw", [128, 1], F32, "ExternalInput")
            mask_d = cx.dram("mask", [128, 4, 512], BF16, "ExternalInput")
            KT_io = cx.dram("KT_in", [2, 128, SL], BF16, "ExternalInput")
            V_io = cx.dram("V_in", [SL, 256], BF16, "ExternalInput")
            oT_out = cx.dram("oT_out", [256, SL], BF16, "ExternalOutput")
        fe = frontend_alloc(cx, False, BLK)
        Wq = cx.sb("Wq", [128, NCH, 512], BF16)
        Wqr = cx.sb("Wqr", [128, NCH, 256], BF16)
        gsb = cx.sb("gsb", [128, NCH], F32)
        invf = cx.sb("invf_sb", [128, 1], F32)
        posi = cx.sb("posi", [128, BLK], I32)
        ang = cx.sb("ang", [128, BLK], F32)
        Ct = cx.sb("Ct", [128, BLK], F32)
        St = cx.sb("St", [128, BLK], F32)
        A1 = cx.sb("A1", [128, BLK], F32)
        A2 = cx.sb("A2", [128, BLK], F32)
        sqa = cx.sb("sqa", [128, BLK], F32)
        if kv:
            KTb = cx.sb("KTb", [128, 2, BLK], BF16)
            VAb = cx.sb("VAb", [128, 4, 256], BF16)
        else:
            KT = cx.sb("KT", [128, 2, SL], BF16)
            VA = cx.sb("VA", [128, NKT, 256], BF16)
            maskb = cx.sb("maskb", [128, 4, 512], BF16)
            ones_bf = cx.sb("ones_bf", [128, 128], BF16)
            lam_t = cx.sb("lam_t", [128, 4, 64], F32)
            lam_s = cx.sb("lam_s", [128, 4], F32)
            nlam = cx.sb("nlam", [128, 1], F32)
            sw = cx.sb("sw", [128, 1], F32)
            Qp = cx.sb("Qp", [128, 2, 2, BLK], BF16)
            gT = cx.sb("gT", [128, 2, BLK], BF16)
            pt = [cx.sb("pt%d" % i, [128, BLK], BF16) for i in range(4)]
            rl = [cx.sb("rl%d" % i, [128, BLK], F32) for i in range(2)]
            ob = cx.sb("ob", [128, BLK], BF16)
        stage = fe["xT"]
        cx.dma(gsb[:], gn[:, :], [], ["gsb"], "su1")
        cx.dma(invf[:], invf_d[:, :], [], ["invf"], "su0")
        if not kv:
            cx.dma(sw[:], subw[:, :], [], ["sw"], "su2")
            cx.dma(maskb[:], mask_d[:, :, :], [], ["maskb"], "su4")
            for i in range(4):
                cx.dma(lam_t[:, i, :], lam_in[i:i + 1, :].partition_broadcast(128), [], ["lam_t"], "su3")
            cx.memset("pool", ones_bf[:], 1.0, ["ones_bf"])
            cx.memset("pool", Qp[:], 0.0, ["Qp"])
            lam_init = LAM_INIT[mode]
            cx.tt("dve", lam_t[:, 0, :], lam_t[:, 0, :], lam_t[:, 1, :], ALU.mult, ["lam_t"], ["lam_t"])
            cx.tt("dve", lam_t[:, 2, :], lam_t[:, 2, :], lam_t[:, 3, :], ALU.mult, ["lam_t"], ["lam_t"])
            cx.red("dve", lam_s[:, 0:1], lam_t[:, 0, :], ["lam_t"], ["lam_s"])
            cx.red("dve", lam_s[:, 1:2], lam_t[:, 2, :], ["lam_t"], ["lam_s"])
            cx.act(lam_s[:, 0:2], lam_s[:, 0:2], AF.Exp, ["lam_s"], ["lam_s"])
            cx.tt("dve", lam_s[:, 2:3], lam_s[:, 1:2], lam_s[:, 0:1], ALU.subtract, ["lam_s"], ["lam_s"])
            cx.ts("dve", nlam[:], lam_s[:, 2:3], -lam_init, None, ALU.add, None, ["lam_s"], ["nlam"])
            cx.ts("dve", sw[:], sw[:], 1.0 - lam_init, None, ALU.mult, None, ["sw"], ["sw"])
        cx.dma(stage[:], wq.rearrange("(c p) n -> p c n", p=128), [], ["fe_xT"], "stage")
        cx.tt("dve", Wq[:], stage[:], gsb[:].unsqueeze(2).to_broadcast([128, NCH, 512]), ALU.mult, ["fe_xT", "gsb"], ["W"])
        cx.memset("pool", Wqr[:], 0.0, ["Wr"])
        for mp in range(4):
            c = mp * 64
            cx.ts("dve", Wqr[:, :, c:c + 8], Wq[:, :, c + 8:c + 16], -1.0, None, ALU.mult, None, ["W"], ["Wr"])
            cx.cp("dve", Wqr[:, :, c + 8:c + 16], Wq[:, :, c:c + 8], ["W"], ["Wr"])
        if not kv:
            for j in range(2):
                for q4 in range(4):
                    sl = slice(q4 * (SL // 4), (q4 + 1) * (SL // 4))
                    cx.dma(KT[:, j, sl], KT_io[j, :, sl], [], ["KT"], "ktl")
            vv = V_io.rearrange("(t p) n -> p t n", p=128)
            for q4 in range(4):
                sl = slice(q4 * (NKT // 4), (q4 + 1) * (NKT // 4))
                cx.dma(VA[:, sl, :], vv[:, sl, :], [], ["VA"], "val")

        M = 12582912.0
        TWO_PI = float(2 * np.pi)

        def sin_table(dst, dkey, shift):
            cx.ts("dve", A1[:], ang[:], shift, None, ALU.add, None, ["ang"], ["A1"])
            cx.ts("dve", A2[:], A1[:], 1.0 / TWO_PI, M, ALU.mult, ALU.add, ["A1"], ["A2"])
            cx.ts("dve", A2[:], A2[:], M, None, ALU.subtract, None, ["A2"], ["A2"])
            cx.stt("dve", A1[:], A2[:], -TWO_PI, A1[:], ALU.mult, ALU.add, ["A2", "A1"], ["A1"])
            cx.ts("dve", A1[:], A1[:], -3.1415925, 3.1415925, ALU.max, ALU.min, ["A1"], ["A1"])
            cx.act(dst[:], A1[:], AF.Sin, ["A1"], [dkey])

        def proj_fm(W, wkey, c0, bank):
            for kc in range(NCH):
                cx.mm(ps[bank][:], W[:, kc, c0:c0 + 128], fe["xh"][:, kc, 1:BLK + 1], [wkey, "fe_xh"], ["ps%d" % bank],
                      start=(kc == 0), stop=(kc == NCH - 1))

        def roped(j):
            proj_fm(Wq, "W", j * 128, 6)
            proj_fm(Wqr, "Wr", j * 128, 7)
            cx.tt("dve", A1[:], ps[6][:], Ct[:], ALU.mult, ["ps6", "Ct"], ["A1"])
            cx.tt("dve", A2[:], ps[7][:], St[:], ALU.mult, ["ps7", "St"], ["A2"])

        for b in range(NB):
            t0 = b * BLK
            frontend_block(cx, fe, xT, None, None, b, False, BLK)
            cx.dma(posi[:], pos[0:1, t0:t0 + BLK].partition_broadcast(128), [], ["posi"], "posl")
            cx.cp("dve", sqa[:], posi[:], ["posi"], ["sqa"])
            cx.ts("dve", ang[:], sqa[:], invf[:, 0:1], None, ALU.mult, None, ["sqa", "invf"], ["ang"])
            sin_table(St, "St", 0.0)
            sin_table(Ct, "Ct", float(np.pi / 2))
            for j in range(2):
                roped(j)
                if kv:
                    cx.tt("pool", KTb[:, j, :], A1[:], A2[:], ALU.add, ["A1", "A2"], ["KTb"])
                    for ti in range(4):
                        for kc in range(NCH):
                            cx.mm(ps[5][:, 0:128], fe["xh"][:, kc, 1 + ti * 128:1 + (ti + 1) * 128], Wq[:, kc, 256 + j * 128:256 + (j + 1) * 128],
                                  ["fe_xh", "W"], ["ps5"], start=(kc == 0), stop=(kc == NCH - 1))
                        cx.cp("act", VAb[:, ti, j * 128:(j + 1) * 128], ps[5][:, 0:128], ["ps5"], ["VAb"])
                else:
                    cx.tt("pool", Qp[0:64, j, 0, :], A1[0:64, :], A2[0:64, :], ALU.add, ["A1", "A2"], ["Qp"])
                    cx.tt("pool", Qp[64:128, j, 1, :], A1[64:128, :], A2[64:128, :], ALU.add, ["A1", "A2"], ["Qp"])
                    proj_fm(Wq, "W", 256 + j * 128, 6)
                    cx.act(sqa[:], ps[6][:], AF.Sigmoid, ["ps6"], ["sqa"])
                    cx.tt("dve", gT[:, j, :], ps[6][:], sqa[:], ALU.mult, ["ps6", "sqa"], ["gT"])
            if kv:
                for j in range(2):
                    cx.dma(KT_io[j, :, t0:t0 + BLK], KTb[:, j, :], ["KTb"], [], "kto", is_out=True)
                vo = V_io.rearrange("(t p) n -> p t n", p=128)
                cx.dma(vo[:, b * 4:(b + 1) * 4, :], VAb[:], ["VAb"], [], "vao", is_out=True)
                continue
            for j in range(2):
                nkt = 4 * b + 4
                pairs = [(kt, m) for kt in range(nkt) for m in range(2)]
                NP = len(pairs)
                DEPTH = 3
                SCB = (0, 1, 6, 7)

                def emit_sc(i):
                    kt, m = pairs[i]
                    sbk = SCB[i % 4]
                    p_t = pt[i % 4]
                    pk = "pt%d" % (i % 4)
                    cx.mm(ps[sbk][:], KT[:, j, kt * 128:(kt + 1) * 128], Qp[:, j, m, :], ["KT", "Qp"], ["ps%d" % sbk])
                    cx.act(p_t[:], ps[sbk][:], AF.Exp, ["ps%d" % sbk], [pk], scale=0.125)
                    if kt >= 4 * b:
                        cx.tt("dve", p_t[:], p_t[:], maskb[:, kt - 4 * b, :], ALU.mult, [pk, "maskb"], [pk])

                def emit_pv(i):
                    kt, m = pairs[i]
                    p_t = pt[i % 4]
                    pk = "pt%d" % (i % 4)
                    cx.mm(ps[2 + m][:], VA[:, kt, j * 128:(j + 1) * 128], p_t[:], ["VA", pk], ["ps%d" % (2 + m)],
                          start=(kt == 0), stop=(kt == nkt - 1))
                    if m == 1:
                        cx.mm(ps[5][:], ones_bf[:], p_t[:], ["ones_bf", pk], ["ps5"], start=(kt == 0), stop=(kt == nkt - 1))
                    elif kt == 0:
                        cx.cp("dve", rl[0][:], p_t[:], [pk], ["rl0"])
                    else:
                        cx.tt("dve", rl[0][:], rl[0][:], p_t[:], ALU.add, ["rl0", pk], ["rl0"])

                for i in range(NP + DEPTH):
                    if i < NP:
                        emit_sc(i)
                    if i >= DEPTH:
                        emit_pv(i - DEPTH)
                cx.mm(ps[4][:], fe["ones"][:], rl[0][:], ["fe_ones", "rl0"], ["ps4"])
                cx.recip(rl[0][:], ps[4][:], ["ps4"], ["rl0"])
                cx.recip(rl[1][:], ps[5][:], ["ps5"], ["rl1"])
                cx.tt("dve", A1[:], ps[2][:], rl[0][:], ALU.mult, ["ps2", "rl0"], ["A1"])
                cx.tt("dve", A2[:], ps[3][:], rl[1][:], ALU.mult, ["ps3", "rl1"], ["A2"])
                cx.stt("dve", A1[:], A2[:], nlam[:, 0:1], A1[:], ALU.mult, ALU.add, ["A2", "nlam", "A1"], ["A1"])
                cx.tt("pool", sqa[:], A1[:], A1[:], ALU.mult, ["A1"], ["sqa"])
                cx.mm(ps[6][:], fe["ones"][:], sqa[:], ["fe_ones", "sqa"], ["ps6"])
                cx.act(sqa[:], ps[6][:], AF.Sqrt, ["ps6"], ["sqa"], bias=1e-5, scale=1.0 / 128)
                cx.recip(sqa[:], sqa[:], ["sqa"], ["sqa"])
                cx.stt("dve", A1[:], A1[:], sw[:, 0:1], sqa[:], ALU.mult, ALU.mult, ["A1", "sw", "sqa"], ["A1"])
                cx.tt("dve", ob[:], A1[:], gT[:, j, :], ALU.mult, ["A1", "gT"], ["ob"])
                cx.dma(oT_out[j * 128:(j + 1) * 128, t0:t0 + BLK], ob[:], ["ob"], [], "oTo", is_out=True)
        cx.done()
    return nc


def _attn_consts():
    i = np.arange(128)
    q = np.arange(512)
    mask = np.zeros((128, 4, 512), np.float32)
    for a in range(4):
        mask[:, a, :] = (q[None, :] >= (i[:, None] + 128 * a))
    invf = np.zeros((128, 1), np.float32)
    base = (500000.0 ** (-np.arange(0, 16, 2, dtype=np.float32) / np.float32(16))).astype(np.float32)
    for m in range(2):
        for d in range(16):
            invf[m * 64 + d, 0] = base[d % 8]
    return mask.astype(ml_dtypes.bfloat16), invf


def attn_in_map(inp, mode, g, xT_b, pos_b, KT_c=None, V_c=None):
    cs = slice(g * 256, (g + 1) * 256)
    cs2 = slice(1024 + g * 256, 1024 + (g + 1) * 256)
    mask, invf = _attn_consts()
    m = {"xT": xT_b, "pos": np.ascontiguousarray(pos_b.reshape(1, -1)), "invf": invf}
    if mode == "kv":
        w = inp["w_kv"]
        m["gn"] = fm8(inp["kv_norm"])
    else:
        j = mode - 2
        w = inp["b_w_in"][j]
        m["gn"] = fm8(inp["b_norm"][j])
        m["lam_in"] = np.stack([inp["b_lq1"][j], inp["b_lk1"][j], inp["b_lq2"][j], inp["b_lk2"][j]]).astype(np.float32)
        m["subw"] = np.ascontiguousarray(inp["b_subln"][j].reshape(128, 1))
        m["mask"] = mask
        m["KT_in"] = KT_c
        m["V_in"] = V_c
    m["wq"] = np.ascontiguousarray(np.concatenate([w[:, cs], w[:, cs2]], axis=1))
    return m


def _run(nc, maps):
    return run_bass_kernel_spmd(nc, maps, core_ids=list(range(8))).results


def _xupd(xT, oT, wout, SL, final, gfin=None):
    NT = SL // 4
    nc = build_xupd(NT, final)
    maps = []
    for c in range(8):
        b, q = c // 4, c % 4
        m = {"xT": np.ascontiguousarray(xT[b][:, q * NT:(q + 1) * NT]),
             "oT": np.ascontiguousarray(oT[b][:, q * NT:(q + 1) * NT]),
             "wout": np.ascontiguousarray(wout)}
        if final:
            m["gfin"] = gfin
        maps.append(m)
    r = _run(nc, maps)
    return [np.concatenate([r[b * 4 + q]["xT_out"] for q in range(4)], axis=1) for b in range(2)]


def _forward(inp, SL):
    inp = {k: np.asarray(v) for k, v in inp.items()}
    xT = [np.ascontiguousarray(inp["x"][b].T) for b in range(2)]
    gather = lambda r: [np.concatenate([r[b * 4 + g]["oT_out"] for g in range(4)], axis=0) for b in range(2)]
    r = _run(build_rwkv(SL, 0), [rwkv_in_map(inp, 0, c % 4, xT[c // 4]) for c in range(8)])
    oT = gather(r)
    vf = [r[c]["vfirst_out"] for c in range(8)]
    r = _run(build_rwkv(SL, 1), [rwkv_in_map(inp, 1, c % 4, xT[c // 4], oT[c // 4], vf[c]) for c in range(8)])
    xT = [r[b * 4]["xT_out"] for b in range(2)]
    oT = gather(r)
    xT = _xupd(xT, oT, inp["a_w_out"][1], SL, False)
    r = _run(build_attn(SL, "kv"), [attn_in_map(inp, "kv", c % 4, xT[c // 4], inp["positions"][c // 4]) for c in range(8)])
    KT = [r[c]["KT_out"] for c in range(8)]
    V = [r[c]["V_out"] for c in range(8)]
    for j in range(2):
        r = _run(build_attn(SL, 2 + j),
                 [attn_in_map(inp, 2 + j, c % 4, xT[c // 4], inp["positions"][c // 4], KT[c], V[c]) for c in range(8)])
        oT = gather(r)
        if j == 0:
            xT = _xupd(xT, oT, inp["b_w_out"][0], SL, False)
        else:
            xT = _xupd(xT, oT, inp["b_w_out"][1], SL, True, fm8(inp["final_norm"]))
    return np.stack([np.ascontiguousarray(xT[b].T) for b in range(2)]).astype(np.float32)


def kernel(**inputs):
    return _forward(inputs, 16384)
```

```python
import numpy as np
import ml_dtypes
from contextlib import ExitStack
import concourse.bass as bass
import concourse.mybir as mybir
from concourse.bass_utils import run_bass_kernel_spmd

F32 = mybir.dt.float32
BF16 = mybir.dt.bfloat16
I32 = mybir.dt.int32
AF = mybir.ActivationFunctionType
ALU = mybir.AluOpType
AX = mybir.AxisListType

D = 1024
NCH = 8
ENGS = ["pe", "act", "dve", "pool", "sp"]


class Sched:
    def __init__(self, nc, es):
        self.nc = nc
        self.es = es
        self.streams = {e: [] for e in ENGS}
        self.sem = {e: es.enter_context(nc.semaphore("c_" + e)) for e in ENGS}
        self.count = {e: 0 for e in ENGS}
        self.waited = {e: {} for e in ENGS}
        self.last_w = {}
        self.readers = {}
        self.dsem = {}
        self.out_tokens = []
        self.sub = {}
        self.parity = None
        self.dbkeys = set()
        self.dbkeys3 = set()

    def dma_sem(self, name):
        if name not in self.dsem:
            self.dsem[name] = [self.es.enter_context(self.nc.semaphore("d_" + name)), 0]
        return name

    def op(self, eng, fn, reads=(), writes=(), dma=None, is_out=False):
        import os as _os
        self.nops = getattr(self, "nops", 0) + 1
        if self.nops > int(_os.environ.get("OPLIMIT", "100000000")):
            return None
        if self.parity is not None:
            p2, p3 = self.parity
            km = lambda k: (k + "_t%d" % p3) if k in self.dbkeys3 else ((k + "_%d" % p2) if k in self.dbkeys else k)
            reads = [km(k) for k in reads]
            writes = [km(k) for k in writes]
        nk = lambda k: k[:3] if (k.startswith("ps") and len(k) > 3 and k[2].isdigit()) else k
        reads = [nk(k) for k in reads]
        writes = [nk(k) for k in writes]
        writes = writes + [k for k in reads if k.startswith("ps") and k[2].isdigit()]
        reads = [k for k in reads if not (k.startswith("ps") and k[2].isdigit())]
        deps = []

        def bank_of(k):
            return k[:3] if (k.startswith("ps") and len(k) > 3 and k[2].isdigit()) else None

        def is_bank(k):
            return k.startswith("ps") and len(k) == 3 and k[2].isdigit()

        def rdep(k):
            if k in self.last_w:
                deps.append(self.last_w[k])

        def wdep(k):
            if k in self.last_w:
                deps.append(self.last_w[k])
            deps.extend(self.readers.get(k, {}).values())

        for k in reads:
            rdep(k)
            bk = bank_of(k)
            if bk:
                self.sub.setdefault(bk, set()).add(k)
                rdep(bk)
            if is_bank(k):
                for s_ in self.sub.get(k, ()):
                    rdep(s_)
        for k in writes:
            wdep(k)
            bk = bank_of(k)
            if bk:
                self.sub.setdefault(bk, set()).add(k)
                wdep(bk)
            if is_bank(k):
                for s_ in self.sub.get(k, ()):
                    wdep(s_)
        waits = []
        wd = self.waited[eng]
        for (sid, sem, val) in deps:
            if sid == "pe" and eng == "pe":
                continue
            if wd.get(sid, 0) < val:
                wd[sid] = val
                waits.append((sem, val))
        if dma is None:
            self.count[eng] += 1
            tok = (eng, self.sem[eng], self.count[eng])
            inc = (self.sem[eng], 1)
        else:
            self.dma_sem(dma)
            d = self.dsem[dma]
            d[1] += 16
            tok = ("d_" + dma, d[0], d[1])
            inc = (d[0], 16)
        self.streams[eng].append((waits, fn, inc))
        for k in reads:
            self.readers.setdefault(k, {})[tok[0]] = tok
        for k in writes:
            self.last_w[k] = tok
            self.readers[k] = {}
        if is_out:
            self.out_tokens.append(tok)
        return tok

    def finish(self):
        final = {}
        for (sid, sem, val) in self.out_tokens:
            if final.get(sid, (None, 0))[1] < val:
                final[sid] = (sem, val)
        self.streams["sp"].append((list(final.values()), None, None))

    def emit(self, block):
        def run(eng_name):
            def body(eng):
                for waits, fn, inc in self.streams[eng_name]:
                    for (s, v) in waits:
                        eng.wait_ge(s, v)
                    if fn is not None:
                        fn(eng).then_inc(inc[0], inc[1])
            return body
        block.tensor(run("pe"))
        block.scalar(run("act"))
        block.vector(run("dve"))
        block.gpsimd(run("pool"))
        block.sync(run("sp"))


class Ctx:
    def __init__(self, nc, es):
        self.nc = nc
        self.es = es
        self.S = Sched(nc, es)
        self.psall = es.enter_context(nc.psum_tensor("psall", [128, 8, 512], F32))
        self.ps = [self.psall[:, i, :] for i in range(8)]

    def sb(self, name, shape, dt):
        return self.es.enter_context(self.nc.sbuf_tensor(name, list(shape), dt))

    def dram(self, name, shape, dt, kind):
        return self.nc.dram_tensor(name, list(shape), dt, kind=kind).ap()

    def dma(self, out, in_, r, w, sem, eng="sp", is_out=False):
        return self.S.op(eng, lambda e: e.dma_start(out=out, in_=in_), reads=r, writes=w, dma=sem, is_out=is_out)

    def mm(self, out, lhsT, rhs, r, w, start=True, stop=True):
        return self.S.op("pe", lambda e: e.matmul(out, lhsT=lhsT, rhs=rhs, start=start, stop=stop), reads=r, writes=w)

    def tr(self, out, in_, ident, r, w):
        return self.S.op("pe", lambda e: e.transpose(out, in_, ident), reads=r, writes=w)

    def act(self, out, in_, func, r, w, bias=None, scale=None, accum_out=None):
        kw = {}
        if bias is not None:
            kw["bias"] = bias
        if scale is not None:
            kw["scale"] = scale
        if accum_out is not None:
            kw["accum_out"] = accum_out
        return self.S.op("act", lambda e: e.activation(out=out, in_=in_, func=func, **kw), reads=r, writes=w)

    def tt(self, eng, out, in0, in1, op, r, w):
        return self.S.op(eng, lambda e: e.tensor_tensor(out=out, in0=in0, in1=in1, op=op), reads=r, writes=w)

    def ts(self, eng, out, in0, s1, s2, op0, op1, r, w):
        if op1 is None:
            return self.S.op(eng, lambda e: e.tensor_scalar(out=out, in0=in0, scalar1=s1, scalar2=None, op0=op0), reads=r, writes=w)
        return self.S.op(eng, lambda e: e.tensor_scalar(out=out, in0=in0, scalar1=s1, scalar2=s2, op0=op0, op1=op1), reads=r, writes=w)

    def stt(self, eng, out, in0, scalar, in1, op0, op1, r, w):
        return self.S.op(eng, lambda e: e.scalar_tensor_tensor(out=out, in0=in0, scalar=scalar, in1=in1, op0=op0, op1=op1), reads=r, writes=w)

    def cp(self, eng, out, in_, r, w):
        if eng == "act":
            return self.S.op("act", lambda e: e.copy(out=out, in_=in_), reads=r, writes=w)
        return self.S.op(eng, lambda e: e.tensor_copy(out=out, in_=in_), reads=r, writes=w)

    def red(self, eng, out, in_, r, w):
        return self.S.op(eng, lambda e: e.tensor_reduce(out=out, in_=in_, axis=AX.X, op=ALU.add), reads=r, writes=w)

    def recip(self, out, in_, r, w):
        return self.S.op("dve", lambda e: e.reciprocal(out=out, in_=in_), reads=r, writes=w)

    def memset(self, eng, ap, val, w):
        return self.S.op(eng, lambda e: e.memset(ap, val), writes=w)

    def done(self):
        self.S.finish()
        with self.nc.Block() as block:
            self.S.emit(block)


def frontend_alloc(cx, has_prev, BLK=512):
    fe = {}
    fe["xT"] = cx.sb("fe_xT", [128, NCH, BLK], F32)
    fe["sq"] = cx.sb("fe_sq", [128, 2, BLK], F32)
    fe["xh"] = cx.sb("fe_xh", [128, NCH, BLK + 1], BF16)
    fe["rstd"] = cx.sb("fe_rstd", [128, BLK], F32)
    fe["ones"] = cx.sb("fe_ones", [128, 128], F32)
    cx.memset("pool", fe["ones"][:], 1.0, ["fe_ones"])
    cx.memset("pool", fe["xh"][:], 0.0, ["fe_xh"])
    if has_prev:
        fe["oT"] = cx.sb("fe_oT", [128, NCH, BLK], BF16)
        fe["wo"] = cx.sb("fe_wo", [128, NCH, D], BF16)
    return fe


def frontend_load_wout(cx, fe, wout_ap, stage):
    for half in range(2):
        cx.dma(stage[:, 0:4, :], wout_ap.rearrange("(c p) d -> p c d", p=128)[:, half * 4:(half + 1) * 4, :],
               [], ["stage"], "stage")
        cx.cp("dve", fe["wo"][:, half * 4:(half + 1) * 4, :], stage[:, 0:4, :], ["stage"], ["fe_wo"])


def frontend_block(cx, fe, xT_ap, oT_ap, xT_out_ap, b, has_prev, BLK=512, store_x=True, norm=True, sb=4):
    t0 = b * BLK
    xv = xT_ap.rearrange("(c p) s -> p c s", p=128)
    cx.dma(fe["xT"][:], xv[:, :, t0:t0 + BLK], [], ["fe_xT"], "fe_x")
    if has_prev:
        ov = oT_ap.rearrange("(c p) s -> p c s", p=128)
        cx.dma(fe["oT"][:], ov[:, :, t0:t0 + BLK], [], ["fe_oT"], "fe_o")
        for dc in range(NCH):
            bank = dc % 4
            for kc in range(NCH):
                cx.mm(cx.ps[bank][:], fe["wo"][:, kc, dc * 128:(dc + 1) * 128], fe["oT"][:, kc, :],
                      ["fe_wo", "fe_oT"], ["ps%d" % bank], start=(kc == 0), stop=(kc == NCH - 1))
            cx.tt("dve", fe["xT"][:, dc, :], fe["xT"][:, dc, :], cx.ps[bank][:], ALU.add,
                  ["fe_xT", "ps%d" % bank], ["fe_xT"])
        if store_x:
            ovx = xT_out_ap.rearrange("(c p) s -> p c s", p=128)
            cx.dma(ovx[:, :, t0:t0 + BLK], fe["xT"][:], ["fe_xT"], [], "fe_xs", is_out=True)
    for dc in range(NCH):
        cx.act(fe["sq"][:, dc % 2, :], fe["xT"][:, dc, :], AF.Square, ["fe_xT"], ["fe_sq%d" % (dc % 2)])
        cx.mm(cx.ps[sb][:], fe["ones"][:], fe["sq"][:, dc % 2, :], ["fe_ones", "fe_sq%d" % (dc % 2)], ["ps%d" % sb],
              start=(dc == 0), stop=(dc == NCH - 1))
    cx.act(fe["rstd"][:], cx.ps[sb][:], AF.Sqrt, ["ps%d" % sb], ["fe_rstd"], bias=1e-6, scale=1.0 / D)
    cx.recip(fe["rstd"][:], fe["rstd"][:], ["fe_rstd"], ["fe_rstd"])
    if b > 0:
        cx.cp("pool", fe["xh"][:, :, 0:1], fe["xh"][:, :, BLK:BLK + 1], ["fe_xh"], ["fe_xh"])
    for dc in range(NCH):
        eng = "dve" if dc % 2 == 0 else "pool"
        cx.tt(eng, fe["xh"][:, dc, 1:BLK + 1], fe["xT"][:, dc, :], fe["rstd"][:], ALU.mult,
              ["fe_xT", "fe_rstd"], ["fe_xh"])


C_DEC = 0.6065306597126334
HC = 256


def build_rwkv(SL, layer, STOP=99, prev=None):
    has_prev = (layer > 0) if prev is None else prev
    vres = layer > 0
    BLK = 512
    NB = SL // BLK
    nc = bass.Bass("TRN2", target_bir_lowering=False)
    with ExitStack() as es:
        cx = Ctx(nc, es)
        ps = cx.ps
        xT = cx.dram("xT", [D, SL], F32, "ExternalInput")
        win = cx.dram("win", [4, D, HC], F32, "ExternalInput")
        gmu = cx.dram("gmu", [128, 7, NCH], F32, "ExternalInput")
        l1w = cx.dram("l1w", [D, 160], F32, "ExternalInput")
        l2w = cx.dram("l2w", [64, 3, HC], F32, "ExternalInput")
        vecs = cx.dram("vecs", [8, HC], F32, "ExternalInput")
        cst = cx.dram("cst", [128, 1152], F32, "ExternalInput")
        oT_out = cx.dram("oT_out", [HC, SL], BF16, "ExternalOutput")
        oT = wout = xT_out = None
        if has_prev:
            oT = cx.dram("oT", [D, SL], BF16, "ExternalInput")
            wout = cx.dram("wout", [D, D], F32, "ExternalInput")
            xT_out = cx.dram("xT_out", [D, SL], F32, "ExternalOutput")
        if vres:
            vfirst = cx.dram("vfirst", [SL, HC], F32, "ExternalInput")
        else:
            vfirst_out = cx.dram("vfirst_out", [SL, HC], F32, "ExternalOutput")
        fe = frontend_alloc(cx, has_prev, BLK)
        stage = cx.sb("stage", [128, NCH, HC * 2], F32)
        csb = cx.sb("csb", [128, 1152], F32)
        ident_bf = cx.sb("ident_bf", [128, 128], BF16)
        gm = cx.sb("gm", [128, 7, NCH], F32)
        coef = cx.sb("coef", [128, 12, NCH], F32)
        Wc = cx.sb("Wc", [128, NCH, 4 * HC], BF16)
        Wp = cx.sb("Wp", [128, NCH, 4 * HC], BF16)
        L1c = cx.sb("L1c", [128, NCH, 160], BF16)
        L1p = cx.sb("L1p", [128, NCH, 160], BF16)
        L2 = cx.sb("L2", [64, 3, HC], BF16)
        L2f = cx.sb("L2f", [64, 3, HC], F32)
        vb = cx.sb("vb", [128, 8, HC], F32)
        h1 = cx.sb("h1", [64, 3, BLK], BF16)
        ST = cx.sb("ST", [64, 4, 64], F32)
        STb = cx.sb("STb", [64, 4, 64], BF16)
        gl = cx.sb("gl", [64, 4], F32)

        def T(name, dt=F32, w=HC):
            return cx.sb(name, [128, w], dt)
        r_sb, k_sb, v_sb, gate, sg, a_sb = T("r_sb"), T("k_sb"), T("v_sb"), T("gate"), T("sg"), T("a_sb")
        t1, t2, t3, kkn, kmod, bvec = T("t1"), T("t2"), T("t3"), T("kkn"), T("kmod"), T("bvec")
        epos, eneg, eprev, ehat = T("epos"), T("eneg"), T("eprev"), T("ehat")
        ss4, bs4, s14, s24 = T("ss4", w=4), T("bs4", w=4), T("s14", w=4), T("s24", w=4)
        tmT = cx.sb("tmT", [128, 4, HC], BF16)
        khat, bhat, vbf = T("khat", BF16), T("bhat", BF16), T("vbf", BF16)
        fm = cx.sb("fm", [64, 4, 4, 128], BF16)
        Am = cx.sb("Am", [128, 4, 512], BF16)
        Nf = cx.sb("Nf", [128, 4, 2, 128], BF16)
        NTf = cx.sb("NTf", [128, 4, 2, 128], BF16)
        Tf = cx.sb("Tf", [128, 4, 2, 128], BF16)
        Wf = cx.sb("Wf", [128, 4, 64], BF16)
        Zb = cx.sb("Zb", [128, 4, 64], BF16)
        y_sb, yc, o_bf = T("y_sb"), T("yc"), T("o_bf", BF16)
        oT_sb = cx.sb("oT_sb", [128, 2, 128], BF16)
        vf_sb = T("vf_sb")

        cx.dma(csb[:], cst[:, :], [], ["csb"], "su0")
        cx.dma(gm[:], gmu[:, :, :], [], ["gm"], "su1")
        cx.dma(L2f[:], l2w[:, :, :], [], ["L2f"], "su2")
        for i in range(8):
            cx.dma(vb[:, i, :], vecs[i:i + 1, :].partition_broadcast(128), [], ["vb"], "su3")
        ident_f = csb[:, 0:128]
        tri = csb[:, 128:256]
        sfx = csb[:, 256:384]
        mask4 = csb[:, 384:896]
        maskNT = csb[:, 896:1024]
        ones_col = csb[:, 1024:1025]
        cx.cp("dve", ident_bf[:], ident_f, ["csb"], ["ident_bf"])
        cx.cp("dve", L2[:], L2f[:], ["L2f"], ["L2"])
        for p in range(6):
            cx.tt("dve", coef[:, 6 + p, :], gm[:, 0, :], gm[:, 1 + p, :], ALU.mult, ["gm"], ["coef"])
            cx.tt("dve", coef[:, p, :], gm[:, 0, :], coef[:, 6 + p, :], ALU.subtract, ["gm", "coef"], ["coef"])
        for p in range(4):
            cx.dma(stage[:, :, 0:HC], win[p].rearrange("(c p) n -> p c n", p=128), [], ["stage"], "stage")
            cx.tt("dve", Wc[:, :, p * HC:(p + 1) * HC], stage[:, :, 0:HC],
                  coef[:, p, :].unsqueeze(2).to_broadcast([128, NCH, HC]), ALU.mult, ["stage", "coef"], ["Wc"])
            cx.tt("pool", Wp[:, :, p * HC:(p + 1) * HC], stage[:, :, 0:HC],
                  coef[:, 6 + p, :].unsqueeze(2).to_broadcast([128, NCH, HC]), ALU.mult, ["stage", "coef"], ["Wp"])
        cx.dma(stage[:, :, 0:160], l1w.rearrange("(c p) n -> p c n", p=128), [], ["stage"], "stage")
        for (lo, hi, p) in ((0, 64, 4), (64, 128, 5), (128, 160, 2)):
            cx.tt("dve", L1c[:, :, lo:hi], stage[:, :, lo:hi],
                  coef[:, p, :].unsqueeze(2).to_broadcast([128, NCH, hi - lo]), ALU.mult, ["stage", "coef"], ["L1c"])
            cx.tt("pool", L1p[:, :, lo:hi], stage[:, :, lo:hi],
                  coef[:, 6 + p, :].unsqueeze(2).to_broadcast([128, NCH, hi - lo]), ALU.mult, ["stage", "coef"], ["L1p"])
        if has_prev:
            wv = wout.rearrange("(c p) d -> p c d", p=128)
            for kc in range(NCH):
                cx.dma(stage[:, 0:2, :], wv[:, kc, :].rearrange("p (a n) -> p a n", a=2), [], ["stage"], "stage")
                cx.cp("dve", fe["wo"][:, kc, :].rearrange("p (a n) -> p a n", a=2), stage[:, 0:2, :], ["stage"], ["fe_wo"])
        cx.memset("dve", ST[:], 0.0, ["ST"])
        cx.memset("dve", STb[:], 0.0, ["STb"])

        w0_b, a0_b, v0_b, kk_b, ka_b, rk_b, lnw_b, lnb_b = [vb[:, i, :] for i in range(8)]

        def bc4(ap4):
            return ap4.unsqueeze(2).to_broadcast([128, 4, 64])

        def v3(ap):
            return ap.rearrange("p (h n) -> p h n", h=4)

        e1, e2 = T("e1"), T("e2")
        DB2 = [(fm, Am, Nf, NTf, Tf, vbf, khat, bhat, gl),
               (cx.sb("fm_b", [64, 4, 4, 128], BF16), cx.sb("Am_b", [128, 4, 512], BF16),
                cx.sb("Nf_b", [128, 4, 2, 128], BF16), cx.sb("NTf_b", [128, 4, 2, 128], BF16),
                cx.sb("Tf_b", [128, 4, 2, 128], BF16), T("vbf_b", BF16), T("khat_b", BF16), T("bhat_b", BF16),
                cx.sb("gl_b", [64, 4], F32))]
        DB3 = [(v_sb, bs4, gate, y_sb)] + [(T("v_sb_%d" % i), T("bs4_%d" % i, w=4), T("gate_%d" % i), T("y_sb_%d" % i)) for i in (1, 2)]
        cx.S.dbkeys = set(["fm", "vbf", "khat", "bhat", "gl"] +
                          ["%s%d" % (n, h) for n in ("Am", "Nf", "NTf", "Tf") for h in range(4)])
        cx.S.dbkeys3 = set(["v_sb", "bs4", "gate", "y_sb"])
        xh = fe["xh"]

        def prep(b, ti, tt_):
            fm, Am, Nf, NTf, Tf, vbf, khat, bhat, gl, v_sb, bs4, gate, y_sb = DB2[tt_ % 2] + DB3[tt_ % 3]
            if ti == 0:
                frontend_block(cx, fe, xT, oT, xT_out, b, has_prev, BLK, sb=0)
                for j, (lo, hi) in enumerate(((0, 64), (64, 128), (128, 160))):
                    m = hi - lo
                    bank = 1 + j
                    for kc in range(NCH):
                        cx.mm(ps[bank][0:m, :], L1c[:, kc, lo:hi], xh[:, kc, 1:BLK + 1], ["L1c", "fe_xh"], ["ps%d" % bank],
                              start=(kc == 0), stop=False)
                        cx.mm(ps[bank][0:m, :], L1p[:, kc, lo:hi], xh[:, kc, 0:BLK], ["L1p", "fe_xh"], ["ps%d" % bank],
                              start=False, stop=(kc == NCH - 1))
                        if kc % 2 == 1:
                            yield
                    if j == 0:
                        cx.act(h1[0:m, j, :], ps[bank][0:m, :], AF.Tanh, ["ps%d" % bank], ["h1"])
                    else:
                        cx.cp("dve", h1[0:m, j, :], ps[bank][0:m, :], ["ps%d" % bank], ["h1"])
                yield
            tok0 = b * BLK + ti * 128
            c0 = 1 + ti * 128
            tok0 = b * BLK + ti * 128
            c0 = 1 + ti * 128
            for (bank, lo) in ((0, 0), (1, 512)):
                for kc in range(NCH):
                    cx.mm(ps[bank][:], xh[:, kc, c0:c0 + 128], Wc[:, kc, lo:lo + 512], ["fe_xh", "Wc"], ["ps%d" % bank],
                          start=(kc == 0), stop=False)
                    cx.mm(ps[bank][:], xh[:, kc, c0 - 1:c0 + 127], Wp[:, kc, lo:lo + 512], ["fe_xh", "Wp"], ["ps%d" % bank],
                          start=False, stop=(kc == NCH - 1))
                    yield
            hs = slice(ti * 128, (ti + 1) * 128)
            cx.mm(ps[2][:, 0:HC], h1[0:64, 0, hs], L2[0:64, 0, :], ["h1", "L2"], ["ps2"])
            cx.mm(ps[2][:, HC:2 * HC], h1[0:64, 1, hs], L2[0:64, 1, :], ["h1", "L2"], ["ps2"])
            if vres:
                cx.mm(ps[3][:, 0:HC], h1[0:32, 2, hs], L2[0:32, 2, :], ["h1", "L2"], ["ps3"])
            cx.cp("act", r_sb[:], ps[0][:, 0:HC], ["ps0"], ["r_sb"])
            yield
            cx.cp("dve", k_sb[:], ps[0][:, HC:2 * HC], ["ps0"], ["k_sb"])
            cx.cp("act", v_sb[:], ps[1][:, 0:HC], ["ps1"], ["v_sb"])
            cx.act(t1[:], ps[1][:, HC:2 * HC], AF.Sigmoid, ["ps1"], ["t1"])
            yield
            cx.tt("dve", gate[:], ps[1][:, HC:2 * HC], t1[:], ALU.mult, ["ps1", "t1"], ["gate"])
            cx.tt("dve", t2[:], ps[2][:, 0:HC], w0_b, ALU.add, ["ps2", "vb"], ["t2"])
            cx.act(sg[:], t2[:], AF.Sigmoid, ["t2"], ["sg"])
            yield
            cx.tt("dve", t3[:], ps[2][:, HC:2 * HC], a0_b, ALU.add, ["ps2", "vb"], ["t3"])
            cx.act(a_sb[:], t3[:], AF.Sigmoid, ["t3"], ["a_sb"])
            if vres:
                cx.dma(vf_sb[:], vfirst[tok0:tok0 + 128, :], [], ["vf_sb"], "vf")
                cx.tt("dve", t2[:], ps[3][:, 0:HC], v0_b, ALU.add, ["ps3", "vb"], ["t2"])
                cx.act(t2[:], t2[:], AF.Sigmoid, ["t2"], ["t2"])
                cx.tt("pool", t3[:], vf_sb[:], v_sb[:], ALU.subtract, ["vf_sb", "v_sb"], ["t3"])
                cx.tt("pool", t3[:], t3[:], t2[:], ALU.mult, ["t3", "t2"], ["t3"])
                cx.tt("pool", v_sb[:], v_sb[:], t3[:], ALU.add, ["v_sb", "t3"], ["v_sb"])
            else:
                cx.dma(vfirst_out[tok0:tok0 + 128, :], v_sb[:], ["v_sb"], [], "vfo", is_out=True)
            cx.mm(ps[2][:, 0:HC], tri, sg[:], ["csb", "sg"], ["ps2"])
            yield
            cx.mm(ps[2][:, HC:2 * HC], sfx, sg[:], ["csb", "sg"], ["ps2"])
            for h in range(4):
                cx.mm(ps[3][0:64, HC + h:HC + h + 1], sg[:, h * 64:(h + 1) * 64], ones_col, ["sg", "csb"], ["ps3b"])
            cx.act(gl[:], ps[3][0:64, HC:HC + 4], AF.Exp, ["ps3b"], ["gl"], scale=-C_DEC)
            cx.act(epos[:], ps[2][:, 0:HC], AF.Exp, ["ps2"], ["epos"], scale=-C_DEC)
            yield
            cx.act(eneg[:], ps[2][:, 0:HC], AF.Exp, ["ps2"], ["eneg"], scale=C_DEC)
            cx.tt("dve", t2[:], ps[2][:, 0:HC], sg[:], ALU.subtract, ["ps2", "sg"], ["t2"])
            cx.act(eprev[:], t2[:], AF.Exp, ["t2"], ["eprev"], scale=-C_DEC)
            yield
            cx.act(ehat[:], ps[2][:, HC:2 * HC], AF.Exp, ["ps2"], ["ehat"], scale=-C_DEC)
            cx.tt("pool", kkn[:], k_sb[:], kk_b, ALU.mult, ["k_sb", "vb"], ["kkn"])
            cx.tt("pool", t1[:], kkn[:], kkn[:], ALU.mult, ["kkn"], ["t1"])
            yield
            cx.red("dve", ss4[:], v3(t1[:]), ["t1"], ["ss4"])
            cx.act(ss4[:], ss4[:], AF.Sqrt, ["ss4"], ["ss4"])
            cx.ts("dve", ss4[:], ss4[:], 1e-12, None, ALU.max, None, ["ss4"], ["ss4"])
            yield
            cx.recip(ss4[:], ss4[:], ["ss4"], ["ss4"])
            cx.tt("dve", v3(kkn[:]), v3(kkn[:]), bc4(ss4[:]), ALU.mult, ["kkn", "ss4"], ["kkn"])
            cx.stt("dve", t3[:], a_sb[:], -1.0, ka_b, ALU.add, ALU.mult, ["a_sb", "vb"], ["t3"])
            yield
            cx.stt("dve", kmod[:], t3[:], 1.0, k_sb[:], ALU.add, ALU.mult, ["t3", "k_sb"], ["kmod"])
            cx.tt("pool", bvec[:], kkn[:], a_sb[:], ALU.mult, ["kkn", "a_sb"], ["bvec"])
            cx.tt("pool", t1[:], r_sb[:], kmod[:], ALU.mult, ["r_sb", "kmod"], ["t1"])
            yield
            cx.tt("pool", t1[:], t1[:], rk_b, ALU.mult, ["t1", "vb"], ["t1"])
            cx.red("dve", bs4[:], v3(t1[:]), ["t1"], ["bs4"])
            cx.stt("dve", tmT[:, 0, :], kkn[:], -1.0, eprev[:], ALU.mult, ALU.mult, ["kkn", "eprev"], ["tmT0"])
            yield
            cx.tt("pool", tmT[:, 1, :], r_sb[:], epos[:], ALU.mult, ["r_sb", "epos"], ["tmT1"])
            cx.tt("dve", tmT[:, 2, :], bvec[:], eneg[:], ALU.mult, ["bvec", "eneg"], ["tmT2"])
            cx.tt("pool", tmT[:, 3, :], kmod[:], eneg[:], ALU.mult, ["kmod", "eneg"], ["tmT3"])
            yield
            cx.tt("dve", khat[:], kmod[:], ehat[:], ALU.mult, ["kmod", "ehat"], ["khat"])
            cx.tt("pool", bhat[:], bvec[:], ehat[:], ALU.mult, ["bvec", "ehat"], ["bhat"])
            cx.cp("act", vbf[:], v_sb[:], ["v_sb"], ["vbf"])
            yield
            for q in range(4):
                bank = q // 2
                pv = ps[bank][:].bitcast(BF16)
                for h in range(4):
                    col = ((q % 2) * 4 + h) * 128
                    cx.tr(pv[0:64, col:col + 128], tmT[:, q, h * 64:(h + 1) * 64], ident_bf[:],
                          ["tmT%d" % q, "ident_bf"], ["ps%d" % bank])
                yield
            for half in range(2):
                pv = ps[half][:].bitcast(BF16)
                src = pv[0:64, :].rearrange("k (q h t) -> k h q t", q=2, h=4)
                cx.cp("act" if half == 0 else "dve", fm[:, :, 2 * half:2 * half + 2, :], src, ["ps%d" % half], ["fm"])
            for h in range(4):
                bank = h % 2
                ar = fm[:, h, 0:2, :].rearrange("k q t -> k (q t)")
                cx.mm(ps[bank][:, 0:256], fm[:, h, 2, :], ar, ["fm"], ["ps%d" % bank])
                cx.mm(ps[bank][:, 256:512], fm[:, h, 3, :], ar, ["fm"], ["ps%d" % bank])
                cx.mm(ps[2 + bank][:, 0:128], fm[:, h, 0, :], fm[:, h, 2, :], ["fm"], ["ps%da" % (2 + bank)])
                cx.tt("dve", Am[:, h, :], ps[bank][:], mask4, ALU.mult, ["ps%d" % bank, "csb"], ["Am%d" % h])
                cx.tt("dve", Nf[:, h, 0, :], ps[bank][:, 0:128], mask4[:, 0:128], ALU.mult, ["ps%d" % bank, "csb"], ["Nf%d" % h])
                cx.tt("dve", NTf[:, h, 0, :], ps[2 + bank][:, 0:128], maskNT, ALU.mult, ["ps%da" % (2 + bank), "csb"], ["NTf%d" % h])
                cx.tt("pool", Tf[:, h, 0, :], Nf[:, h, 0, :], ident_f, ALU.add, ["Nf%d" % h, "csb"], ["Tf%d" % h])
                yield

        def inverse(tt_):
            fm, Am, Nf, NTf, Tf, vbf, khat, bhat, gl, v_sb, bs4, gate, y_sb = DB2[tt_ % 2] + DB3[tt_ % 3]
            for kstep in range(1, 7):
                src, dst = (kstep - 1) % 2, kstep % 2
                for h in range(4):
                    bk = 4 + h
                    cx.mm(ps[bk][:, 0:128], Nf[:, h, src, :], NTf[:, h, src, :], ["Nf%d" % h, "NTf%d" % h], ["ps%d" % bk])
                    if kstep < 6:
                        cx.mm(ps[bk][:, 128:256], NTf[:, h, src, :], Nf[:, h, src, :], ["Nf%d" % h, "NTf%d" % h], ["ps%d" % bk])
                yield
                for h in range(4):
                    bk = 4 + h
                    cx.cp("act", NTf[:, h, dst, :], ps[bk][:, 0:128], ["ps%d" % bk], ["NTf%d" % h])
                    if kstep < 6:
                        cx.cp("dve", Nf[:, h, dst, :], ps[bk][:, 128:256], ["ps%d" % bk], ["Nf%d" % h])
                yield
                for h in range(4):
                    bk = 4 + h
                    cx.mm(ps[bk][:, 256:384], NTf[:, h, dst, :], Tf[:, h, src, :], ["NTf%d" % h, "Tf%d" % h], ["ps%d" % bk])
                yield
                for h in range(4):
                    bk = 4 + h
                    cx.tt("dve", Tf[:, h, dst, :], ps[bk][:, 256:384], Tf[:, h, src, :], ALU.add, ["ps%d" % bk, "Tf%d" % h], ["Tf%d" % h])

        def state_epi(b, ti, tt_):
            fm, Am, Nf, NTf, Tf, vbf, khat, bhat, gl, v_sb, bs4, gate, y_sb = DB2[tt_ % 2] + DB3[tt_ % 3]
            tok0 = b * BLK + ti * 128
            TI = 6 % 2
            HS = [slice(h * 64, (h + 1) * 64) for h in range(4)]
            for h in range(4):
                cx.mm(ps[0][:, HS[h]], fm[:, h, 0, :], STb[:, h, :], ["fm", "STb"], ["ps0"], start=True, stop=False)
                cx.mm(ps[0][:, HS[h]], Am[:, h, 256:384], vbf[:, HS[h]], ["Am%d" % h, "vbf"], ["ps0"], start=False, stop=True)
            cx.cp("act", Wf[:].rearrange("p h v -> p (h v)"), ps[0][:, 0:HC], ["ps0"], ["Wf"])
            for h in range(4):
                cx.mm(ps[1][:, HS[h]], Tf[:, h, TI, :], Wf[:, h, :], ["Tf%d" % h, "Wf"], ["ps1"])
            cx.cp("dve", Zb[:].rearrange("p h v -> p (h v)"), ps[1][:, 0:HC], ["ps1"], ["Zb"])
            for h in range(4):
                cx.mm(ps[2][:, HS[h]], fm[:, h, 1, :], STb[:, h, :], ["fm", "STb"], ["ps2"], start=True, stop=False)
                cx.mm(ps[2][:, HS[h]], Am[:, h, 128:256], Zb[:, h, :], ["Am%d" % h, "Zb"], ["ps2"], start=False, stop=False)
                cx.mm(ps[2][:, HS[h]], Am[:, h, 384:512], vbf[:, HS[h]], ["Am%d" % h, "vbf"], ["ps2"], start=False, stop=True)
            for h in range(4):
                cx.mm(ps[3][0:64, HS[h]], bhat[:, HS[h]], Zb[:, h, :], ["bhat", "Zb"], ["ps3"], start=True, stop=False)
                cx.mm(ps[3][0:64, HS[h]], khat[:, HS[h]], vbf[:, HS[h]], ["khat", "vbf"], ["ps3"], start=False, stop=True)
            cx.tt("dve", ST[:], ST[:], gl[:].unsqueeze(2).to_broadcast([64, 4, 64]), ALU.mult, ["ST", "gl"], ["ST"])
            cx.tt("dve", ST[:].rearrange("p h v -> p (h v)"), ST[:].rearrange("p h v -> p (h v)"), ps[3][0:64, 0:HC], ALU.add,
                  ["ST", "ps3"], ["ST"])
            cx.cp("act", STb[:], ST[:], ["ST"], ["STb"])
            cx.cp("act", y_sb[:], ps[2][:, 0:HC], ["ps2"], ["y_sb"])

        def epilogue(b, ti, tt_):
            fm, Am, Nf, NTf, Tf, vbf, khat, bhat, gl, v_sb, bs4, gate, y_sb = DB2[tt_ % 2] + DB3[tt_ % 3]
            tok0 = b * BLK + ti * 128
            cx.red("dve", s14[:], v3(y_sb[:]), ["y_sb"], ["s14"])
            cx.ts("dve", s14[:], s14[:], -1.0 / 64, None, ALU.mult, None, ["s14"], ["s14"])
            yield
            cx.tt("dve", v3(yc[:]), v3(y_sb[:]), bc4(s14[:]), ALU.add, ["y_sb", "s14"], ["yc"])
            cx.tt("pool", e1[:], yc[:], yc[:], ALU.mult, ["yc"], ["e1"])
            yield
            cx.red("dve", s24[:], v3(e1[:]), ["e1"], ["s24"])
            cx.act(s24[:], s24[:], AF.Sqrt, ["s24"], ["s24"], bias=64e-5, scale=1.0 / 64)
            yield
            cx.recip(s24[:], s24[:], ["s24"], ["s24"])
            cx.tt("dve", v3(yc[:]), v3(yc[:]), bc4(s24[:]), ALU.mult, ["yc", "s24"], ["yc"])
            yield
            cx.tt("pool", yc[:], yc[:], lnw_b, ALU.mult, ["yc", "vb"], ["yc"])
            cx.tt("pool", yc[:], yc[:], lnb_b, ALU.add, ["yc", "vb"], ["yc"])
            yield
            cx.tt("dve", v3(e2[:]), v3(v_sb[:]), bc4(bs4[:]), ALU.mult, ["v_sb", "bs4"], ["e2"])
            cx.tt("pool", yc[:], yc[:], e2[:], ALU.add, ["yc", "e2"], ["yc"])
            yield
            cx.tt("dve", o_bf[:], yc[:], gate[:], ALU.mult, ["yc", "gate"], ["o_bf"])
            pv = ps[7][:].bitcast(BF16)
            for j in range(2):
                cx.tr(pv[:, 768 + j * 128:768 + (j + 1) * 128], o_bf[:, j * 128:(j + 1) * 128], ident_bf[:],
                      ["o_bf", "ident_bf"], ["ps7o"])
            cx.cp("act", oT_sb[:], pv[:, 768:1024].rearrange("p (j t) -> p j t", j=2), ["ps7o"], ["oT_sb"])
            yield
            cx.dma(oT_out.rearrange("(j p) s -> p j s", p=128)[:, :, tok0:tok0 + 128], oT_sb[:], ["oT_sb"], [], "oTo", is_out=True)
            yield

        def drive(gens):
            live = list(gens)
            while live:
                nxt = []
                for (g, tt_) in live:
                    cx.S.parity = (tt_ % 2, tt_ % 3)
                    try:
                        next(g)
                        nxt.append((g, tt_))
                    except StopIteration:
                        pass
                live = nxt
            cx.S.parity = None

        tiles = [(b, ti) for b in range(NB) for ti in range(BLK // 128)]
        drive([(prep(tiles[0][0], tiles[0][1], 0), 0)])
        for t, (b, ti) in enumerate(tiles):
            gens = [(inverse(t), t)]
            if t + 1 < len(tiles):
                gens.append((prep(tiles[t + 1][0], tiles[t + 1][1], t + 1), t + 1))
            if t > 0:
                gens.append((epilogue(tiles[t - 1][0], tiles[t - 1][1], t - 1), t - 1))
            drive(gens)
            cx.S.parity = (t % 2, t % 3)
            state_epi(b, ti, t)
            cx.S.parity = None
        drive([(epilogue(tiles[-1][0], tiles[-1][1], len(tiles) - 1), len(tiles) - 1)])
        cx.done()
    return nc


def _consts():
    c = np.zeros((128, 1152), np.float32)
    i = np.arange(128)
    c[:, 0:128] = np.eye(128)
    c[:, 128:256] = (i[:, None] <= i[None, :])
    c[:, 256:384] = (i[:, None] > i[None, :])
    su = (i[:, None] < i[None, :]).astype(np.float32)
    iu = (i[:, None] <= i[None, :]).astype(np.float32)
    c[:, 384:512] = su
    c[:, 512:640] = iu
    c[:, 640:768] = su
    c[:, 768:896] = iu
    c[:, 896:1024] = (i[:, None] > i[None, :])
    c[:, 1024] = 1.0
    return c


def fm8(v):
    return np.ascontiguousarray(np.asarray(v).reshape(NCH, 128).T)


def rwkv_in_map(inp, layer, g, xT_b, oT_b=None, vfirst_c=None):
    cs = slice(g * HC, (g + 1) * HC)
    m = {}
    m["xT"] = xT_b
    m["win"] = np.ascontiguousarray(inp["a_w_in"][layer][:, :, cs])
    gmu = np.zeros((128, 7, NCH), np.float32)
    gmu[:, 0, :] = fm8(inp["a_norm"][layer])
    for p in range(6):
        gmu[:, 1 + p, :] = fm8(inp["a_mu"][layer][p])
    m["gmu"] = gmu
    l1 = np.zeros((D, 160), np.float32)
    l1[:, 0:64] = inp["a_w1"][layer]
    l1[:, 64:128] = inp["a_a1"][layer]
    l2 = np.zeros((64, 3, HC), np.float32)
    l2[:, 0, :] = inp["a_w2"][layer][:, cs]
    l2[:, 1, :] = inp["a_a2"][layer][:, cs]
    vecs = np.zeros((8, HC), np.float32)
    vecs[0] = inp["a_w0"][layer][cs]
    vecs[1] = inp["a_a0"][layer][cs]
    if layer > 0:
        l1[:, 128:160] = inp["a_v1"][layer - 1]
        l2[0:32, 2, :] = inp["a_v2"][layer - 1][:, cs]
        vecs[2] = inp["a_v0"][layer - 1][cs]
    vecs[3] = inp["a_k_k"][layer][cs]
    vecs[4] = inp["a_k_a"][layer][cs]
    vecs[5] = inp["a_r_k"][layer].reshape(-1)[cs]
    vecs[6] = inp["a_ln_w"][layer][cs]
    vecs[7] = inp["a_ln_b"][layer][cs]
    m["l1w"], m["l2w"], m["vecs"] = l1, l2, vecs
    m["cst"] = _consts()
    if layer > 0:
        m["vfirst"] = vfirst_c
        if oT_b is not None:
            m["oT"] = oT_b
            m["wout"] = np.ascontiguousarray(inp["a_w_out"][layer - 1])
    return m


def build_xupd(NT, final):
    BLK = 512
    nc = bass.Bass("TRN2", target_bir_lowering=False)
    with ExitStack() as es:
        cx = Ctx(nc, es)
        xT = cx.dram("xT", [D, NT], F32, "ExternalInput")
        oT = cx.dram("oT", [D, NT], BF16, "ExternalInput")
        wout = cx.dram("wout", [D, D], F32, "ExternalInput")
        xT_out = cx.dram("xT_out", [D, NT], F32, "ExternalOutput")
        fe = frontend_alloc(cx, True, BLK)
        stage = cx.sb("stage", [128, 2, 512], F32)
        wv = wout.rearrange("(c p) d -> p c d", p=128)
        for kc in range(NCH):
            cx.dma(stage[:], wv[:, kc, :].rearrange("p (a n) -> p a n", a=2), [], ["stage"], "stage")
            cx.cp("dve", fe["wo"][:, kc, :].rearrange("p (a n) -> p a n", a=2), stage[:], ["stage"], ["fe_wo"])
        if final:
            gfin = cx.dram("gfin", [128, NCH], F32, "ExternalInput")
            g_sb = cx.sb("g_sb", [128, NCH], F32)
            cx.dma(g_sb[:], gfin[:, :], [], ["g_sb"], "su1")
            yT = cx.sb("yT", [128, NCH, BLK], F32)
        for b in range(NT // BLK):
            frontend_block(cx, fe, xT, oT, xT_out, b, True, BLK, store_x=not final, norm=final)
            if final:
                for dc in range(NCH):
                    cx.stt("dve", yT[:, dc, :], fe["xT"][:, dc, :], g_sb[:, dc:dc + 1], fe["rstd"][:], ALU.mult, ALU.mult,
                           ["fe_xT", "g_sb", "fe_rstd"], ["yT"])
                ovx = xT_out.rearrange("(c p) s -> p c s", p=128)
                cx.dma(ovx[:, :, b * BLK:(b + 1) * BLK], yT[:], ["yT"], [], "yo", is_out=True)
        cx.done()
    return nc


LAM_INIT = {2: 0.8 - 0.6 * float(np.exp(-0.3 * 2)), 3: 0.8 - 0.6 * float(np.exp(-0.3 * 3))}


def build_attn(SL, mode):
    BLK = 512
    NB = SL // BLK
    NKT = SL // 128
    kv = (mode == "kv")
    nc = bass.Bass("TRN2", target_bir_lowering=False)
    with ExitStack() as es:
        cx = Ctx(nc, es)
        ps = cx.ps
        xT = cx.dram("xT", [D, SL], F32, "ExternalInput")
        pos = cx.dram("pos", [1, SL], I32, "ExternalInput")
        wq = cx.dram("wq", [D, 512], F32, "ExternalInput")
        gn = cx.dram("gn", [128, NCH], F32, "ExternalInput")
        invf_d = cx.dram("invf", [128, 1], F32, "ExternalInput")
        if kv:
            KT_io = cx.dram("KT_out", [2, 128, SL], BF16, "ExternalOutput")
            V_io = cx.dram("V_out", [SL, 256], BF16, "ExternalOutput")
        else:
            lam_in = cx.dram("lam_in", [4, 64], F32, "ExternalInput")
            subw = cx.dram("subw", [128, 1], F32, "ExternalInput")
            mask_d = cx.dram("mask", [128, 4, 512], BF16, "ExternalInput")
            KT_io = cx.dram("KT_in", [2, 128, SL], BF16, "ExternalInput")
            V_io = cx.dram("V_in", [SL, 256], BF16, "ExternalInput")
            oT_out = cx.dram("oT_out", [256, SL], BF16, "ExternalOutput")
        fe = frontend_alloc(cx, False, BLK)
        Wq = cx.sb("Wq", [128, NCH, 512], BF16)
        Wqr = cx.sb("Wqr", [128, NCH, 256], BF16)
        gsb = cx.sb("gsb", [128, NCH], F32)
        invf = cx.sb("invf_sb", [128, 1], F32)
        posi = cx.sb("posi", [128, BLK], I32)
        ang = cx.sb("ang", [128, BLK], F32)
        Ct = cx.sb("Ct", [128, BLK], F32)
        St = cx.sb("St", [128, BLK], F32)
        A1 = cx.sb("A1", [128, BLK], F32)
        A2 = cx.sb("A2", [128, BLK], F32)
        sqa = cx.sb("sqa", [128, BLK], F32)
        if kv:
            KTb = cx.sb("KTb", [128, 2, BLK], BF16)
            VAb = cx.sb("VAb", [128, 4, 256], BF16)
        else:
            KT = cx.sb("KT", [128, 2, SL], BF16)
            VA = cx.sb("VA", [128, NKT, 256], BF16)
            maskb = cx.sb("maskb", [128, 4, 512], BF16)
            ones_bf = cx.sb("ones_bf", [128, 128], BF16)
            lam_t = cx.sb("lam_t", [128, 4, 64], F32)
            lam_s = cx.sb("lam_s", [128, 4], F32)
            nlam = cx.sb("nlam", [128, 1], F32)
            sw = cx.sb("sw", [128, 1], F32)
            Qp = cx.sb("Qp", [128, 2, 2, BLK], BF16)
            gT = cx.sb("gT", [128, 2, BLK], BF16)
            pt = [cx.sb("pt%d" % i, [128, BLK], BF16) for i in range(4)]
            rl = [cx.sb("rl%d" % i, [128, BLK], F32) for i in range(2)]
            ob = cx.sb("ob", [128, BLK], BF16)
        stage = fe["xT"]
        cx.dma(gsb[:], gn[:, :], [], ["gsb"], "su1")
        cx.dma(invf[:], invf_d[:, :], [], ["invf"], "su0")
        if not kv:
            cx.dma(sw[:], subw[:, :], [], ["sw"], "su2")
            cx.dma(maskb[:], mask_d[:, :, :], [], ["maskb"], "su4")
            for i in range(4):
                cx.dma(lam_t[:, i, :], lam_in[i:i + 1, :].partition_broadcast(128), [], ["lam_t"], "su3")
            cx.memset("pool", ones_bf[:], 1.0, ["ones_bf"])
            cx.memset("pool", Qp[:], 0.0, ["Qp"])
            lam_init = LAM_INIT[mode]
            cx.tt("dve", lam_t[:, 0, :], lam_t[:, 0, :], lam_t[:, 1, :], ALU.mult, ["lam_t"], ["lam_t"])
            cx.tt("dve", lam_t[:, 2, :], lam_t[:, 2, :], lam_t[:, 3, :], ALU.mult, ["lam_t"], ["lam_t"])
            cx.red("dve", lam_s[:, 0:1], lam_t[:, 0, :], ["lam_t"], ["lam_s"])
            cx.red("dve", lam_s[:, 1:2], lam_t[:, 2, :], ["lam_t"], ["lam_s"])
            cx.act(lam_s[:, 0:2], lam_s[:, 0:2], AF.Exp, ["lam_s"], ["lam_s"])
            cx.tt("dve", lam_s[:, 2:3], lam_s[:, 1:2], lam_s[:, 0:1], ALU.subtract, ["lam_s"], ["lam_s"])
            cx.ts("dve", nlam[:], lam_s[:, 2:3], -lam_init, None, ALU.add, None, ["lam_s"], ["nlam"])
            cx.ts("dve", sw[:], sw[:], 1.0 - lam_init, None, ALU.mult, None, ["sw"], ["sw"])
        cx.dma(stage[:], wq.rearrange("(c p) n -> p c n", p=128), [], ["fe_xT"], "stage")
        cx.tt("dve", Wq[:], stage[:], gsb[:].unsqueeze(2).to_broadcast([128, NCH, 512]), ALU.mult, ["fe_xT", "gsb"], ["W"])
        cx.memset("pool", Wqr[:], 0.0, ["Wr"])
        for mp in range(4):
            c = mp * 64
            cx.ts("dve", Wqr[:, :, c:c + 8], Wq[:, :, c + 8:c + 16], -1.0, None, ALU.mult, None, ["W"], ["Wr"])
            cx.cp("dve", Wqr[:, :, c + 8:c + 16], Wq[:, :, c:c + 8], ["W"], ["Wr"])
        if not kv:
            for j in range(2):
                for q4 in range(4):
                    sl = slice(q4 * (SL // 4), (q4 + 1) * (SL // 4))
                    cx.dma(KT[:, j, sl], KT_io[j, :, sl], [], ["KT"], "ktl")
            vv = V_io.rearrange("(t p) n -> p t n", p=128)
            for q4 in range(4):
                sl = slice(q4 * (NKT // 4), (q4 + 1) * (NKT // 4))
                cx.dma(VA[:, sl, :], vv[:, sl, :], [], ["VA"], "val")

        M = 12582912.0
        TWO_PI = float(2 * np.pi)

        def sin_table(dst, dkey, shift):
            cx.ts("dve", A1[:], ang[:], shift, None, ALU.add, None, ["ang"], ["A1"])
            cx.ts("dve", A2[:], A1[:], 1.0 / TWO_PI, M, ALU.mult, ALU.add, ["A1"], ["A2"])
            cx.ts("dve", A2[:], A2[:], M, None, ALU.subtract, None, ["A2"], ["A2"])
            cx.stt("dve", A1[:], A2[:], -TWO_PI, A1[:], ALU.mult, ALU.add, ["A2", "A1"], ["A1"])
            cx.ts("dve", A1[:], A1[:], -3.1415925, 3.1415925, ALU.max, ALU.min, ["A1"], ["A1"])
            cx.act(dst[:], A1[:], AF.Sin, ["A1"], [dkey])

        def proj_fm(W, wkey, c0, bank):
            for kc in range(NCH):
                cx.mm(ps[bank][:], W[:, kc, c0:c0 + 128], fe["xh"][:, kc, 1:BLK + 1], [wkey, "fe_xh"], ["ps%d" % bank],
                      start=(kc == 0), stop=(kc == NCH - 1))

        def roped(j):
            proj_fm(Wq, "W", j * 128, 6)
            proj_fm(Wqr, "Wr", j * 128, 7)
            cx.tt("dve", A1[:], ps[6][:], Ct[:], ALU.mult, ["ps6", "Ct"], ["A1"])
            cx.tt("dve", A2[:], ps[7][:], St[:], ALU.mult, ["ps7", "St"], ["A2"])

        for b in range(NB):
            t0 = b * BLK
            frontend_block(cx, fe, xT, None, None, b, False, BLK)
            cx.dma(posi[:], pos[0:1, t0:t0 + BLK].partition_broadcast(128), [], ["posi"], "posl")
            cx.cp("dve", sqa[:], posi[:], ["posi"], ["sqa"])
            cx.ts("dve", ang[:], sqa[:], invf[:, 0:1], None, ALU.mult, None, ["sqa", "invf"], ["ang"])
            sin_table(St, "St", 0.0)
            sin_table(Ct, "Ct", float(np.pi / 2))
            for j in range(2):
                roped(j)
                if kv:
                    cx.tt("pool", KTb[:, j, :], A1[:], A2[:], ALU.add, ["A1", "A2"], ["KTb"])
                    for ti in range(4):
                        for kc in range(NCH):
                            cx.mm(ps[5][:, 0:128], fe["xh"][:, kc, 1 + ti * 128:1 + (ti + 1) * 128], Wq[:, kc, 256 + j * 128:256 + (j + 1) * 128],
                                  ["fe_xh", "W"], ["ps5"], start=(kc == 0), stop=(kc == NCH - 1))
                        cx.cp("act", VAb[:, ti, j * 128:(j + 1) * 128], ps[5][:, 0:128], ["ps5"], ["VAb"])
                else:
                    cx.tt("pool", Qp[0:64, j, 0, :], A1[0:64, :], A2[0:64, :], ALU.add, ["A1", "A2"], ["Qp"])
                    cx.tt("pool", Qp[64:128, j, 1, :], A1[64:128, :], A2[64:128, :], ALU.add, ["A1", "A2"], ["Qp"])
                    proj_fm(Wq, "W", 256 + j * 128, 6)
                    cx.act(sqa[:], ps[6][:], AF.Sigmoid, ["ps6"], ["sqa"])
                    cx.tt("dve", gT[:, j, :], ps[6][:], sqa[:], ALU.mult, ["ps6", "sqa"], ["gT"])
            if kv:
                for j in range(2):
                    cx.dma(KT_io[j, :, t0:t0 + BLK], KTb[:, j, :], ["KTb"], [], "kto", is_out=True)
                vo = V_io.rearrange("(t p) n -> p t n", p=128)
                cx.dma(vo[:, b * 4:(b + 1) * 4, :], VAb[:], ["VAb"], [], "vao", is_out=True)
                continue
            for j in range(2):
                nkt = 4 * b + 4
                pairs = [(kt, m) for kt in range(nkt) for m in range(2)]
                NP = len(pairs)
                DEPTH = 3
                SCB = (0, 1, 6, 7)

                def emit_sc(i):
                    kt, m = pairs[i]
                    sbk = SCB[i % 4]
                    p_t = pt[i % 4]
                    pk = "pt%d" % (i % 4)
                    cx.mm(ps[sbk][:], KT[:, j, kt * 128:(kt + 1) * 128], Qp[:, j, m, :], ["KT", "Qp"], ["ps%d" % sbk])
                    cx.act(p_t[:], ps[sbk][:], AF.Exp, ["ps%d" % sbk], [pk], scale=0.125)
                    if kt >= 4 * b:
                        cx.tt("dve", p_t[:], p_t[:], maskb[:, kt - 4 * b, :], ALU.mult, [pk, "maskb"], [pk])

                def emit_pv(i):
                    kt, m = pairs[i]
                    p_t = pt[i % 4]
                    pk = "pt%d" % (i % 4)
                    cx.mm(ps[2 + m][:], VA[:, kt, j * 128:(j + 1) * 128], p_t[:], ["VA", pk], ["ps%d" % (2 + m)],
                          start=(kt == 0), stop=(kt == nkt - 1))
                    if m == 1:
                        cx.mm(ps[5][:], ones_bf[:], p_t[:], ["ones_bf", pk], ["ps5"], start=(kt == 0), stop=(kt == nkt - 1))
                    elif kt == 0:
                        cx.cp("dve", rl[0][:], p_t[:], [pk], ["rl0"])
                    else:
                        cx.tt("dve", rl[0][:], rl[0][:], p_t[:], ALU.add, ["rl0", pk], ["rl0"])

                for i in range(NP + DEPTH):
                    if i < NP:
                        emit_sc(i)
                    if i >= DEPTH:
                        emit_pv(i - DEPTH)
                cx.mm(ps[4][:], fe["ones"][:], rl[0][:], ["fe_ones", "rl0"], ["ps4"])
                cx.recip(rl[0][:], ps[4][:], ["ps4"], ["rl0"])
                cx.recip(rl[1][:], ps[5][:], ["ps5"], ["rl1"])
                cx.tt("dve", A1[:], ps[2][:], rl[0][:], ALU.mult, ["ps2", "rl0"], ["A1"])
                cx.tt("dve", A2[:], ps[3][:], rl[1][:], ALU.mult, ["ps3", "rl1"], ["A2"])
                cx.stt("dve", A1[:], A2[:], nlam[:, 0:1], A1[:], ALU.mult, ALU.add, ["A2", "nlam", "A1"], ["A1"])
                cx.tt("pool", sqa[:], A1[:], A1[:], ALU.mult, ["A1"], ["sqa"])
                cx.mm(ps[6][:], fe["ones"][:], sqa[:], ["fe_ones", "sqa"], ["ps6"])
                cx.act(sqa[:], ps[6][:], AF.Sqrt, ["ps6"], ["sqa"], bias=1e-5, scale=1.0 / 128)
                cx.recip(sqa[:], sqa[:], ["sqa"], ["sqa"])
                cx.stt("dve", A1[:], A1[:], sw[:, 0:1], sqa[:], ALU.mult, ALU.mult, ["A1", "sw", "sqa"], ["A1"])
                cx.tt("dve", ob[:], A1[:], gT[:, j, :], ALU.mult, ["A1", "gT"], ["ob"])
                cx.dma(oT_out[j * 128:(j + 1) * 128, t0:t0 + BLK], ob[:], ["ob"], [], "oTo", is_out=True)
        cx.done()
    return nc


def _attn_consts():
    i = np.arange(128)
    q = np.arange(512)
    mask = np.zeros((128, 4, 512), np.float32)
    for a in range(4):
        mask[:, a, :] = (q[None, :] >= (i[:, None] + 128 * a))
    invf = np.zeros((128, 1), np.float32)
    base = (500000.0 ** (-np.arange(0, 16, 2, dtype=np.float32) / np.float32(16))).astype(np.float32)
    for m in range(2):
        for d in range(16):
            invf[m * 64 + d, 0] = base[d % 8]
    return mask.astype(ml_dtypes.bfloat16), invf


def attn_in_map(inp, mode, g, xT_b, pos_b, KT_c=None, V_c=None):
    cs = slice(g * 256, (g + 1) * 256)
    cs2 = slice(1024 + g * 256, 1024 + (g + 1) * 256)
    mask, invf = _attn_consts()
    m = {"xT": xT_b, "pos": np.ascontiguousarray(pos_b.reshape(1, -1)), "invf": invf}
    if mode == "kv":
        w = inp["w_kv"]
        m["gn"] = fm8(inp["kv_norm"])
    else:
        j = mode - 2
        w = inp["b_w_in"][j]
        m["gn"] = fm8(inp["b_norm"][j])
        m["lam_in"] = np.stack([inp["b_lq1"][j], inp["b_lk1"][j], inp["b_lq2"][j], inp["b_lk2"][j]]).astype(np.float32)
        m["subw"] = np.ascontiguousarray(inp["b_subln"][j].reshape(128, 1))
        m["mask"] = mask
        m["KT_in"] = KT_c
        m["V_in"] = V_c
    m["wq"] = np.ascontiguousarray(np.concatenate([w[:, cs], w[:, cs2]], axis=1))
    return m


def _run(nc, maps):
    return run_bass_kernel_spmd(nc, maps, core_ids=list(range(8))).results


def _xupd(xT, oT, wout, SL, final, gfin=None):
    NT = SL // 4
    nc = build_xupd(NT, final)
    maps = []
    for c in range(8):
        b, q = c // 4, c % 4
        m = {"xT": np.ascontiguousarray(xT[b][:, q * NT:(q + 1) * NT]),
             "oT": np.ascontiguousarray(oT[b][:, q * NT:(q + 1) * NT]),
             "wout": np.ascontiguousarray(wout)}
        if final:
            m["gfin"] = gfin
        maps.append(m)
    r = _run(nc, maps)
    return [np.concatenate([r[b * 4 + q]["xT_out"] for q in range(4)], axis=1) for b in range(2)]


def _forward(inp, SL):
    inp = {k: np.asarray(v) for k, v in inp.items()}
    xT = [np.ascontiguousarray(inp["x"][b].T) for b in range(2)]
    gather = lambda r: [np.concatenate([r[b * 4 + g]["oT_out"] for g in range(4)], axis=0) for b in range(2)]
    r = _run(build_rwkv(SL, 0), [rwkv_in_map(inp, 0, c % 4, xT[c // 4]) for c in range(8)])
    oT = gather(r)
    vf = [r[c]["vfirst_out"] for c in range(8)]
    xT = _xupd(xT, oT, inp["a_w_out"][0], SL, False)
    r = _run(build_rwkv(SL, 1, prev=False), [rwkv_in_map(inp, 1, c % 4, xT[c // 4], None, vf[c]) for c in range(8)])
    oT = gather(r)
    xT = _xupd(xT, oT, inp["a_w_out"][1], SL, False)
    r = _run(build_attn(SL, "kv"), [attn_in_map(inp, "kv", c % 4, xT[c // 4], inp["positions"][c // 4]) for c in range(8)])
    KT = [r[c]["KT_out"] for c in range(8)]
    V = [r[c]["V_out"] for c in range(8)]
    for j in range(2):
        r = _run(build_attn(SL, 2 + j),
                 [attn_in_map(inp, 2 + j, c % 4, xT[c // 4], inp["positions"][c // 4], KT[c], V[c]) for c in range(8)])
        oT = gather(r)
        if j == 0:
            xT = _xupd(xT, oT, inp["b_w_out"][0], SL, False)
        else:
            xT = _xupd(xT, oT, inp["b_w_out"][1], SL, True, fm8(inp["final_norm"]))
    return np.stack([np.ascontiguousarray(xT[b].T) for b in range(2)]).astype(np.float32)


def kernel(**inputs):
    return _forward(inputs, 16384)
```
